# Optimizing a Trainium2 kernel written in Bass

```python
import math
import jax, jax.numpy as jnp
from jax import lax
import numpy as np

D_MODEL = 1024
BATCH = 4
SEQ = 4096
DEPTH = 2
DEC_BATCH = 32
DEC_SEQ = 4
PAST_LEN = 8192
PAGE_SIZE = 128

N_EVEN = (DEPTH + 1) // 2
N_ODD = DEPTH // 2
EPS = 1e-6
A_WIDTH = D_MODEL // 2
A_CONV = 3
B_WIDTH = D_MODEL // 2
B_WINDOWS = (2, 4, 8, 16)
B_GROUP = B_WIDTH // len(B_WINDOWS)
B_PREV = max(B_WINDOWS) - 1
C_PAIRS = ((128, 1), (512, 4), (2048, 16))
C_HPG = 4
C_HEAD_DIM = 64
C_HEADS = len(C_PAIRS) * C_HPG
C_QKV = 3 * C_HEADS * C_HEAD_DIM
C_OUT = C_HPG * C_HEAD_DIM
ATTN_SCALE = C_HEAD_DIM ** -0.5
Q_BLOCK = 128
N_BUCKETS = 32
MAX_DISTANCE = 2048
D_WIDTH = D_MODEL // 2
D_CONV = 31
D_FF = 2816
FFN_CONV = 3

kernel_name = "hybrid_conv_pool_dilattn_conformer_step"


def _rmsnorm(x, g):
    xf = x.astype(jnp.float32)
    y = xf * lax.rsqrt(jnp.mean(xf * xf, axis=-1, keepdims=True) + EPS)
    return (y * g.astype(jnp.float32)).astype(x.dtype)


def _layernorm(x, g, b):
    xf = x.astype(jnp.float32)
    mu = jnp.mean(xf, axis=-1, keepdims=True)
    var = jnp.mean(jnp.square(xf - mu), axis=-1, keepdims=True)
    y = (xf - mu) * lax.rsqrt(var + EPS) * g.astype(jnp.float32) + b.astype(jnp.float32)
    return y.astype(x.dtype)


def _causal_dwconv(u, prev, w):
    ext = jnp.concatenate([prev.astype(u.dtype), u], axis=1)
    out = lax.conv_general_dilated(ext, w[:, None, :].astype(u.dtype), window_strides=(1,), padding="VALID",
                                   dimension_numbers=("NWC", "WIO", "NWC"), feature_group_count=u.shape[-1])
    return out, ext[:, -(w.shape[0] - 1):]


def _causal_multiscale_pool(u, prev, pos0):
    T = u.shape[1]
    ext = jnp.concatenate([prev.astype(u.dtype), u], axis=1)
    ef = ext.astype(jnp.float32)
    cs = jnp.concatenate([jnp.zeros_like(ef[:, :1]), jnp.cumsum(ef, axis=1)], axis=1)
    end = cs[:, B_PREV + 1:B_PREV + 1 + T]
    pos = pos0 + jnp.arange(T)
    outs = []
    for gi, win in enumerate(B_WINDOWS):
        sl = slice(gi * B_GROUP, (gi + 1) * B_GROUP)
        start = cs[:, B_PREV + 1 - win:B_PREV + 1 - win + T, sl]
        cnt = jnp.minimum(pos + 1, win).astype(jnp.float32)[None, :, None]
        outs.append((end[..., sl] - start) / cnt)
    mean = jnp.concatenate(outs, axis=-1)
    return (mean - u.astype(jnp.float32)).astype(u.dtype), ext[:, -B_PREV:]


def _t5_bucket(dist):
    dist = np.asarray(dist)
    max_exact = N_BUCKETS // 2
    large = max_exact + (np.log(np.maximum(dist, max_exact) / max_exact) / np.log(MAX_DISTANCE / max_exact)
                         * (N_BUCKETS - max_exact)).astype(np.int32)
    large = np.minimum(large, N_BUCKETS - 1)
    return np.where(dist < max_exact, dist, large).astype(np.int32)


def _group_bias(rel_bias, g, window, dil):
    buckets = _t5_bucket(dil * np.arange(window // dil + 1))
    return rel_bias[buckets][:, g * C_HPG:(g + 1) * C_HPG].T


def _dilated_attn_prompt(q, k, v, bias_g, window, dil):
    n, S, H, E = q.shape
    L = S // dil
    taps = window // dil
    bq = math.gcd(L, Q_BLOCK)
    nb = L // bq

    def by_residue(t):
        return t.reshape(n, L, dil, H, E).transpose(0, 2, 1, 3, 4)

    pad = ((0, 0), (0, 0), (taps, 0), (0, 0), (0, 0))
    kp = jnp.pad(by_residue(k), pad)
    vp = jnp.pad(by_residue(v), pad)
    kidx = (np.arange(nb) * bq)[:, None] + np.arange(bq + taps)[None, :]
    kb = kp[:, :, kidx]
    vb = vp[:, :, kidx]
    qb = by_residue(q).reshape(n, dil, nb, bq, H, E)
    dist = np.arange(bq)[:, None] + taps - np.arange(bq + taps)[None, :]
    mask = ((dist >= 0) & (dist <= taps))[None] & (kidx >= taps)[:, None, :]
    bias = bias_g[:, np.clip(dist, 0, taps)].astype(jnp.float32)
    logits = jnp.einsum("brnqhe,brnkhe->brnhqk", qb, kb, preferred_element_type=jnp.float32) * ATTN_SCALE
    logits = jnp.where(mask[None, None, :, None], logits + bias[None, None, None], -jnp.inf)
    m = jnp.max(logits, axis=-1, keepdims=True)
    p = jnp.exp(logits - m)
    s = jnp.sum(p, axis=-1, keepdims=True)
    o = jnp.einsum("brnhqk,brnkhe->brnqhe", (p / s).astype(v.dtype), vb)
    lse = (m + jnp.log(s))[..., 0]
    o = o.reshape(n, dil, L, H, E).transpose(0, 2, 1, 3, 4).reshape(n, S, H, E)
    lse = lse.transpose(0, 1, 2, 4, 3).reshape(n, dil, L, H).transpose(0, 2, 1, 3).reshape(n, S, H)
    wb = min(window, S)
    buf = jnp.stack([k[:, -wb:], v[:, -wb:]], axis=2)
    return o, lse, buf


def _dilated_attn_sample(q, k, v, kv_buf, bias_g, window, dil):
    n, T, H, E = q.shape
    wb = kv_buf.shape[1]
    taps = window // dil
    ek = jnp.concatenate([kv_buf[:, :, 0].astype(k.dtype), k], axis=1)
    ev = jnp.concatenate([kv_buf[:, :, 1].astype(v.dtype), v], axis=1)
    idx = wb + np.arange(T)[:, None] - dil * np.arange(taps + 1)[None, :]
    valid = idx >= 0
    idx = np.maximum(idx, 0)
    kg = ek[:, idx]
    vg = ev[:, idx]
    logits = jnp.einsum("bthe,btkhe->bhtk", q, kg, preferred_element_type=jnp.float32) * ATTN_SCALE
    logits = jnp.where(valid[None, None], logits + bias_g.astype(jnp.float32)[:, None, :], -jnp.inf)
    m = jnp.max(logits, axis=-1, keepdims=True)
    p = jnp.exp(logits - m)
    s = jnp.sum(p, axis=-1, keepdims=True)
    o = jnp.einsum("bhtk,btkhe->bthe", (p / s).astype(v.dtype), vg)
    lse = (m + jnp.log(s))[..., 0].transpose(0, 2, 1)
    buf = jnp.stack([ek[:, -wb:], ev[:, -wb:]], axis=2)
    return o, lse, buf


def _mixer_ab(h, a_prev, b_prev, pos0, w_in, a_conv_w, b_w_grp, b_scale, w_out):
    n, T, _ = h.shape
    hh, bg, cg, u = jnp.split(h @ w_in, [A_WIDTH, 2 * A_WIDTH, 3 * A_WIDTH], axis=-1)
    z, a_new = _causal_dwconv(cg * hh, a_prev, a_conv_w)
    ya = bg * z
    pooled, b_new = _causal_multiscale_pool(u, b_prev, pos0)
    yb = jnp.einsum("btgc,gcd->btgd", pooled.reshape(n, T, len(B_WINDOWS), B_GROUP), b_w_grp)
    yb = yb.reshape(n, T, B_WIDTH) * b_scale
    return jnp.concatenate([ya, yb], axis=-1) @ w_out, a_new, b_new


def _mixer_cd(h, kv_bufs, d_prev, rel_bias, w_in, d_conv_w, d_conv_b, d_ln_g, d_ln_b, w_out):
    n, T, _ = h.shape
    proj = h @ w_in
    qkv = proj[..., :C_QKV].reshape(n, T, 3, C_HEADS, C_HEAD_DIM)
    q, k, v = qkv[:, :, 0], qkv[:, :, 1], qkv[:, :, 2]
    outs, lses, new_bufs = [], [], []
    for g, (win, dil) in enumerate(C_PAIRS):
        hs = slice(g * C_HPG, (g + 1) * C_HPG)
        bias_g = _group_bias(rel_bias, g, win, dil)
        if kv_bufs is None:
            o, lse, buf = _dilated_attn_prompt(q[:, :, hs], k[:, :, hs], v[:, :, hs], bias_g, win, dil)
        else:
            o, lse, buf = _dilated_attn_sample(q[:, :, hs], k[:, :, hs], v[:, :, hs], kv_bufs[g], bias_g, win, dil)
        outs.append(o)
        lses.append(lse)
        new_bufs.append(buf)
    wgt = jax.nn.softmax(jnp.stack(lses, axis=2), axis=2)
    yc = jnp.einsum("btgh,btghe->bthe", wgt.astype(q.dtype), jnp.stack(outs, axis=2)).reshape(n, T, C_OUT)
    dv, dg = jnp.split(proj[..., C_QKV:], 2, axis=-1)
    z, d_new = _causal_dwconv(dv * jax.nn.sigmoid(dg), d_prev, d_conv_w)
    yd = jax.nn.silu(_layernorm(z + d_conv_b, d_ln_g, d_ln_b))
    return jnp.concatenate([yc, yd], axis=-1) @ w_out, new_bufs, d_new


def _conv_ffn(h, prev, w_up, conv_w, conv_b, w_down):
    u, new_prev = _causal_dwconv(h @ w_up, prev, conv_w)
    a, g = jnp.split(u + conv_b, 2, axis=-1)
    return (a * jax.nn.silu(g)) @ w_down, new_prev


def _layer_stack(x, c, pos0, st, w):
    n = x.shape[0]
    dt = x.dtype
    new = {key: [] for key in ("a", "b", "c0", "c1", "c2", "d", "f")}
    c_act = jax.nn.silu(c)
    for l in range(DEPTH):
        mod = (c_act @ w["ada_w"][l] + w["ada_b"][l])[:, None, :]
        sh1, sc1, g1, sh2, sc2, g2 = jnp.split(mod, 6, axis=-1)
        ng = w["norm_g"][l]
        i = l // 2
        h = _rmsnorm(x, ng[0]) * (1 + sc1) + sh1
        if l % 2 == 0:
            a_prev = st["a"][i] if st is not None else jnp.zeros((n, A_CONV - 1, A_WIDTH), dt)
            b_prev = st["b"][i] if st is not None else jnp.zeros((n, B_PREV, B_WIDTH), dt)
            y, a_new, b_new = _mixer_ab(h, a_prev, b_prev, pos0, w["ab_w_in"][i], w["a_conv_w"][i],
                                        w["b_w_grp"][i], w["b_scale"][i], w["ab_w_out"][i])
            new["a"].append(a_new)
            new["b"].append(b_new)
        else:
            kv_bufs = [buf[i] for buf in st["c"]] if st is not None else None
            d_prev = st["d"][i] if st is not None else jnp.zeros((n, D_CONV - 1, D_WIDTH), dt)
            y, bufs, d_new = _mixer_cd(h, kv_bufs, d_prev, w["rel_bias"], w["cd_w_in"][i], w["d_conv_w"][i],
                                       w["d_conv_b"][i], w["d_ln_g"][i], w["d_ln_b"][i], w["cd_w_out"][i])
            for g in range(len(C_PAIRS)):
                new["c%d" % g].append(bufs[g])
            new["d"].append(d_new)
        x = x + g1 * _rmsnorm(y, ng[1])
        h = _rmsnorm(x, ng[2]) * (1 + sc2) + sh2
        f_prev = st["f"][l] if st is not None else jnp.zeros((n, FFN_CONV - 1, 2 * D_FF), dt)
        y, f_new = _conv_ffn(h, f_prev, w["ffn_w_up"][l], w["ffn_conv_w"][l], w["ffn_conv_b"][l], w["ffn_w_down"][l])
        new["f"].append(f_new)
        x = x + g2 * _rmsnorm(y, ng[3])
    return x, {key: jnp.stack(val) for key, val in new.items()}


def setup_inputs(seed: int = 0) -> dict:
    key = jax.random.key(seed)
    ks = iter(jax.random.split(key, 40))

    def nrm(shape, scale):
        return jax.random.normal(next(ks), shape, jnp.float32) * scale

    cw = lambda wlen: (N_ODD, DEC_BATCH, min(wlen, PAST_LEN), 2, C_HPG, C_HEAD_DIM)
    return {
        "x_prompt": nrm((BATCH, SEQ, D_MODEL), 1.0),
        "x_sample": nrm((DEC_BATCH, DEC_SEQ, D_MODEL), 1.0),
        "state_a_conv": nrm((N_EVEN, DEC_BATCH, A_CONV - 1, A_WIDTH), 1.0),
        "state_b_pool": nrm((N_EVEN, DEC_BATCH, B_PREV, B_WIDTH), 1.0),
        "cache_c_win128": nrm(cw(C_PAIRS[0][0]), 1.0),
        "cache_c_win512": nrm(cw(C_PAIRS[1][0]), 1.0),
        "cache_c_win2048": nrm(cw(C_PAIRS[2][0]), 1.0),
        "state_d_conv": nrm((N_ODD, DEC_BATCH, D_CONV - 1, D_WIDTH), 0.5),
        "state_ffn_conv": nrm((DEPTH, DEC_BATCH, FFN_CONV - 1, 2 * D_FF), 1.0),
        "c_prompt": nrm((BATCH, D_MODEL), 1.0),
        "c_sample": nrm((DEC_BATCH, D_MODEL), 1.0),
        "ada_w": nrm((DEPTH, D_MODEL, 6 * D_MODEL), 0.5 * D_MODEL ** -0.5),
        "ada_b": nrm((DEPTH, 6 * D_MODEL), 0.02),
        "norm_g": 1.0 + nrm((DEPTH, 4, D_MODEL), 0.05),
        "rel_bias": nrm((N_BUCKETS, C_HEADS), 0.5),
        "ab_w_in": nrm((N_EVEN, D_MODEL, 3 * A_WIDTH + B_WIDTH), D_MODEL ** -0.5),
        "a_conv_w": nrm((N_EVEN, A_CONV, A_WIDTH), A_CONV ** -0.5),
        "b_w_grp": nrm((N_EVEN, len(B_WINDOWS), B_GROUP, B_GROUP), B_GROUP ** -0.5),
        "b_scale": 1.0 + nrm((N_EVEN, B_WIDTH), 0.1),
        "ab_w_out": nrm((N_EVEN, A_WIDTH + B_WIDTH, D_MODEL), (A_WIDTH + B_WIDTH) ** -0.5),
        "cd_w_in": nrm((N_ODD, D_MODEL, C_QKV + 2 * D_WIDTH), D_MODEL ** -0.5),
        "d_conv_w": nrm((N_ODD, D_CONV, D_WIDTH), D_CONV ** -0.5),
        "d_conv_b": nrm((N_ODD, D_WIDTH), 0.02),
        "d_ln_g": 1.0 + nrm((N_ODD, D_WIDTH), 0.05),
        "d_ln_b": nrm((N_ODD, D_WIDTH), 0.02),
        "cd_w_out": nrm((N_ODD, C_OUT + D_WIDTH, D_MODEL), (C_OUT + D_WIDTH) ** -0.5),
        "ffn_w_up": nrm((DEPTH, D_MODEL, 2 * D_FF), D_MODEL ** -0.5),
        "ffn_conv_w": nrm((DEPTH, FFN_CONV, 2 * D_FF), FFN_CONV ** -0.5),
        "ffn_conv_b": nrm((DEPTH, 2 * D_FF), 0.02),
        "ffn_w_down": nrm((DEPTH, D_FF, D_MODEL), D_FF ** -0.5),
    }


def reference(x_prompt, x_sample, state_a_conv, state_b_pool, cache_c_win128, cache_c_win512, cache_c_win2048,
              state_d_conv, state_ffn_conv, c_prompt, c_sample, ada_w, ada_b, norm_g, rel_bias, ab_w_in, a_conv_w,
              b_w_grp, b_scale, ab_w_out, cd_w_in, d_conv_w, d_conv_b, d_ln_g, d_ln_b, cd_w_out, ffn_w_up,
              ffn_conv_w, ffn_conv_b, ffn_w_down):
    w = dict(ada_w=ada_w, ada_b=ada_b, norm_g=norm_g, rel_bias=rel_bias, ab_w_in=ab_w_in, a_conv_w=a_conv_w,
             b_w_grp=b_w_grp, b_scale=b_scale, ab_w_out=ab_w_out, cd_w_in=cd_w_in, d_conv_w=d_conv_w,
             d_conv_b=d_conv_b, d_ln_g=d_ln_g, d_ln_b=d_ln_b, cd_w_out=cd_w_out, ffn_w_up=ffn_w_up,
             ffn_conv_w=ffn_conv_w, ffn_conv_b=ffn_conv_b, ffn_w_down=ffn_w_down)
    st = dict(a=state_a_conv, b=state_b_pool, c=(cache_c_win128, cache_c_win512, cache_c_win2048),
              d=state_d_conv, f=state_ffn_conv)
    y_prompt, sp = _layer_stack(x_prompt, c_prompt, 0, None, w)
    y_sample, ss = _layer_stack(x_sample, c_sample, PAST_LEN, st, w)
    return (y_prompt, y_sample, sp["a"], ss["a"], sp["b"], ss["b"], sp["c0"], ss["c0"], sp["c1"], ss["c1"],
            sp["c2"], ss["c2"], sp["d"], ss["d"], sp["f"], ss["f"])
```

```python
import math
import os
import types
from contextlib import ExitStack
import numpy as np
import concourse.bass as bass
import concourse.mybir as mybir
from concourse.bass_utils import run_bass_kernel_spmd

F32 = mybir.dt.float32
BF16 = mybir.dt.bfloat16
AF = mybir.ActivationFunctionType
ALU = mybir.AluOpType
AX = mybir.AxisListType

D = 1024
DFF = 2816
SEQ = 4096
NT = 512
EPS = 1e-6
NEG = -30000.0


class Res:
    __slots__ = ("name", "lw", "rd", "psum")

    def __init__(self, name=""):
        self.name = name
        self.psum = False
        self.lw = None
        self.rd = {}


def _freeze(fn):
    if fn.__closure__ is None:
        return fn
    cells = []
    for c in fn.__closure__:
        try:
            cells.append(types.CellType(c.cell_contents))
        except ValueError:
            cells.append(c)
    return types.FunctionType(fn.__code__, fn.__globals__, fn.__name__, fn.__defaults__, tuple(cells))


class Sched:
    ENG = ("pe", "act", "dve", "pool", "sp")

    def __init__(self, nc, es, n_dma_sems=48):
        self.nc = nc
        self.ops = {e: [] for e in self.ENG}
        self.cnt = {e: 0 for e in self.ENG}
        self.sig = {e: 0 for e in self.ENG}
        self.seen = {e: {} for e in self.ENG}
        self.semh = {}
        for e in self.ENG:
            self.semh[e] = es.enter_context(nc.semaphore("sem_" + e))
        self.ndma = n_dma_sems
        for i in range(n_dma_sems):
            self.semh[("d", i)] = es.enter_context(nc.semaphore("semd%d" % i))
        self.dval = [0] * n_dma_sems
        self.dnext = 0
        self.dnext_pool = 0
        self.out_tokens = []

    def _wait(self, e, tok):
        sem, val = tok
        if self.seen[e].get(sem, 0) >= val:
            return
        self.seen[e][sem] = val
        h = self.semh[sem]
        self.ops[e].append(lambda eng, h=h, v=val: eng.wait_ge(h, v))

    def _deps(self, e, reads, writes):
        toks = {}
        for r in reads:
            if r.lw is not None:
                s, v = r.lw
                toks[s] = max(toks.get(s, 0), v)
        for w in writes:
            if w.lw is not None:
                s, v = w.lw
                toks[s] = max(toks.get(s, 0), v)
            for s, v in w.rd.items():
                toks[s] = max(toks.get(s, 0), v)
        for s, v in toks.items():
            if s == e and e == "pe":
                continue
            self._wait(e, (s, v))

    def _commit(self, tok, reads, writes):
        for w in writes:
            w.lw = tok
            w.rd = {}
        for r in reads:
            if r in writes:
                continue
            s, v = tok
            if r.rd.get(s, 0) < v:
                r.rd[s] = v

    def op(self, e, fn, reads=(), writes=(), signal=True):
        fn = _freeze(fn)
        signal = True
        reads, writes = list(reads), list(writes)
        for r in reads:
            if r.psum and r not in writes:
                writes.append(r)
        self._deps(e, reads, writes)
        self.cnt[e] += 1
        c = self.cnt[e]
        if signal:
            inc = c - self.sig[e]
            self.sig[e] = c
            h = self.semh[e]
            self.ops[e].append(lambda eng, fn=fn, h=h, inc=inc: fn(eng).then_inc(h, inc))
        else:
            self.ops[e].append(lambda eng, fn=fn: fn(eng))
        tok = (e, c)
        self._commit(tok, reads, writes)
        return tok

    def dma(self, q, out, in_, reads=(), writes=(), is_output=False, **kw):
        self._deps(q, reads, writes)
        half = self.ndma // 2
        if q == "pool":
            i = self.dnext_pool
            self.dnext_pool = (i + 1) % half
        else:
            i = half + self.dnext
            self.dnext = (self.dnext + 1) % (self.ndma - half)
        key = ("d", i)
        prev = self.dval[i]
        if prev > 0:
            self._wait(q, (key, prev))
        self.dval[i] = prev + 16
        tok = (key, prev + 16)
        h = self.semh[key]
        self.ops[q].append(lambda eng, o=out, a=in_, h=h, kw=kw: eng.dma_start(out=o, in_=a, **kw).then_inc(h, 16))
        self._commit(tok, reads, writes)
        if is_output:
            self.out_tokens.append(tok)
        return tok

    def barrier(self):
        toks = []
        for e in ("pe", "act", "dve", "pool"):
            if self.sig[e] > 0:
                toks.append((e, self.sig[e]))
        for i in range(self.ndma):
            if self.dval[i] > 0:
                toks.append((("d", i), self.dval[i]))
        for e in self.ENG:
            for t in toks:
                if t[0] != e:
                    self._wait(e, t)

    def finish(self):
        for i in range(self.ndma):
            if self.dval[i] > 0:
                self._wait("sp", (("d", i), self.dval[i]))
        for e in ("pe", "act", "dve", "pool"):
            if self.sig[e] > 0:
                self._wait("sp", (e, self.sig[e]))

    def emit(self, block):
        nc = self.nc
        ops = self.ops

        @block.tensor
        def _(eng):
            for f in ops["pe"]:
                f(eng)

        @block.scalar
        def _(eng):
            for f in ops["act"]:
                f(eng)

        @block.vector
        def _(eng):
            for f in ops["dve"]:
                f(eng)

        @block.gpsimd
        def _(eng):
            for f in ops["pool"]:
                f(eng)

        @block.sync
        def _(eng):
            for f in ops["sp"]:
                f(eng)


class Buf:
    def __init__(self, t, nres=1, name=""):
        self.t = t
        self.r = [Res("%s[%d]" % (name, i)) for i in range(nres)]

    @property
    def all(self):
        return self.r


class Cfg:
    def __init__(self, ntiles=8, do_sample=True, depth=2, debug_x1=False):
        self.ntiles = ntiles
        self.seq = ntiles * NT
        self.do_sample = do_sample
        self.depth = depth
        self.debug_x1 = debug_x1


def t5_bucket(dist):
    dist = np.asarray(dist)
    max_exact = 16
    large = max_exact + (np.log(np.maximum(dist, max_exact) / max_exact) / np.log(2048 / max_exact)
                         * (32 - max_exact)).astype(np.int32)
    large = np.minimum(large, 31)
    return np.where(dist < max_exact, dist, large).astype(np.int32)


C_PAIRS = ((128, 1), (512, 4), (2048, 16))


class Builder:
    def __init__(self, cfg):
        self.cfg = cfg
        self.nc = bass.Bass("TRN2", target_bir_lowering=False)
        self.es = ExitStack()
        self.S = None
        self.din = {}
        self.dout = {}
        self.scr = {}

    def inp(self, name, shape, dt=F32):
        t = self.nc.dram_tensor(name, list(shape), dt, kind="ExternalInput").ap()
        self.din[name] = t
        return t

    def outp(self, name, shape, dt=F32):
        t = self.nc.dram_tensor(name, list(shape), dt, kind="ExternalOutput").ap()
        self.dout[name] = t
        return t

    def scratch(self, name, shape, dt):
        t = self.nc.dram_tensor(name, list(shape), dt, kind="Internal").ap()
        self.scr[name] = (t, Res(name))
        return t

    def sb(self, name, shape, dt=F32, nres=1):
        t = self.es.enter_context(self.nc.sbuf_tensor(name, list(shape), dt))
        return Buf(t, nres, name)

    def psb(self, name, shape, dt=F32):
        t = self.es.enter_context(self.nc.psum_tensor(name, list(shape), dt))
        b = Buf(t, 1, name)
        b.r[0].psum = True
        return b

    def ps(self):
        b = self.psum[self.psi]
        self.psi = (self.psi + 1) % len(self.psum)
        return b

    def wslot(self):
        b = self.wsl[self.wsi]
        self.wsi = (self.wsi + 1) % len(self.wsl)
        return b

    def declare_io(self):
        cfg = self.cfg
        i = self.inp
        i("xp", [cfg.seq, D])
        i("cp", [1, D])
        i("xs", [16, D])
        i("cs", [4, D])
        i("sa", [4, 2, 512])
        i("sb", [4, 15, 512])
        i("c128", [4, 128, 512])
        i("c512", [4, 512, 512])
        i("c2048", [4, 2048, 512])
        i("sd", [4, 30, 512])
        i("sf", [2, 4, 2, 2 * DFF])
        i("ada_w", [2, D, 6 * D])
        i("ada_b", [2, 6 * D])
        i("norm_g", [2, 4, D])
        i("rel_bias", [32, 12])
        i("ab_w_in", [D, 2048])
        i("a_conv_w", [3, 512])
        i("b_w_grp", [4, 128, 128])
        i("b_scale", [512])
        i("ab_w_out", [D, D])
        i("cd_w_in", [D, 3328])
        i("d_conv_w", [31, 512])
        i("d_conv_b", [512])
        i("d_ln_g", [512])
        i("d_ln_b", [512])
        i("cd_w_out", [768, D])
        i("ffn_w_up", [2, D, 2 * DFF])
        i("ffn_conv_w", [2, 3, 2 * DFF])
        i("ffn_conv_b", [2, 2 * DFF])
        i("ffn_w_down", [2, DFF, D])
        i("oh", [3, 33, 384])
        i("ohs", [3, 128, 132])
        i("dmask", [128, 4])
        i("masks", [16, 3, 132])
        o = self.outp
        o("yp", [cfg.seq, D])
        o("ys", [16, D])
        o("a_p", [2, 512])
        o("a_s", [4, 2, 512])
        o("b_p", [15, 512])
        o("b_s", [4, 15, 512])
        o("c0_p", [min(128, cfg.seq), 512])
        o("c0_s", [4, 128, 512])
        o("c1_p", [min(512, cfg.seq), 512])
        o("c1_s", [4, 512, 512])
        o("c2_p", [min(2048, cfg.seq), 512])
        o("c2_s", [4, 2048, 512])
        o("d_p", [30, 512])
        o("d_s", [4, 30, 512])
        o("f_p", [2, 2, 2 * DFF])
        o("f_s", [2, 4, 2, 2 * DFF])

    def build(self):
        cfg = self.cfg
        nc = self.nc
        es = self.es
        self.declare_io()
        S = self.S = Sched(nc, es)
        self.ident = self.sb("ident", [128, 128], F32)
        self.identb = self.sb("identb", [128, 128], BF16)
        self.onesb = self.sb("onesb", [128, 128], BF16)
        self.onesf = self.sb("onesf", [128, 128], F32)
        self.psum = [self.psb("ps%d" % k, [128, 512], F32) for k in range(5)]
        self.pslong = self.psb("pslong", [128, 512], F32)
        self.psi = 0
        self.psbf = [self.psb("psbf%d" % k, [128, 1024], BF16) for k in range(2)]
        self.psbi = 0
        self.wsl = [self.sb("wsl%d" % k, [128, 4096], BF16) for k in range(3)]
        self.wsi = 0
        self.vstage = self.sb("vstage", [128, 128], F32)
        ident, identb, onesb, onesf = self.ident, self.identb, self.onesb, self.onesf
        S.op("pool", lambda e: e.iota(ident.t[:], pattern=[[1, 128]], base=0, channel_multiplier=-1,
                                      allow_small_or_imprecise_dtypes=True), writes=ident.r)
        S.op("pool", lambda e: e.tensor_single_scalar(ident.t[:], ident.t[:], 0.0, ALU.is_equal),
             reads=ident.r, writes=ident.r)
        S.op("pool", lambda e: e.tensor_copy(identb.t[:], ident.t[:]), reads=ident.r, writes=identb.r)
        S.op("pool", lambda e: e.memset(onesb.t[:], 1.0), writes=onesb.r)
        S.op("pool", lambda e: e.memset(onesf.t[:], 1.0), writes=onesf.r)
        self.epsb = self.sb("epsb", [128, 2], F32)
        S.op("pool", lambda e: e.memset(self.epsb.t[:, 0:1], float(D * EPS)), writes=self.epsb.r)
        S.op("pool", lambda e: e.memset(self.epsb.t[:, 1:2], float(EPS)), writes=self.epsb.r)

        self.W = {}
        self.load_small()
        self.compute_mod()
        self.prep_weights(0)
        self.alloc_tile_bufs()
        for l in range(cfg.depth):
            if l == 0:
                self.layer0()
            else:
                self.layer1()
        S.finish()
        with nc.Block() as block:
            S.emit(block)
        es.close()
        return nc

    def cast_group(self, name, kc, ncols, blocks):
        t = self.scratch(name, [128, kc, ncols], BF16)
        res = self.scr[name][1]
        for src, c0 in blocks:
            w = src.shape[1]
            self.S.dma("pool", out=t[:, :, c0:c0 + w], in_=src.rearrange("(k p) c -> p k c", p=128),
                       writes=[res])
        return (t, res, kc, ncols)

    def prep_weights(self, which):
        d = self.din
        W = self.W
        cfg = self.cfg
        if which == 0:
            self.prep_l0(d, W)
            self.prep_ffn(d, W, 0)
        elif cfg.depth > 1:
            self.prep_l1(d, W)
            self.prep_ffn(d, W, 1)

    def prep_l0(self, d, W):
        wi = d["ab_w_in"]
        W["in0"] = [self.cast_group("w_in0_%d" % i, 8, 512,
                                    [(wi[:, (b * 4 + i) * 128:(b * 4 + i + 1) * 128], b * 128) for b in range(4)])
                    for i in range(4)]
        wo = d["ab_w_out"]
        W["out0"] = [self.cast_group("w_out0_%d" % g, 8, 512, [(wo[:, g * 512:(g + 1) * 512], 0)]) for g in range(2)]

    def prep_ffn(self, d, W, l):
        if True:
            up = d["ffn_w_up"][l]
            W["up%d" % l] = [self.cast_group("w_up%d_%d" % (l, j), 8, 512,
                                             [(up[:, (2 * j) * 128:(2 * j + 1) * 128], 0),
                                              (up[:, DFF + (2 * j) * 128:DFF + (2 * j + 1) * 128], 128),
                                              (up[:, (2 * j + 1) * 128:(2 * j + 2) * 128], 256),
                                              (up[:, DFF + (2 * j + 1) * 128:DFF + (2 * j + 2) * 128], 384)])
                             for j in range(11)]
            dn = d["ffn_w_down"][l]
            W["down%d" % l] = [self.cast_group("w_dn%d_%d" % (l, g), 22, 128, [(dn[:, g * 128:(g + 1) * 128], 0)])
                               for g in range(8)]

    def prep_l1(self, d, W):
        if True:
            ci = d["cd_w_in"]
            W["qk1"] = [self.cast_group("w_qk1_%d" % g, 8, 512, [(ci[:, g * 512:(g + 1) * 512], 0)]) for g in range(3)]
            W["kv1"] = [self.cast_group("w_kv1_%d" % g, 8, 512,
                                        [(ci[:, 768 + g * 256:768 + (g + 1) * 256], 0),
                                         (ci[:, 1536 + g * 256:1536 + (g + 1) * 256], 256)]) for g in range(3)]
            W["d1"] = [self.cast_group("w_d1_%d" % g, 8, 512,
                                       [(ci[:, 2304 + (2 * g) * 128:2304 + (2 * g + 1) * 128], 0),
                                        (ci[:, 2816 + (2 * g) * 128:2816 + (2 * g + 1) * 128], 128),
                                        (ci[:, 2304 + (2 * g + 1) * 128:2304 + (2 * g + 2) * 128], 256),
                                        (ci[:, 2816 + (2 * g + 1) * 128:2816 + (2 * g + 2) * 128], 384)])
                       for g in range(2)]
            co = d["cd_w_out"]
            W["out1"] = [self.cast_group("w_out1_%d" % g, 6, 512, [(co[:, g * 512:(g + 1) * 512], 0)]) for g in range(2)]

    def load_w(self, grp):
        t, res, kc, ncols = grp
        sl = self.wslot()
        self.S.dma("sp", out=sl.t[:, 0:kc * ncols], in_=t.rearrange("p k c -> p (k c)"), reads=[res], writes=sl.r)
        return sl, sl.t[:, 0:kc * ncols].rearrange("p (k c) -> p k c", k=kc)

    def load_vec_T(self, src_rows, dst_ap, dst_res, R):
        S = self.S
        vs = self.vstage
        S.dma("sp", out=vs.t[0:R, :], in_=src_rows, writes=vs.r)
        p = self.ps()
        S.op("pe", lambda e: e.transpose(p.t[:, 0:R], vs.t[0:R, :], self.ident.t[0:R, 0:R]),
             reads=vs.r + self.ident.r, writes=p.r)
        S.op("dve", lambda e: e.tensor_copy(dst_ap, p.t[:, 0:R]), reads=p.r, writes=dst_res)

    def load_small(self):
        d = self.din
        cfg = self.cfg
        V = self.V = {}

        def vec(name, src, R):
            b = self.sb("v_" + name, [128, R], F32)
            for r0 in range(0, R, 128):
                rr = min(128, R - r0)
                self.load_vec_T(src[r0:r0 + rr, :], b.t[:, r0:r0 + rr], b.r, rr)
            V[name] = b
            return b

        for l in range(cfg.depth):
            vec("ada_b%d" % l, d["ada_b"][l].rearrange("(r p) -> r p", p=128), 48)
            vec("ng%d" % l, d["norm_g"][l].rearrange("k (r p) -> (k r) p", p=128), 32)
            vec("fcw%d" % l, d["ffn_conv_w"][l].rearrange("k (r p) -> (k r) p", p=128), 132)
            vec("fcb%d" % l, d["ffn_conv_b"][l].rearrange("(r p) -> r p", p=128), 44)
        vec("acw", d["a_conv_w"].rearrange("k (r p) -> (k r) p", p=128), 12)
        vec("bsc", d["b_scale"].rearrange("(r p) -> r p", p=128), 4)
        if cfg.depth > 1:
            vec("dcw", d["d_conv_w"].rearrange("k (r p) -> (k r) p", p=128), 124)
            vec("dcb", d["d_conv_b"].rearrange("(r p) -> r p", p=128), 4)
            vec("dlg", d["d_ln_g"].rearrange("(r p) -> r p", p=128), 4)
            vec("dlb", d["d_ln_b"].rearrange("(r p) -> r p", p=128), 4)
        self.bw = self.sb("bw", [128, 4, 128], BF16)
        self.S.dma("pool", out=self.bw.t[:], in_=d["b_w_grp"].rearrange("g c d -> c g d"), writes=self.bw.r)

    def compute_mod(self):
        S = self.S
        d = self.din
        cfg = self.cfg
        NS = 5
        crow = self.sb("crow", [8, D], F32)
        S.dma("sp", out=crow.t[0:1, :], in_=d["cp"], writes=crow.r)
        S.dma("sp", out=crow.t[1:5, :], in_=d["cs"], writes=crow.r)
        cT = self.sb("cT", [128, 8, NS], BF16)
        p = self.ps()
        for kc in range(8):
            S.op("pe", lambda e, kc=kc: e.transpose(p.t[:, kc * NS:(kc + 1) * NS], crow.t[0:NS, kc * 128:(kc + 1) * 128],
                                                    self.ident.t[0:NS, 0:NS]),
                 reads=crow.r + self.ident.r, writes=p.r)
        S.op("act", lambda e: e.activation(cT.t[:].rearrange("p k s -> p (k s)"), p.t[:, 0:8 * NS], AF.Silu),
             reads=p.r, writes=cT.r)
        self.modp = []
        for l in range(cfg.depth):
            mod = self.sb("mod%d" % l, [128, 48, NS], F32)
            pm = self.pslong
            aw = d["ada_w"][l]
            for g in range(12):
                sl = self.wslot()
                S.dma("pool", out=sl.t[:, 0:4096].rearrange("p (k c) -> p k c", k=8),
                      in_=aw[:, g * 512:(g + 1) * 512].rearrange("(k p) c -> p k c", p=128), writes=sl.r)
                wv = sl.t[:, 0:4096].rearrange("p (k c) -> p k c", k=8)
                for c4 in range(4):
                    oc = g * 4 + c4
                    for kc in range(8):
                        S.op("pe", lambda e, oc=oc, kc=kc, c4=c4, wv=wv: e.matmul(
                            pm.t[:, oc * NS:(oc + 1) * NS], lhsT=wv[:, kc, c4 * 128:(c4 + 1) * 128], rhs=cT.t[:, kc, :],
                            start=(kc == 0), stop=(kc == 7)),
                             reads=sl.r + cT.r, writes=pm.r, signal=(kc == 7))
            ab = self.V["ada_b%d" % l]
            for s in range(NS):
                S.op("dve", lambda e, s=s: e.tensor_tensor(mod.t[:, :, s], pm.t[:, 0:48 * NS].rearrange("p (c s) -> p c s", s=NS)[:, :, s],
                                                          ab.t[:, :], ALU.add),
                     reads=pm.r + ab.r, writes=mod.r)
            ng = self.V["ng%d" % l]
            mp = {}
            for nm in ("A1", "G1", "A2", "G2"):
                mp[nm] = self.sb("mp%d%s" % (l, nm), [128, 8, NS], F32)
            for s in range(NS):
                S.op("dve", lambda e, s=s: e.scalar_tensor_tensor(mp["A1"].t[:, :, s], mod.t[:, 8:16, s], 1.0, ng.t[:, 0:8], ALU.add, ALU.mult),
                     reads=mod.r + ng.r, writes=mp["A1"].r)
                S.op("dve", lambda e, s=s: e.scalar_tensor_tensor(mp["A2"].t[:, :, s], mod.t[:, 32:40, s], 1.0, ng.t[:, 16:24], ALU.add, ALU.mult),
                     reads=mod.r + ng.r, writes=mp["A2"].r)
                S.op("dve", lambda e, s=s: e.tensor_tensor(mp["G1"].t[:, :, s], mod.t[:, 16:24, s], ng.t[:, 8:16], ALU.mult),
                     reads=mod.r + ng.r, writes=mp["G1"].r)
                S.op("dve", lambda e, s=s: e.tensor_tensor(mp["G2"].t[:, :, s], mod.t[:, 40:48, s], ng.t[:, 24:32], ALU.mult),
                     reads=mod.r + ng.r, writes=mp["G2"].r)
            for nm in ("A1", "G1", "A2", "G2"):
                S.op("dve", lambda e, nm=nm: e.tensor_scalar(mp[nm].t[:], mp[nm].t[:], 32.0, None, ALU.mult),
                     reads=mp[nm].r, writes=mp[nm].r)
            mp["mod"] = mod
            self.modp.append(mp)

    def alloc_tile_bufs(self):
        self.TB = {}
        for kind, nseg, sl in (("p", 1, NT), ("s", 4, 4)):
            if kind == "s" and not self.cfg.do_sample:
                continue
            n = nseg * sl
            B = {}
            B["nseg"], B["sl"], B["n"] = nseg, sl, n
            if kind == "p":
                B["xtok"] = self.sb(kind + "xtok", [128, max(1, n // 128), D], F32)
            else:
                B["xtok"] = Buf(self.TB["p"]["xtok"].t[:, 3:4, :], 1, "sxtok")
                B["xtok"].r = self.TB["p"]["xtok"].r
            B["xT"] = self.sb(kind + "xT", [128, 8, n], F32, nres=8)
            B["hT"] = self.sb(kind + "hT", [128, 8, n], BF16)
            B["rstd"] = self.sb(kind + "rstd", [128, n], F32)
            B["tmp"] = [self.sb(kind + "tmp%d" % k, [128, n], F32) for k in range(4)]
            B["tmpi"] = 0
            B["y"] = self.sb(kind + "y", [128, 8, n], F32, nres=8)
            B["cat"] = self.sb(kind + "cat", [128, 8, n], BF16, nres=8)
            B["act"] = self.sb(kind + "act", [128, 22, n], BF16, nres=22)
            B["sq"] = Buf(B["act"].t, 1, kind + "sq")
            B["sq"].r = B["act"].r[0:8]
            B["pbuf"] = [self.sb(kind + "pbuf%d" % i, [128, nseg, 2 + sl], F32) for i in range(4)]
            B["ud"] = [self.sb(kind + "ud%d" % i, [128, nseg, 15 + sl], F32) for i in range(4)]
            B["ubuf"] = [Buf(u.t[:, :, 0:15 + sl], 1, "ub") for u in B["ud"]]
            for u, v in zip(B["ubuf"], B["ud"]):
                u.r = v.r
            B["sbuf"] = [self.sb(kind + "sbuf%d" % i, [128, nseg, 15 + sl], F32) for i in range(2)]
            B["pool"] = self.sb(kind + "pool", [128, 4, n], BF16, nres=4)
            B["fbuf"] = [self.sb(kind + "fbuf%d" % i, [128, nseg, 2 + sl], F32) for i in range(4)]
            B["fhalo"] = [self.sb(kind + "fhalo%d" % l, [128, 44, nseg, 2], F32) for l in range(self.cfg.depth)]
            B["ostg"] = B["xtok"]
            self.TB[kind] = B
        xk = self.TB["p"]["xtok"]
        self.sstage = Buf(xk.t[:, 0, :], 1, "sstage"); self.sstage.r = xk.r
        self.sstage2 = Buf(xk.t[:, 1, 0:512], 1, "sstage2"); self.sstage2.r = xk.r
        self.rowstg = Buf(xk.t[:, 2, 0:512], 1, "rowstg"); self.rowstg.r = xk.r
        self.rcnt = self.sb("rcnt", [128, 4, 16], F32)
        S = self.S
        S.op("pool", lambda e: e.iota(self.rcnt.t[:, 0, :], pattern=[[1, 16]], base=1, channel_multiplier=0,
                                      allow_small_or_imprecise_dtypes=True), writes=self.rcnt.r)
        for g in range(1, 4):
            S.op("pool", lambda e, g=g: e.tensor_copy(self.rcnt.t[:, g, :], self.rcnt.t[:, 0, :]),
                 reads=self.rcnt.r, writes=self.rcnt.r)
        for g in range(4):
            S.op("pool", lambda e, g=g: e.tensor_scalar(self.rcnt.t[:, g, :], self.rcnt.t[:, g, :], float(2 ** (g + 1)), None, ALU.min),
                 reads=self.rcnt.r, writes=self.rcnt.r)
        S.op("dve", lambda e: e.reciprocal(self.rcnt.t[:], self.rcnt.t[:]), reads=self.rcnt.r, writes=self.rcnt.r)

    def tmp(self, B):
        t = B["tmp"][B["tmpi"]]
        B["tmpi"] = (B["tmpi"] + 1) % len(B["tmp"])
        return t

    def load_x_tokmajor(self, B, src_rows):
        S = self.S
        n = B["n"]
        xt = B["xtok"]
        xT = B["xT"]
        if n >= 128:
            S.dma("sp", out=xt.t[:, :, :], in_=src_rows.rearrange("(s p) f -> p s f", p=128), writes=xt.r)
            ns, rows = n // 128, 128
        else:
            S.dma("sp", out=xt.t[0:n, 0, :], in_=src_rows, writes=xt.r)
            ns, rows = 1, n
        for fc in range(8):
            p = self.ps()
            for s in range(ns):
                S.op("pe", lambda e, fc=fc, s=s, p=p: e.transpose(p.t[:, s * rows:(s + 1) * rows], xt.t[0:rows, s, fc * 128:(fc + 1) * 128],
                                                                  self.ident.t[0:rows, 0:rows]),
                     reads=xt.r + self.ident.r, writes=p.r, signal=(s == ns - 1))
            S.op("act", lambda e, fc=fc, p=p: e.copy(xT.t[:, fc, :], p.t[:, 0:n]), reads=p.r, writes=[xT.r[fc]])

    def store_x_tokmajor(self, B, dst_rows):
        S = self.S
        n = B["n"]
        xT = B["xT"]
        og = B["ostg"]
        ns, rows = (n // 128, 128) if n >= 128 else (1, n)
        for s in range(ns):
            for half in range(2):
                p = self.ps()
                for c4 in range(4):
                    fc = half * 4 + c4
                    S.op("pe", lambda e, fc=fc, s=s, c4=c4, p=p: e.transpose(p.t[0:rows, c4 * 128:(c4 + 1) * 128],
                                                                             xT.t[:, fc, s * rows:(s + 1) * rows], self.ident.t[:, :]),
                         reads=[xT.r[fc]] + self.ident.r, writes=p.r, signal=(c4 == 3))
                S.op("act", lambda e, s=s, half=half, p=p: e.copy(og.t[0:rows, s, half * 512:(half + 1) * 512], p.t[0:rows, :]),
                     reads=p.r, writes=og.r)
        if n >= 128:
            S.dma("act", out=dst_rows.rearrange("(s p) f -> p s f", p=128), in_=og.t[:, :, :], reads=og.r, is_output=True)
        else:
            S.dma("act", out=dst_rows, in_=og.t[0:n, 0, :], reads=og.r, is_output=True)

    def rms_stats(self, B, src, src_res):
        S = self.S
        n = B["n"]
        sq = B["sq"]
        for kc in range(8):
            S.op("act", lambda e, kc=kc: e.activation(sq.t[:, kc, :], src.t[:, kc, :], AF.Square),
                 reads=[src_res[kc]] if len(src_res) == 8 else src_res, writes=sq.r)
        p = self.ps()
        for kc in range(8):
            S.op("pe", lambda e, kc=kc: e.matmul(p.t[:, 0:n], lhsT=self.onesb.t[:, :], rhs=sq.t[:, kc, :], start=(kc == 0), stop=(kc == 7)),
                 reads=sq.r + self.onesb.r, writes=p.r, signal=(kc == 7))
        rstd = B["rstd"]
        S.op("act", lambda e: e.activation(rstd.t[:, :], p.t[:, 0:n], AF.Sqrt, bias=self.epsb.t[:, 0:1], scale=1.0),
             reads=p.r + self.epsb.r, writes=rstd.r)
        S.op("dve", lambda e: e.reciprocal(rstd.t[:, :], rstd.t[:, :]), reads=rstd.r, writes=rstd.r)
        return rstd

    def norm_mod(self, B, seqcols, A, Bsh_mod, bchunk0):
        S = self.S
        xT, hT = B["xT"], B["hT"]
        rstd = self.rms_stats(B, xT, xT.r)
        for kc in range(8):
            for (sc, c0, w) in seqcols:
                t = self.tmp(B)
                S.op("dve", lambda e, kc=kc, sc=sc, c0=c0, w=w, t=t: e.scalar_tensor_tensor(
                    t.t[:, c0:c0 + w], xT.t[:, kc, c0:c0 + w], A.t[:, kc, sc:sc + 1], rstd.t[:, c0:c0 + w], ALU.mult, ALU.mult),
                     reads=[xT.r[kc]] + A.r + rstd.r, writes=t.r)
                S.op("act", lambda e, kc=kc, sc=sc, c0=c0, w=w, t=t: e.activation(
                    hT.t[:, kc, c0:c0 + w], t.t[:, c0:c0 + w], AF.Identity, bias=Bsh_mod.t[:, bchunk0 + kc, sc:sc + 1], scale=1.0),
                     reads=t.r + Bsh_mod.r, writes=hT.r)

    def resid_update(self, B, seqcols, G):
        S = self.S
        xT, y = B["xT"], B["y"]
        rstd = self.rms_stats(B, y, y.r)
        for kc in range(8):
            for (sc, c0, w) in seqcols:
                t = self.tmp(B)
                S.op("dve", lambda e, kc=kc, sc=sc, c0=c0, w=w, t=t: e.scalar_tensor_tensor(
                    t.t[:, c0:c0 + w], y.t[:, kc, c0:c0 + w], G.t[:, kc, sc:sc + 1], rstd.t[:, c0:c0 + w], ALU.mult, ALU.mult),
                     reads=[y.r[kc]] + G.r + rstd.r, writes=t.r)
                S.op("pool", lambda e, kc=kc, c0=c0, w=w, t=t: e.tensor_tensor(
                    xT.t[:, kc, c0:c0 + w], xT.t[:, kc, c0:c0 + w], t.t[:, c0:c0 + w], ALU.add),
                     reads=t.r + [xT.r[kc]], writes=[xT.r[kc]])

    def mm(self, p, n, wv, col0, rhs_fn, nk, reads):
        S = self.S
        for kc in range(nk):
            rhs, rres = rhs_fn(kc)
            S.op("pe", lambda e, kc=kc, rhs=rhs: e.matmul(p.t[:, 0:n], lhsT=wv[:, kc, col0:col0 + 128], rhs=rhs,
                                                          start=(kc == 0), stop=(kc == nk - 1)),
                 reads=reads + rres, writes=p.r, signal=(kc == nk - 1))

    def ffn(self, B, l, seqcols, first, state_src=None):
        S = self.S
        n, nseg, sl = B["n"], B["nseg"], B["sl"]
        hT, act, y = B["hT"], B["act"], B["y"]
        fh = B["fhalo"][l]
        fcw, fcb = self.V["fcw%d" % l], self.V["fcb%d" % l]
        rhs_h = lambda kc: (hT.t[:, kc, :], hT.r)

        def v3(ap):
            return ap.rearrange("p (s t) -> p s t", s=nseg)

        def back(j, ta, tg):
            S.op("act", lambda e, tg=tg: e.activation(tg.t[:, 0:n], tg.t[:, 0:n], AF.Silu), reads=tg.r, writes=tg.r)
            S.op("dve", lambda e, ta=ta, tg=tg, j=j: e.tensor_tensor(act.t[:, j, :], ta.t[:, 0:n], tg.t[:, 0:n], ALU.mult),
                 reads=ta.r + tg.r, writes=[act.r[j]])

        prev = None
        for jj in range(11):
            sl_w, wv = self.load_w(self.W["up%d" % l][jj])
            for j2 in range(2):
                j = jj * 2 + j2
                res = []
                for part in range(2):
                    ch = part * 22 + j
                    p = self.ps()
                    self.mm(p, n, wv, j2 * 256 + part * 128, rhs_h, 8, sl_w.r)
                    fb = B["fbuf"][(j2 * 2 + part) % 4]
                    S.op("act", lambda e, p=p, fb=fb: e.copy(fb.t[:, :, 2:2 + sl], v3(p.t[:, 0:n])), reads=p.r, writes=fb.r)
                    S.op("pool", lambda e, fb=fb, ch=ch: e.tensor_copy(fb.t[:, :, 0:2], fh.t[:, ch, :, :]), reads=fh.r, writes=fb.r)
                    t = self.tmp(B)
                    S.op("act", lambda e, fb=fb, t=t, ch=ch: e.activation(v3(t.t[:, 0:n]), fb.t[:, :, 0:sl], AF.Identity,
                                                                         bias=fcb.t[:, ch:ch + 1], scale=fcw.t[:, ch:ch + 1]),
                         reads=fb.r + fcw.r + fcb.r, writes=t.r)
                    S.op("dve", lambda e, fb=fb, t=t, ch=ch: e.scalar_tensor_tensor(v3(t.t[:, 0:n]), fb.t[:, :, 1:1 + sl], fcw.t[:, 44 + ch:44 + ch + 1],
                                                                                   v3(t.t[:, 0:n]), ALU.mult, ALU.add),
                         reads=fb.r + fcw.r + t.r, writes=t.r)
                    S.op("dve", lambda e, fb=fb, t=t, ch=ch: e.scalar_tensor_tensor(v3(t.t[:, 0:n]), fb.t[:, :, 2:2 + sl], fcw.t[:, 88 + ch:88 + ch + 1],
                                                                                   v3(t.t[:, 0:n]), ALU.mult, ALU.add),
                         reads=fb.r + fcw.r + t.r, writes=t.r)
                    S.op("pool", lambda e, fb=fb, ch=ch: e.tensor_copy(fh.t[:, ch, :, :], fb.t[:, :, sl:sl + 2]), reads=fb.r, writes=fh.r)
                    res.append(t)
                if prev is not None:
                    back(*prev)
                prev = (j, res[0], res[1])
        back(*prev)
        for oc in range(8):
            sl_w, wv = self.load_w(self.W["down%d" % l][oc])
            p = self.ps()
            self.mm(p, n, wv, 0, lambda kc: (act.t[:, kc, :], [act.r[kc]]), 22, sl_w.r)
            S.op("act", lambda e, oc=oc, p=p: e.copy(y.t[:, oc, :], p.t[:, 0:n]), reads=p.r, writes=[y.r[oc]])

    def mixer_ab(self, B, seqcols, first):
        S = self.S
        n, nseg, sl = B["n"], B["nseg"], B["sl"]
        hT, cat, y = B["hT"], B["cat"], B["y"]
        acw, bsc = self.V["acw"], self.V["bsc"]
        rhs_h = lambda kc: (hT.t[:, kc, :], hT.r)

        def v3(ap):
            return ap.rearrange("p (s t) -> p s t", s=nseg)

        for i in range(4):
            sl_w, wv = self.load_w(self.W["in0"][i])
            pA = self.ps(); self.mm(pA, n, wv, 0, rhs_h, 8, sl_w.r)
            pC = self.ps(); self.mm(pC, n, wv, 256, rhs_h, 8, sl_w.r)
            pB = self.ps(); self.mm(pB, n, wv, 128, rhs_h, 8, sl_w.r)
            pU = self.ps(); self.mm(pU, n, wv, 384, rhs_h, 8, sl_w.r)
            t1 = self.tmp(B)
            S.op("act", lambda e, pA=pA, t1=t1: e.copy(t1.t[:, 0:n], pA.t[:, 0:n]), reads=pA.r, writes=t1.r)
            pb = B["pbuf"][i]
            S.op("dve", lambda e, pC=pC, t1=t1, pb=pb: e.tensor_tensor(pb.t[:, :, 2:2 + sl], v3(pC.t[:, 0:n]), v3(t1.t[:, 0:n]), ALU.mult),
                 reads=pC.r + t1.r, writes=pb.r)
            z = self.tmp(B)
            S.op("pool", lambda e, pb=pb, z=z, i=i: e.tensor_scalar(v3(z.t[:, 0:n]), pb.t[:, :, 0:sl], acw.t[:, i:i + 1], None, ALU.mult),
                 reads=pb.r + acw.r, writes=z.r)
            S.op("dve", lambda e, pb=pb, z=z, i=i: e.scalar_tensor_tensor(v3(z.t[:, 0:n]), pb.t[:, :, 1:1 + sl], acw.t[:, 4 + i:5 + i], v3(z.t[:, 0:n]), ALU.mult, ALU.add),
                 reads=pb.r + acw.r + z.r, writes=z.r)
            S.op("dve", lambda e, pb=pb, z=z, i=i: e.scalar_tensor_tensor(v3(z.t[:, 0:n]), pb.t[:, :, 2:2 + sl], acw.t[:, 8 + i:9 + i], v3(z.t[:, 0:n]), ALU.mult, ALU.add),
                 reads=pb.r + acw.r + z.r, writes=z.r)
            S.op("dve", lambda e, pB=pB, z=z, i=i: e.tensor_tensor(cat.t[:, i, :], pB.t[:, 0:n], z.t[:, 0:n], ALU.mult),
                 reads=pB.r + z.r, writes=[cat.r[i]])
            ub = B["ubuf"][i]
            S.op("act", lambda e, pU=pU, ub=ub: e.copy(ub.t[:, :, 15:15 + sl], v3(pU.t[:, 0:n])), reads=pU.r, writes=ub.r)
            cur = ub
            L = 15 + sl
            sh = 1
            for lev in range(i + 1):
                nxt = B["sbuf"][lev % 2]
                lo = 2 * sh - 1
                eng = "dve" if lev % 2 == 0 else "pool"
                S.op(eng, lambda e, cur=cur, nxt=nxt, lo=lo, sh=sh: e.tensor_tensor(nxt.t[:, :, lo:L], cur.t[:, :, lo:L], cur.t[:, :, lo - sh:L - sh], ALU.add),
                     reads=cur.r, writes=nxt.r)
                cur = nxt
                sh *= 2
            win = 2 ** (i + 1)
            pl = B["pool"]
            S.op("dve", lambda e, cur=cur, ub=ub, i=i, win=win: e.scalar_tensor_tensor(
                v3(pl.t[:, i, :]), cur.t[:, :, 15:15 + sl], 1.0 / win, ub.t[:, :, 15:15 + sl], ALU.mult, ALU.subtract),
                 reads=cur.r + ub.r, writes=[pl.r[i]])
            if first:
                S.op("dve", lambda e, cur=cur, i=i: e.tensor_tensor(cur.t[:, 0, 15:30], cur.t[:, 0, 15:30], self.rcnt.t[:, i, 0:15], ALU.mult),
                     reads=cur.r + self.rcnt.r, writes=cur.r)
                S.op("dve", lambda e, cur=cur, ub=ub, i=i: e.tensor_tensor(pl.t[:, i, 0:15], cur.t[:, 0, 15:30], ub.t[:, 0, 15:30], ALU.subtract),
                     reads=cur.r + ub.r, writes=[pl.r[i]])
            pG = self.ps()
            S.op("pe", lambda e, pG=pG, i=i: e.matmul(pG.t[:, 0:n], lhsT=self.bw.t[:, i, :], rhs=pl.t[:, i, :], start=True, stop=True),
                 reads=self.bw.r + [pl.r[i]], writes=pG.r)
            S.op("act", lambda e, pG=pG, i=i: e.activation(cat.t[:, 4 + i, :], pG.t[:, 0:n], AF.Copy, scale=bsc.t[:, i:i + 1]),
                 reads=pG.r + bsc.r, writes=[cat.r[4 + i]])
        for g in range(2):
            sl_w, wv = self.load_w(self.W["out0"][g])
            for c4 in range(4):
                oc = g * 4 + c4
                p = self.ps()
                self.mm(p, n, wv, c4 * 128, lambda kc: (cat.t[:, kc, :], [cat.r[kc]]), 8, sl_w.r)
                S.op("act", lambda e, oc=oc, p=p: e.copy(y.t[:, oc, :], p.t[:, 0:n]), reads=p.r, writes=[y.r[oc]])

    def halo_shift_ab(self, B):
        S = self.S
        sl = B["sl"]
        for i in range(4):
            pb, ub = B["pbuf"][i], B["ubuf"][i]
            S.op("pool", lambda e, pb=pb: e.tensor_copy(pb.t[:, :, 0:2], pb.t[:, :, sl:sl + 2]), reads=pb.r, writes=pb.r)
            S.op("pool", lambda e, ub=ub: e.tensor_copy(ub.t[:, :, 0:15], ub.t[:, :, sl:sl + 15]), reads=ub.r, writes=ub.r)

    def store_rows_T(self, src_fn, nchunks, R, dst_rows_fn, src_res):
        S = self.S
        for c0 in range(0, nchunks, 4):
            nn = min(4, nchunks - c0)
            p = self.ps()
            for c in range(nn):
                S.op("pe", lambda e, c=c, c0=c0, p=p: e.transpose(p.t[0:R, c * 128:(c + 1) * 128], src_fn(c0 + c), self.ident.t[:, :]),
                     reads=src_res + self.ident.r, writes=p.r, signal=(c == nn - 1))
            og = self.rowstg
            S.op("act", lambda e, p=p, nn=nn: e.copy(og.t[0:R, 0:nn * 128], p.t[0:R, 0:nn * 128]), reads=p.r, writes=og.r)
            S.dma("pool", out=dst_rows_fn(c0 * 128, nn * 128), in_=og.t[0:R, 0:nn * 128], reads=og.r, is_output=True)

    def layer0(self):
        S = self.S
        cfg = self.cfg
        d, o = self.din, self.dout
        mp = self.modp[0]
        Bp = self.TB["p"]
        for i in range(4):
            S.op("pool", lambda e, i=i: e.memset(Bp["pbuf"][i].t[:, :, 0:2], 0.0), writes=Bp["pbuf"][i].r)
            S.op("pool", lambda e, i=i: e.memset(Bp["ubuf"][i].t[:, :, 0:15], 0.0), writes=Bp["ubuf"][i].r)
        for i in range(2):
            S.op("pool", lambda e, i=i: e.memset(Bp["sbuf"][i].t[:, :, :], 0.0), writes=Bp["sbuf"][i].r)
        S.op("pool", lambda e: e.memset(Bp["fhalo"][0].t[:], 0.0), writes=Bp["fhalo"][0].r)
        if cfg.depth > 1:
            self.x1 = self.scratch("x1", [D, cfg.seq], F32)
        for ti in range(cfg.ntiles):
            first, last = ti == 0, ti == cfg.ntiles - 1
            t0 = ti * NT
            seqcols = [(0, 0, NT)]
            self.load_x_tokmajor(Bp, d["xp"][t0:t0 + NT, :])
            self.norm_mod(Bp, seqcols, mp["A1"], mp["mod"], 0)
            self.mixer_ab(Bp, seqcols, first)
            if last:
                self.store_rows_T(lambda c: Bp["pbuf"][c].t[:, 0, NT:NT + 2], 4, 2, lambda c0, nc_: o["a_p"][:, c0:c0 + nc_],
                                  sum([Bp["pbuf"][c].r for c in range(4)], []))
                self.store_rows_T(lambda c: Bp["ubuf"][c].t[:, 0, NT:NT + 15], 4, 15, lambda c0, nc_: o["b_p"][:, c0:c0 + nc_],
                                  sum([Bp["ubuf"][c].r for c in range(4)], []))
            else:
                self.halo_shift_ab(Bp)
            self.resid_update(Bp, seqcols, mp["G1"])
            self.norm_mod(Bp, seqcols, mp["A2"], mp["mod"], 24)
            self.ffn(Bp, 0, seqcols, first)
            if last:
                fh = Bp["fhalo"][0]
                for r in range(2):
                    self.store_rows_T(lambda c, r=r: fh.t[:, c, 0, r:r + 1], 44, 1,
                                      lambda c0, nc_, r=r: o["f_p"][0, r:r + 1, c0:c0 + nc_], fh.r)
            self.resid_update(Bp, seqcols, mp["G2"])
            if cfg.depth > 1:
                x1r = self.scr["x1"][1]
                for kc in range(8):
                    S.dma("pool", out=self.x1[kc * 128:(kc + 1) * 128, t0:t0 + NT], in_=Bp["xT"].t[:, kc, :],
                          reads=[Bp["xT"].r[kc]], writes=[x1r])
            if cfg.depth == 1 or cfg.debug_x1:
                self.store_x_tokmajor(Bp, o["yp"][t0:t0 + NT, :])
            if ti == 0:
                self.prep_weights(1)
        if cfg.do_sample:
            self.layer0_sample()

    def layer0_sample(self):
        S = self.S
        cfg = self.cfg
        d, o = self.din, self.dout
        mp = self.modp[0]
        Bs = self.TB["s"]
        seqcols = [(1 + s, 4 * s, 4) for s in range(4)]
        for i in range(4):
            pass
        self.load_state_T(d["sa"].rearrange("s r f -> (s r) f"), 8, 4, lambda c: Bs["pbuf"][c].t[:, :, 0:2], [Bs["pbuf"][c].r for c in range(4)], 4, 2)
        self.load_state_T(d["sb"].rearrange("s r f -> (s r) f"), 60, 4, lambda c: Bs["ubuf"][c].t[:, :, 0:15], [Bs["ubuf"][c].r for c in range(4)], 4, 15)
        fh = Bs["fhalo"][0]
        self.load_state_T(d["sf"][0].rearrange("s r f -> (s r) f"), 8, 44, lambda c: fh.t[:, c, :, :], [fh.r] * 44, 4, 2)
        self.load_x_tokmajor(Bs, d["xs"])
        self.norm_mod(Bs, seqcols, mp["A1"], mp["mod"], 0)
        self.mixer_ab(Bs, seqcols, False)
        self.store_state_T(lambda c: Bs["pbuf"][c].t[:, :, 4:6], sum([Bs["pbuf"][c].r for c in range(4)], []), 4, 4, 2,
                           lambda c0, nc_: o["a_s"].rearrange("s r f -> (s r) f")[:, c0:c0 + nc_])
        self.store_state_T(lambda c: Bs["ubuf"][c].t[:, :, 4:19], sum([Bs["ubuf"][c].r for c in range(4)], []), 4, 4, 15,
                           lambda c0, nc_: o["b_s"].rearrange("s r f -> (s r) f")[:, c0:c0 + nc_])
        self.resid_update(Bs, seqcols, mp["G1"])
        self.norm_mod(Bs, seqcols, mp["A2"], mp["mod"], 24)
        self.ffn(Bs, 0, seqcols, False)
        self.store_state_T(lambda c: fh.t[:, c, :, :], fh.r, 44, 4, 2,
                           lambda c0, nc_: o["f_s"][0].rearrange("s r f -> (s r) f")[:, c0:c0 + nc_])
        self.resid_update(Bs, seqcols, mp["G2"])
        if cfg.depth == 1 or cfg.debug_x1:
            self.store_x_tokmajor(Bs, o["ys"])

    def load_state_T(self, src_rows, R, nchunks, dst_fn, dst_res, nseg, nr):
        S = self.S
        stg = self.sstage
        for c0 in range(0, nchunks, 8):
            nn = min(8, nchunks - c0)
            S.dma("sp", out=stg.t[0:R, 0:nn * 128], in_=src_rows[:, c0 * 128:(c0 + nn) * 128], writes=stg.r)
            for c in range(nn):
                p = self.ps()
                S.op("pe", lambda e, c=c, p=p: e.transpose(p.t[:, 0:R], stg.t[0:R, c * 128:(c + 1) * 128], self.ident.t[0:R, 0:R]),
                     reads=stg.r + self.ident.r, writes=p.r)
                S.op("dve", lambda e, c=c, c0=c0, p=p: e.tensor_copy(dst_fn(c0 + c), p.t[:, 0:R].rearrange("p (s r) -> p s r", s=nseg)),
                     reads=p.r, writes=dst_res[c0 + c])

    def store_state_T(self, src_fn, src_res, nchunks, nseg, nr, dst_rows_fn):
        S = self.S
        R = nseg * nr
        stg = self.sstage2
        for c0 in range(0, nchunks, 4):
            nn = min(4, nchunks - c0)
            p = self.ps()
            for c in range(nn):
                cp = self.cpstg[self.cpi]
                self.cpi = (self.cpi + 1) % 2
                S.op("dve", lambda e, c=c, c0=c0, cp=cp: e.tensor_copy(cp.t[:, 0:R].rearrange("p (s r) -> p s r", s=nseg), src_fn(c0 + c)),
                     reads=src_res, writes=cp.r)
                S.op("pe", lambda e, c=c, p=p, cp=cp: e.transpose(p.t[0:R, c * 128:(c + 1) * 128], cp.t[:, 0:R], self.ident.t[:, :]),
                     reads=cp.r + self.ident.r, writes=p.r)
            S.op("act", lambda e, p=p, nn=nn: e.copy(stg.t[0:R, 0:nn * 128], p.t[0:R, 0:nn * 128]), reads=p.r, writes=stg.r)
            S.dma("pool", out=dst_rows_fn(c0 * 128, nn * 128), in_=stg.t[0:R, 0:nn * 128], reads=stg.r, is_output=True)

    def pbf(self):
        b = self.psbf[self.psbi]
        self.psbi = (self.psbi + 1) % 2
        return b

    def layer1(self):
        S = self.S
        cfg = self.cfg
        d, o = self.din, self.dout
        seq = cfg.seq
        S.barrier()
        self.qsc, self.ksc = [], []
        for g, (win, dil) in enumerate(C_PAIRS):
            self.qsc.append(self.scratch("qsc%d" % g, [2, 128, dil, seq // dil], BF16))
            self.ksc.append(self.scratch("ksc%d" % g, [2, 128, dil, seq // dil], BF16))
        self.vsc = self.scratch("vsc", [seq, 768], BF16)
        self.osc = self.scratch("osc", [seq, 768], F32)
        self.lsc = self.scratch("lsc", [seq, 12], F32)
        self.ydsc = self.scratch("ydsc", [512, seq], BF16)
        self.bsc = self.scratch("bsc", [12, 128, 384], F32)
        Bp = self.TB["p"]
        self.halfb = self.sb("halfb", [128, 1], F32)
        S.op("pool", lambda e: e.memset(self.halfb.t[:], 0.5), writes=self.halfb.r)
        self.qstg = [self.sb("qstg%d" % k, [128, NT], BF16) for k in range(2)]
        self.qsi = 0
        self.kvrow = []
        for k in range(2):
            b_ = Buf(Bp["pbuf"][k].t[:, 0, 0:512], 1, "kvrow%d" % k)
            b_.r = Bp["pbuf"][k].r
            self.kvrow.append(b_)
        self.kvi = 0
        self.vbf = [self.sb("vbf%d" % k, [128, 256], BF16) for k in range(2)]
        self.vbi = 0
        self.rb = self.sb("rb", [32, 12], F32)
        S.dma("sp", out=self.rb.t[:, :], in_=d["rel_bias"], writes=self.rb.r)
        self.ohg = []
        for g in range(3):
            t_ = Buf(Bp["fbuf"][g].t[0:33, 0, 0:384], 1, "ohg%d" % g)
            t_.r = Bp["fbuf"][g].r
            S.dma("sp", out=t_.t, in_=d["oh"][g], writes=t_.r)
            self.ohg.append(t_)
        self.mst = [self.sb("mst0", [128, 4, 4], F32), self.sb("mst1", [128, 4, 3, 4], F32), self.sb("mst2", [128, 4, 4], F32)]
        self.ycf = [self.sb("ycf%d" % k, [128, 256], F32) for k in range(2)]
        self.ycb = [self.sb("ycb%d" % k, [128, 256], BF16) for k in range(2)]
        self.ltile = self.sb("ltile", [128, 4, 3, 4], F32)
        if cfg.do_sample:
            self.qkT_s = self.sb("qkT_s", [128, 12, 16], BF16)
            self.kvn = []
            for k in range(2):
                kv_ = Buf(Bp["xT"].t[0:4, k, 0:512], 1, "kvn%d" % k)
                kv_.r = [Bp["xT"].r[k]]
                self.kvn.append(kv_)
            self.kvni = 0
            self.kvnb = self.sb("kvnb", [4, 4, 3, 256], BF16)
            self.rbrep = self.sb("rbrep", [128, 12], F32)
            for t_ in range(4):
                S.dma("sp", out=self.rbrep.t[t_ * 32:(t_ + 1) * 32, :], in_=d["rel_bias"], writes=self.rbrep.r)
            self.dmask = self.sb("dmask_sb", [128, 4], F32)
            S.dma("sp", out=self.dmask.t[:, :], in_=d["dmask"], writes=self.dmask.r)
            self.ohs = []
            for g in range(3):
                t_ = self.sb("ohs%d" % g, [128, 132], F32)
                S.dma("sp", out=t_.t[:, :], in_=d["ohs"][g], writes=t_.r)
                self.ohs.append(t_)
            self.masks = self.sb("masks_sb", [16, 3, 132], F32)
            S.dma("sp", out=self.masks.t[:, :, :], in_=d["masks"], writes=self.masks.r)
        S.op("pool", lambda e: e.memset(Bp["fhalo"][1].t[:], 0.0), writes=Bp["fhalo"][1].r)
        self.dg = [self.sb("dg%d" % k, [128, 128], BF16) for k in range(8)]
        self.dgi = 0
        for kind, B_ in self.TB.items():
            B_["dbb"] = [self.sb(kind + "dbb%d" % i, [128, B_["nseg"], 30 + B_["sl"]], BF16) for i in range(4)]
            B_["dbo"] = [self.sb(kind + "dbo%d" % i, [128, B_["nseg"], 30 + B_["sl"]], BF16) for i in range(4)]
            B_["dst"] = self.sb(kind + "dst", [128, 4, B_["nseg"], 30], F32)
        for i in range(4):
            S.op("pool", lambda e, i=i: e.memset(Bp["dbb"][i].t[:, :, 0:30], 0.0), writes=Bp["dbb"][i].r)
        if os.environ.get("KDEBUG_STOP") == "l1start":
            return
        for ti in range(cfg.ntiles):
            self.l1_p1(Bp, ti * NT, [(0, 0, NT)], last=(ti == cfg.ntiles - 1))
        if os.environ.get("KDEBUG_STOP") in ("p1a", "p1b", "p1p"):
            return
        if cfg.do_sample:
            self.l1_p1_sample()
        if os.environ.get("KDEBUG_STOP") in ("p1", "s1", "s2", "s3", "s4"):
            return
        S.barrier()
        self.l1_attn_setup()
        if os.environ.get("KDEBUG_STOP") == "p2setup":
            return
        self.l1_attn_prompt()
        if os.environ.get("KDEBUG_STOP") == "p2p":
            return
        if cfg.do_sample:
            self.l1_attn_sample()
        if os.environ.get("KDEBUG_STOP") == "p2":
            return
        S.barrier()
        for ti in range(cfg.ntiles):
            self.l1_p3(Bp, ti * NT, [(0, 0, NT)], last=(ti == cfg.ntiles - 1))
        if cfg.do_sample:
            self.l1_p3_sample()

    def l1_dpath(self, B, seqcols, ydst_fn, want_state):
        S = self.S
        n, nseg, sl = B["n"], B["nseg"], B["sl"]
        hT, y = B["hT"], B["y"]
        dcw, dcb, dlg, dlb = self.V["dcw"], self.V["dcb"], self.V["dlg"], self.V["dlb"]
        rhs_h = lambda kc: (hT.t[:, kc, :], hT.r)

        def v3(ap):
            return ap.rearrange("p (s t) -> p s t", s=nseg)

        zb = y
        sq = B["sq"]
        dstate = B["dst"]
        for g2 in range(2):
            sl_w, wv = self.load_w(self.W["d1"][g2])
            for i2 in range(2):
                i = g2 * 2 + i2
                pV = self.ps(); self.mm(pV, n, wv, i2 * 256, rhs_h, 8, sl_w.r)
                pG = self.ps(); self.mm(pG, n, wv, i2 * 256 + 128, rhs_h, 8, sl_w.r)
                t1 = self.tmp(B)
                S.op("act", lambda e, pG=pG, t1=t1: e.activation(t1.t[:, 0:n], pG.t[:, 0:n], AF.Tanh, scale=0.5), reads=pG.r, writes=t1.r)
                S.op("act", lambda e, t1=t1: e.activation(t1.t[:, 0:n], t1.t[:, 0:n], AF.Identity, bias=self.halfb.t[:, 0:1], scale=0.5), reads=t1.r + self.halfb.r, writes=t1.r)
                db = B["dbb"][i]
                S.op("dve", lambda e, pV=pV, t1=t1, db=db: e.tensor_tensor(db.t[:, :, 30:30 + sl], v3(pV.t[:, 0:n]), v3(t1.t[:, 0:n]), ALU.mult),
                     reads=pV.r + t1.r, writes=db.r)
                if want_state:
                    k = min(30, sl)
                    S.op("dve", lambda e, pV=pV, t1=t1, i=i, k=k: e.tensor_tensor(dstate.t[:, i, :, 30 - k:30], v3(pV.t[:, 0:n])[:, :, sl - k:sl], v3(t1.t[:, 0:n])[:, :, sl - k:sl], ALU.mult),
                         reads=pV.r + t1.r, writes=dstate.r)
                dbo = B["dbo"][i]
                S.op("pool", lambda e, db=db, dbo=dbo: e.tensor_copy(dbo.t[:, :, 0:29 + sl], db.t[:, :, 1:30 + sl]), reads=db.r, writes=dbo.r)
                pc = self.ps()
                for j in range(31):
                    dg = self.dg[self.dgi]
                    self.dgi = (self.dgi + 1) % len(self.dg)
                    S.op("dve", lambda e, dg=dg, j=j, i=i: e.tensor_scalar(dg.t[:, :], self.identb.t[:, :], dcw.t[:, j * 4 + i:j * 4 + i + 1], None, ALU.mult),
                         reads=self.identb.r + dcw.r, writes=dg.r)
                    src, jj = (db, j) if j % 2 == 0 else (dbo, j - 1)
                    rhs = src.t[:, 0, jj:jj + sl] if nseg == 1 else src.t[:, :, jj:jj + sl]
                    S.op("pe", lambda e, dg=dg, rhs=rhs, pc=pc, j=j: e.matmul(pc.t[:, 0:n], lhsT=dg.t[:, :], rhs=rhs, start=(j == 0), stop=(j == 30)),
                         reads=dg.r + src.r, writes=pc.r)
                S.op("act", lambda e, pc=pc, i=i: e.activation(zb.t[:, i, :], pc.t[:, 0:n], AF.Identity, bias=dcb.t[:, i:i + 1], scale=1.0),
                     reads=pc.r + dcb.r, writes=zb.r)
                S.op("act", lambda e, pc=pc, i=i: e.activation(sq.t[:, 4 + i, :], pc.t[:, 0:n], AF.Square, bias=dcb.t[:, i:i + 1], scale=1.0),
                     reads=pc.r + dcb.r, writes=sq.r)
                S.op("act", lambda e, pc=pc, i=i: e.activation(sq.t[:, i, :], pc.t[:, 0:n], AF.Identity, bias=dcb.t[:, i:i + 1], scale=1.0),
                     reads=pc.r + dcb.r, writes=sq.r)
        p1 = self.ps()
        for i in range(4):
            S.op("pe", lambda e, i=i: e.matmul(p1.t[:, 0:n], lhsT=self.onesb.t[:, :], rhs=sq.t[:, i, :], start=(i == 0), stop=(i == 3)),
                 reads=sq.r + self.onesb.r, writes=p1.r, signal=(i == 3))
        p2 = self.ps()
        for i in range(4):
            S.op("pe", lambda e, i=i: e.matmul(p2.t[:, 0:n], lhsT=self.onesb.t[:, :], rhs=sq.t[:, 4 + i, :], start=(i == 0), stop=(i == 3)),
                 reads=sq.r + self.onesb.r, writes=p2.r, signal=(i == 3))
        mean = Buf(zb.t[:, 4, :], 1, "ln_mean"); mean.r = zb.r
        var = Buf(zb.t[:, 5, :], 1, "ln_var"); var.r = zb.r
        S.op("dve", lambda e: e.tensor_scalar(mean.t[:, 0:n], p1.t[:, 0:n], 1.0 / 512, None, ALU.mult), reads=p1.r, writes=mean.r)
        S.op("dve", lambda e: e.tensor_tensor(var.t[:, 0:n], mean.t[:, 0:n], mean.t[:, 0:n], ALU.mult), reads=mean.r, writes=var.r)
        S.op("dve", lambda e: e.scalar_tensor_tensor(var.t[:, 0:n], p2.t[:, 0:n], 1.0 / 512, var.t[:, 0:n], ALU.mult, ALU.subtract),
             reads=p2.r + var.r, writes=var.r)
        S.op("act", lambda e: e.activation(var.t[:, 0:n], var.t[:, 0:n], AF.Sqrt, bias=self.epsb.t[:, 1:2], scale=1.0), reads=var.r + self.epsb.r, writes=var.r)
        S.op("dve", lambda e: e.reciprocal(var.t[:, 0:n], var.t[:, 0:n]), reads=var.r, writes=var.r)
        for i in range(4):
            t = self.tmp(B)
            S.op("dve", lambda e, i=i, t=t: e.tensor_tensor(t.t[:, 0:n], zb.t[:, i, :], mean.t[:, 0:n], ALU.subtract), reads=zb.r + mean.r, writes=t.r)
            S.op("dve", lambda e, t=t: e.tensor_tensor(t.t[:, 0:n], t.t[:, 0:n], var.t[:, 0:n], ALU.mult), reads=t.r + var.r, writes=t.r)
            dst, dres = ydst_fn(i)
            S.op("act", lambda e, i=i, t=t, dst=dst: e.activation(dst, t.t[:, 0:n], AF.Silu, bias=dlb.t[:, i:i + 1], scale=dlg.t[:, i:i + 1]),
                 reads=t.r + dlb.r + dlg.r, writes=dres)

    def l1_p1(self, B, t0, seqcols, last):
        S = self.S
        cfg = self.cfg
        d, o = self.din, self.dout
        mp = self.modp[1]
        n = NT
        xT, hT, cat = B["xT"], B["hT"], B["cat"]
        x1r = self.scr["x1"][1]
        for kc in range(8):
            S.dma("sp", out=xT.t[:, kc, :], in_=self.x1[kc * 128:(kc + 1) * 128, t0:t0 + NT], reads=[x1r], writes=[xT.r[kc]])
        self.norm_mod(B, seqcols, mp["A1"], mp["mod"], 0)
        rhs_h = lambda kc: (hT.t[:, kc, :], hT.r)
        for grp in range(3):
            sl_w, wv = self.load_w(self.W["qk1"][grp])
            for c4 in range(4):
                cc = grp * 4 + c4
                isq = cc < 6
                g = (cc // 2) if isq else ((cc - 6) // 2)
                half = cc % 2
                dil = C_PAIRS[g][1]
                M = NT // dil
                p = self.ps()
                self.mm(p, n, wv, c4 * 128, rhs_h, 8, sl_w.r)
                stg = self.qstg[self.qsi]
                self.qsi = (self.qsi + 1) % len(self.qstg)
                S.op("act", lambda e, p=p, stg=stg, dil=dil, isq=isq: e.activation(
                    stg.t[:, 0:n].rearrange("p (r m) -> p m r", r=dil), p.t[:, 0:n].rearrange("p (m r) -> p m r", r=dil),
                    AF.Copy, scale=(0.125 if isq else 1.0)), reads=p.r, writes=stg.r)
                dst = (self.qsc if isq else self.ksc)[g]
                dres = self.scr[("qsc%d" if isq else "ksc%d") % g][1]
                S.dma("act", out=dst[half, :, :, t0 // dil:t0 // dil + M], in_=stg.t[:, 0:n].rearrange("p (r m) -> p r m", r=dil),
                      reads=stg.r, writes=[dres])
        if os.environ.get("KDEBUG_STOP") == "p1a":
            return
        vres = self.scr["vsc"][1]
        for g in range(3):
            wb = C_PAIRS[g][0]
            sl_w, wv = self.load_w(self.W["kv1"][g])
            for s in range(4):
                p = self.ps()
                for kc in range(8):
                    S.op("pe", lambda e, kc=kc, s=s, p=p, wv=wv: e.matmul(p.t[:, 0:512], lhsT=hT.t[:, kc, s * 128:(s + 1) * 128], rhs=wv[:, kc, 0:512],
                                                                        start=(kc == 0), stop=(kc == 7)),
                         reads=hT.r + sl_w.r, writes=p.r, signal=(kc == 7))
                kv = self.kvrow[self.kvi]
                self.kvi = (self.kvi + 1) % 2
                S.op("act", lambda e, p=p, kv=kv: e.copy(kv.t[:, 0:512], p.t[:, 0:512]), reads=p.r, writes=kv.r)
                tok = t0 + s * 128
                row = tok - (cfg.seq - min(wb, cfg.seq))
                vb = self.vbf[self.vbi]
                self.vbi = (self.vbi + 1) % 2
                S.op("act", lambda e, p=p, vb=vb: e.copy(vb.t[:, :], p.t[:, 256:512]), reads=p.r, writes=vb.r)
                if row >= 0:
                    S.dma("act", out=o["c%d_p" % g][row:row + 128, :], in_=kv.t[:, 0:512], reads=kv.r, is_output=True)
                S.dma("act", out=self.vsc[tok:tok + 128, g * 256:(g + 1) * 256], in_=vb.t[:, :], reads=vb.r, writes=[vres])
        if os.environ.get("KDEBUG_STOP") == "p1b":
            return
        self.l1_dpath(B, seqcols, lambda i: (cat.t[:, 2 + i, :], [cat.r[2 + i]]), want_state=last)
        ydr = self.scr["ydsc"][1]
        for i in range(4):
            S.dma("act", out=self.ydsc[i * 128:(i + 1) * 128, t0:t0 + NT], in_=cat.t[:, 2 + i, :], reads=[cat.r[2 + i]], writes=[ydr])
        if last:
            self.store_rows_T(lambda c: B["dst"].t[:, c, 0, :], 4, 30, lambda c0, nc_: o["d_p"][:, c0:c0 + nc_], B["dst"].r)
        else:
            for i in range(4):
                db = B["dbb"][i]
                S.op("pool", lambda e, db=db: e.tensor_copy(db.t[:, :, 0:30], db.t[:, :, NT:NT + 30]), reads=db.r, writes=db.r)

    def l1_attn_setup(self):
        S = self.S
        d = self.din
        Bp = self.TB["p"]
        y = Bp["y"]
        self.biasmat = []
        for gh in range(12):
            bm = Buf(y.t[:, gh // 2, (gh % 2) * 256:(gh % 2) * 256 + 256], 1, "biasmat%d" % gh)
            self.biasmat.append(bm)
        rb = self.rb
        self.bl = [self.sb("bl%d" % k, [128, 128], BF16) for k in range(2)]
        hT_, act_ = Bp["hT"], Bp["act"]
        self.ohgb = [Buf(hT_.t[0:33, 6, 0:384], 1, "ohgb0"), Buf(hT_.t[0:33, 7, 0:384], 1, "ohgb1"), Buf(act_.t[0:33, 16, 0:384], 1, "ohgb2")]
        for g in range(3):
            S.op("dve", lambda e, g=g: e.tensor_copy(self.ohgb[g].t, self.ohg[g].t), reads=self.ohg[g].r, writes=self.ohgb[g].r)
        bres = self.scr["bsc"][1]
        for g in range(3):
            ohg = self.ohg[g]
            for h in range(4):
                gh = g * 4 + h
                lt = self.tmp(Bp)
                S.op("dve", lambda e, lt=lt: e.memset(lt.t[0:33, 0:128], 1.0), writes=lt.r)
                S.op("dve", lambda e, lt=lt, gh=gh: e.tensor_scalar(lt.t[0:32, 0:128], lt.t[0:32, 0:128], rb.t[0:32, gh:gh + 1], None, ALU.mult),
                     reads=lt.r + rb.r, writes=lt.r)
                hi, lo = self.bl[0], self.bl[1]
                S.op("dve", lambda e, lt=lt, hi=hi: e.tensor_copy(hi.t[0:33, :], lt.t[0:33, 0:128]), reads=lt.r, writes=hi.r)
                S.op("dve", lambda e, lt=lt, hi=hi: e.tensor_tensor(lt.t[0:33, 0:128], lt.t[0:33, 0:128], hi.t[0:33, :], ALU.subtract), reads=lt.r + hi.r, writes=lt.r)
                S.op("dve", lambda e, lt=lt, lo=lo: e.tensor_copy(lo.t[0:33, :], lt.t[0:33, 0:128]), reads=lt.r, writes=lo.r)
                p = self.ps()
                S.op("pe", lambda e, hi=hi, p=p, g=g: e.matmul(p.t[:, 0:384], lhsT=hi.t[0:33, :], rhs=self.ohgb[g].t, start=True, stop=False),
                     reads=hi.r + self.ohgb[g].r, writes=p.r)
                S.op("pe", lambda e, lo=lo, p=p, g=g: e.matmul(p.t[:, 0:384], lhsT=lo.t[0:33, :], rhs=self.ohgb[g].t, start=False, stop=True),
                     reads=lo.r + self.ohgb[g].r, writes=p.r)
                rt = self.tmp(Bp)
                S.op("act", lambda e, rt=rt, p=p: e.copy(rt.t[:, 0:384], p.t[:, 0:384]), reads=p.r, writes=rt.r)
                S.dma("sp", out=self.bsc[gh], in_=rt.t[:, 0:384], reads=rt.r, writes=[bres])
                skew = bass.AP(tensor=self.bsc.tensor, offset=gh * 128 * 384, ap=[[383, 128], [1, 256]])
                S.dma("sp", out=self.biasmat[gh].t, in_=skew, reads=[bres], writes=self.biasmat[gh].r)

    def l1_attn_prompt(self):
        S = self.S
        cfg = self.cfg
        Bp = self.TB["p"]
        hT, act = Bp["hT"], Bp["act"]
        seq = cfg.seq
        def view(ap, name):
            return Buf(ap, 1, name)
        qt = [view(hT.t[:, k, 0:256].rearrange("p (c q) -> p c q", c=2), "qt%d" % k) for k in range(2)]
        kt = [view(hT.t[:, 2 + k, 0:512].rearrange("p (c q) -> p c q", c=2), "kt%d" % k) for k in range(2)]
        vt = [view(hT.t[:, 4 + k, 0:512].rearrange("p (c q) -> p c q", c=2), "vt%d" % k) for k in range(2)]
        pb_ = [view(act.t[:, k, 0:256], "pbf%d" % k) for k in range(4)]
        pT = [view(act.t[:, 4 + k, 0:256].rearrange("p (c q) -> p c q", c=2), "pT%d" % k) for k in range(4)]
        sc = [view(Bp["pbuf"][k].t[:, 0, 0:256], "sc%d" % k) for k in range(4)]
        ost = [view(Bp["xT"].t[:, k, 0:256].rearrange("p (h e) -> p h e", h=4), "ost%d" % k) for k in range(2)]
        lst = [view(Bp["xT"].t[:, 2 + k, 0:4], "lst%d" % k) for k in range(2)]
        st = [view(Bp["xT"].t[:, 4 + k, 0:8], "stat%d" % k) for k in range(4)]
        osr, lsr, vres = self.scr["osc"][1], self.scr["lsc"][1], self.scr["vsc"][1]
        ui = 0
        hi = 0
        for g, (win, dil) in enumerate(C_PAIRS):
            L = seq // dil
            vview = self.vsc.rearrange("(m r) f -> r m f", r=dil)
            oview = self.osc.rearrange("(m r) f -> r m f", r=dil)
            lview = self.lsc.rearrange("(m r) f -> r m f", r=dil)
            qres, kres = self.scr["qsc%d" % g][1], self.scr["ksc%d" % g][1]
            for r in range(dil):
                for Bk in range(L // 128):
                    m0 = Bk * 128
                    nk = 128 if Bk == 0 else 256
                    k0 = 256 - nk
                    q_, k_, v_ = qt[ui % 2], kt[ui % 2], vt[ui % 2]
                    o_, l_ = ost[ui % 2], lst[ui % 2]
                    ui += 1
                    S.dma("sp", out=q_.t, in_=self.qsc[g][:, :, r, m0:m0 + 128].rearrange("c p q -> p c q"), reads=[qres], writes=q_.r)
                    S.dma("sp", out=k_.t[:, :, k0:256], in_=self.ksc[g][:, :, r, m0 + 128 - nk:m0 + 128].rearrange("c p q -> p c q"),
                          reads=[kres], writes=k_.r)
                    for hh in range(2 - nk // 128, 2):
                        ms = m0 - 128 + hh * 128
                        S.dma("sp", out=v_.t[:, hh, :], in_=vview[r, ms:ms + 128, g * 256:(g + 1) * 256], reads=[vres], writes=v_.r)
                    hs = []
                    for h in range(4):
                        c, pb0 = h // 2, (h % 2) * 64
                        gh = g * 4 + h
                        p = self.ps()
                        S.op("pe", lambda e, p=p, q_=q_, k_=k_, c=c, pb0=pb0, k0=k0: e.matmul(
                            p.t[:, k0:256], lhsT=q_.t[pb0:pb0 + 64, c, :], rhs=k_.t[pb0:pb0 + 64, c, k0:256], start=True, stop=True),
                             reads=q_.r + k_.r, writes=p.r)
                        s_ = sc[hi % 4]; pb = pb_[hi % 4]; pt = pT[hi % 4]; stt = st[hi % 4]
                        hi += 1
                        hs.append((h, pb, pt, stt))
                        bm = self.biasmat[gh]
                        S.op("dve", lambda e, p=p, s_=s_, bm=bm, k0=k0: e.tensor_tensor(s_.t[:, k0:256], p.t[:, k0:256], bm.t[:, k0:256], ALU.add),
                             reads=p.r + bm.r, writes=s_.r)
                        S.op("dve", lambda e, s_=s_, stt=stt, k0=k0: e.tensor_reduce(stt.t[:, 0:1], s_.t[:, k0:256], AX.X, ALU.max),
                             reads=s_.r, writes=stt.r)
                        S.op("dve", lambda e, stt=stt: e.tensor_scalar(stt.t[:, 1:2], stt.t[:, 0:1], -1.0, None, ALU.mult), reads=stt.r, writes=stt.r)
                        S.op("act", lambda e, s_=s_, pb=pb, stt=stt, k0=k0: e.activation(pb.t[:, k0:256], s_.t[:, k0:256], AF.Exp, bias=stt.t[:, 1:2], scale=1.0,
                                                                                       accum_out=stt.t[:, 2:3]),
                             reads=s_.r + stt.r, writes=pb.r + stt.r)
                        S.op("dve", lambda e, stt=stt: e.reciprocal(stt.t[:, 3:4], stt.t[:, 2:3]), reads=stt.r, writes=stt.r)
                    for (h, pb, pt, stt) in hs:
                        pp = self.pbf()
                        for hh in range(2 - nk // 128, 2):
                            S.op("pe", lambda e, pp=pp, pb=pb, hh=hh: e.transpose(pp.t[:, hh * 128:(hh + 1) * 128], pb.t[:, hh * 128:(hh + 1) * 128], self.identb.t[:, :]),
                                 reads=pb.r + self.identb.r, writes=pp.r, signal=(hh == 1))
                        S.op("act", lambda e, pp=pp, pt=pt, k0=k0: e.copy(pt.t[:, :, :].rearrange("p c q -> p (c q)")[:, k0:256], pp.t[:, k0:256]),
                             reads=pp.r, writes=pt.r)
                        po = self.ps()
                        for hh in range(2 - nk // 128, 2):
                            S.op("pe", lambda e, po=po, pt=pt, v_=v_, hh=hh, h=h, nk=nk: e.matmul(
                                po.t[:, 0:64], lhsT=pt.t[:, hh, :], rhs=v_.t[:, hh, h * 64:(h + 1) * 64], start=(hh == 2 - nk // 128), stop=(hh == 1)),
                                 reads=pt.r + v_.r, writes=po.r, signal=(hh == 1))
                        S.op("act", lambda e, po=po, o_=o_, stt=stt, h=h: e.activation(o_.t[:, h, :], po.t[:, 0:64], AF.Copy, scale=stt.t[:, 3:4]),
                             reads=po.r + stt.r, writes=o_.r)
                        S.op("act", lambda e, stt=stt: e.activation(stt.t[:, 4:5], stt.t[:, 2:3], AF.Ln), reads=stt.r, writes=stt.r)
                        S.op("act", lambda e, stt=stt, l_=l_, h=h: e.activation(l_.t[:, h:h + 1], stt.t[:, 4:5], AF.Identity, bias=stt.t[:, 0:1], scale=1.0),
                             reads=stt.r, writes=l_.r)
                    S.dma("act", out=oview[r, m0:m0 + 128, g * 256:(g + 1) * 256], in_=o_.t.rearrange("p h e -> p (h e)"), reads=o_.r, writes=[osr])
                    S.dma("act", out=lview[r, m0:m0 + 128, g * 4:(g + 1) * 4], in_=l_.t, reads=l_.r, writes=[lsr])

    def l1_merge(self, B, n, otile, ltile):
        S = self.S
        ns = max(1, n // 128)
        rows = min(n, 128)
        cat = B["cat"]
        mx, ex, sm = self.mst[0], self.mst[1], self.mst[2]
        lt = ltile.t
        R = slice(0, rows)
        S.op("dve", lambda e: e.tensor_tensor(mx.t[R, 0:ns, :], lt[R, :, 0, :], lt[R, :, 1, :], ALU.max), reads=ltile.r, writes=mx.r)
        S.op("dve", lambda e: e.tensor_tensor(mx.t[R, 0:ns, :], mx.t[R, 0:ns, :], lt[R, :, 2, :], ALU.max), reads=ltile.r + mx.r, writes=mx.r)
        for g in range(3):
            S.op("dve", lambda e, g=g: e.tensor_tensor(ex.t[R, 0:ns, g, :], lt[R, :, g, :], mx.t[R, 0:ns, :], ALU.subtract), reads=ltile.r + mx.r, writes=ex.r)
        S.op("act", lambda e: e.activation(ex.t[R, 0:ns, :, :].rearrange("p s g h -> p (s g h)"), ex.t[R, 0:ns, :, :].rearrange("p s g h -> p (s g h)"), AF.Exp),
             reads=ex.r, writes=ex.r)
        S.op("dve", lambda e: e.tensor_tensor(sm.t[R, 0:ns, :], ex.t[R, 0:ns, 0, :], ex.t[R, 0:ns, 1, :], ALU.add), reads=ex.r, writes=sm.r)
        S.op("dve", lambda e: e.tensor_tensor(sm.t[R, 0:ns, :], sm.t[R, 0:ns, :], ex.t[R, 0:ns, 2, :], ALU.add), reads=ex.r + sm.r, writes=sm.r)
        S.op("dve", lambda e: e.reciprocal(sm.t[R, 0:ns, :], sm.t[R, 0:ns, :]), reads=sm.r, writes=sm.r)
        for g in range(3):
            S.op("dve", lambda e, g=g: e.tensor_tensor(ex.t[R, 0:ns, g, :], ex.t[R, 0:ns, g, :], sm.t[R, 0:ns, :], ALU.mult), reads=ex.r + sm.r, writes=ex.r)
        for s in range(ns):
            yc = self.ycf[s % 2]
            for h in range(4):
                for g in range(3):
                    src = otile.t[R, s, g, h * 64:(h + 1) * 64]
                    wcol = ex.t[R, s, g, h:h + 1]
                    if g == 0:
                        S.op("dve", lambda e, yc=yc, src=src, wcol=wcol, h=h: e.tensor_scalar(yc.t[R, h * 64:(h + 1) * 64], src, wcol, None, ALU.mult),
                             reads=otile.r + ex.r, writes=yc.r)
                    else:
                        S.op("dve", lambda e, yc=yc, src=src, wcol=wcol, h=h: e.scalar_tensor_tensor(
                            yc.t[R, h * 64:(h + 1) * 64], src, wcol, yc.t[R, h * 64:(h + 1) * 64], ALU.mult, ALU.add),
                             reads=otile.r + ex.r + yc.r, writes=yc.r)
            ycb = self.ycb[s % 2]
            S.op("act", lambda e, yc=yc, ycb=ycb: e.copy(ycb.t[R, :], yc.t[R, :]), reads=yc.r, writes=ycb.r)
            pp = self.pbf()
            for c in range(2):
                S.op("pe", lambda e, pp=pp, ycb=ycb, c=c: e.transpose(pp.t[:, c * 128:c * 128 + rows], ycb.t[R, c * 128:(c + 1) * 128], self.identb.t[R, R]),
                     reads=ycb.r + self.identb.r, writes=pp.r, signal=(c == 1))
            for c in range(2):
                S.op("act", lambda e, pp=pp, c=c, s=s: e.copy(cat.t[:, c, s * rows:(s + 1) * rows], pp.t[:, c * 128:c * 128 + rows]),
                     reads=pp.r, writes=[cat.r[c]])

    def l1_p3(self, B, t0, seqcols, last):
        S = self.S
        cfg = self.cfg
        d, o = self.din, self.dout
        mp = self.modp[1]
        n = NT
        xT, cat, y = B["xT"], B["cat"], B["y"]
        x1r = self.scr["x1"][1]
        for kc in range(8):
            S.dma("sp", out=xT.t[:, kc, :], in_=self.x1[kc * 128:(kc + 1) * 128, t0:t0 + NT], reads=[x1r], writes=[xT.r[kc]])
        xt = B["xtok"]
        otile = Buf(xt.t[:, :, 0:768].rearrange("p s (g f) -> p s g f", g=3), 1, "otile")
        otile.r = xt.r
        S.dma("sp", out=xt.t[:, :, 0:768], in_=self.osc[t0:t0 + NT, :].rearrange("(s p) f -> p s f", p=128), reads=[self.scr["osc"][1]], writes=xt.r)
        lt = self.ltile
        S.dma("sp", out=lt.t[:, :, :, :].rearrange("p s g h -> p s (g h)"), in_=self.lsc[t0:t0 + NT, :].rearrange("(s p) f -> p s f", p=128),
              reads=[self.scr["lsc"][1]], writes=lt.r)
        self.l1_merge(B, n, otile, lt)
        ydr = self.scr["ydsc"][1]
        for i in range(4):
            S.dma("sp", out=cat.t[:, 2 + i, :], in_=self.ydsc[i * 128:(i + 1) * 128, t0:t0 + NT], reads=[ydr], writes=[cat.r[2 + i]])
        self.l1_tail(B, seqcols, last, lambda: self.store_x_tokmajor(B, o["yp"][t0:t0 + NT, :]), prompt=True)

    def l1_tail(self, B, seqcols, last, store_fn, prompt):
        S = self.S
        o = self.dout
        mp = self.modp[1]
        n = B["n"]
        cat, y = B["cat"], B["y"]
        for g in range(2):
            sl_w, wv = self.load_w(self.W["out1"][g])
            for c4 in range(4):
                oc = g * 4 + c4
                p = self.ps()
                self.mm(p, n, wv, c4 * 128, lambda kc: (cat.t[:, kc, :], [cat.r[kc]]), 6, sl_w.r)
                S.op("act", lambda e, oc=oc, p=p: e.copy(y.t[:, oc, :], p.t[:, 0:n]), reads=p.r, writes=[y.r[oc]])
        self.resid_update(B, seqcols, mp["G1"])
        self.norm_mod(B, seqcols, mp["A2"], mp["mod"], 24)
        self.ffn(B, 1, seqcols, False)
        fh = B["fhalo"][1]
        if prompt and last:
            for r in range(2):
                self.store_rows_T(lambda c, r=r: fh.t[:, c, 0, r:r + 1], 44, 1,
                                  lambda c0, nc_, r=r: o["f_p"][1, r:r + 1, c0:c0 + nc_], fh.r)
        if not prompt:
            self.store_state_T(lambda c: fh.t[:, c, :, :], fh.r, 44, 4, 2,
                               lambda c0, nc_: o["f_s"][1].rearrange("s r f -> (s r) f")[:, c0:c0 + nc_])
        self.resid_update(B, seqcols, mp["G2"])
        store_fn()

    def l1_p1_sample(self):
        S = self.S
        d, o = self.din, self.dout
        mp = self.modp[1]
        Bs = self.TB["s"]
        n = 16
        seqcols = [(1 + s, 4 * s, 4) for s in range(4)]
        hT, cat = Bs["hT"], Bs["cat"]
        Bp_ = self.TB["p"]
        sdin = Buf(Bp_["xT"].t[:, 2, 0:480].rearrange("p (c s r) -> p c s r", c=4, s=4), 1, "sdin")
        sdin.r = [Bp_["xT"].r[2]]
        self.load_state_T(d["sd"].rearrange("s r f -> (s r) f"), 120, 4, lambda c: sdin.t[:, c, :, :], [sdin.r] * 4, 4, 30)
        for i in range(4):
            S.op("dve", lambda e, i=i: e.tensor_copy(Bs["dbb"][i].t[:, :, 0:30], sdin.t[:, i, :, :]), reads=sdin.r, writes=Bs["dbb"][i].r)
            S.op("dve", lambda e, i=i: e.tensor_copy(Bs["dst"].t[:, i, :, 0:26], sdin.t[:, i, :, 4:30]), reads=sdin.r, writes=Bs["dst"].r)
        fh = Bs["fhalo"][1]
        self.load_state_T(d["sf"][1].rearrange("s r f -> (s r) f"), 8, 44, lambda c: fh.t[:, c, :, :], [fh.r] * 44, 4, 2)
        self.norm_mod(Bs, seqcols, mp["A1"], mp["mod"], 0)
        if os.environ.get("KDEBUG_STOP") == "s1":
            return
        rhs_h = lambda kc: (hT.t[:, kc, :], hT.r)
        qk = self.qkT_s
        for grp in range(3):
            sl_w, wv = self.load_w(self.W["qk1"][grp])
            for c4 in range(4):
                cc = grp * 4 + c4
                p = self.ps()
                self.mm(p, n, wv, c4 * 128, rhs_h, 8, sl_w.r)
                S.op("act", lambda e, p=p, cc=cc: e.activation(qk.t[:, cc, :], p.t[:, 0:n], AF.Copy, scale=(0.125 if cc < 6 else 1.0)),
                     reads=p.r, writes=qk.r)
        if os.environ.get("KDEBUG_STOP") == "s2":
            return
        kvnb = self.kvnb
        for g in range(3):
            wb = C_PAIRS[g][0]
            sl_w, wv = self.load_w(self.W["kv1"][g])
            for b in range(4):
                p = self.ps()
                for kc in range(8):
                    S.op("pe", lambda e, kc=kc, b=b, p=p, wv=wv: e.matmul(p.t[0:4, 0:512], lhsT=hT.t[:, kc, b * 4:(b + 1) * 4], rhs=wv[:, kc, 0:512],
                                                                        start=(kc == 0), stop=(kc == 7)),
                         reads=hT.r + sl_w.r, writes=p.r, signal=(kc == 7))
                kvn = self.kvn[self.kvni]
                self.kvni = (self.kvni + 1) % 2
                flags = os.environ.get("KDEBUG_S3", "")
                if "a" not in flags:
                    S.op("act", lambda e, p=p, kvn=kvn: e.copy(kvn.t[0:4, :], p.t[0:4, 0:512]), reads=p.r, writes=kvn.r)
                if "d" not in flags:
                    S.op("dve", lambda e, p=p, b=b, g=g: e.tensor_copy(kvnb.t[0:4, b, g, :], p.t[0:4, 256:512]), reads=p.r, writes=kvnb.r)
                cin = d[("c128", "c512", "c2048")[g]]
                for r0 in ([] if os.environ.get("KDEBUG_NOCOPY") else range(0, wb - 4, 256)):
                    r1 = min(wb - 4, r0 + 256)
                    S.dma("pool", out=o["c%d_s" % g][b, r0:r1, :], in_=cin[b, 4 + r0:4 + r1, :], is_output=True)
                if "o" not in flags:
                    S.dma("pool", out=o["c%d_s" % g][b, wb - 4:wb, :], in_=kvn.t[0:4, :], reads=kvn.r, is_output=True)
        if os.environ.get("KDEBUG_STOP") == "s3":
            return
        self.l1_dpath(Bs, seqcols, lambda i: (cat.t[:, 2 + i, :], [cat.r[2 + i]]), want_state=True)
        if os.environ.get("KDEBUG_STOP") == "s4":
            return
        self.store_state_T(lambda c: Bs["dst"].t[:, c, :, :], Bs["dst"].r, 4, 4, 30,
                           lambda c0, nc_: o["d_s"].rearrange("s r f -> (s r) f")[:, c0:c0 + nc_])

    def l1_attn_sample(self):
        S = self.S
        d = self.din
        Bp, Bs = self.TB["p"], self.TB["s"]
        act = Bp["act"]
        cat = Bs["cat"]
        qk = self.qkT_s
        kvnb = self.kvnb

        def view(ap, name):
            return Buf(ap, 1, name)
        y = Bp["y"]
        kcf = [view(y.t[:, 6 + k, :], "kcf%d" % k) for k in range(2)]
        kcb = [view(act.t[:, 8 + k, :], "kcb%d" % k) for k in range(3)]
        kTc = [view(act.t[:, 12 + k, 0:256].rearrange("p (c q) -> p c q", c=2), "kTc%d" % k) for k in range(2)]
        xk = Bp["xtok"]
        STs = view(xk.t[:, 1, 0:48].rearrange("p (g h t) -> p g h t", g=3, h=4), "STs")
        SNs = view(xk.t[:, 1, 64:112].rearrange("p (g h t) -> p g h t", g=3, h=4), "SNs")
        scs = view(xk.t[:, 2, 0:396].rearrange("p (g k) -> p g k", g=3), "scs")
        stt = view(xk.t[:, 1, 128:136], "stts")
        pn = view(act.t[:, 14, 0:396].rearrange("p (g k) -> p g k", g=3), "pn")
        pTs = view(act.t[:, 15, 0:48].rearrange("p (g q) -> p g q", g=3), "pTs")
        pTn = view(act.t[:, 15, 64:112].rearrange("p (g q) -> p g q", g=3), "pTn")
        biasS = view(xk.t[:, 0, 0:396].rearrange("p (g k) -> p g k", g=3), "biasS")
        rbrep = self.rbrep
        self.ohsb = [view(act.t[:, 17 + k, 0:132], "ohsb%d" % k) for k in range(2)]
        for g in range(3):
            lt = self.tmp(Bp)
            for h in range(4):
                gh = g * 4 + h
                S.op("dve", lambda e, lt=lt, h=h, gh=gh: e.tensor_scalar(lt.t[:, h * 4:(h + 1) * 4], self.dmask.t[:, 0:4], rbrep.t[:, gh:gh + 1], None, ALU.mult),
                     reads=self.dmask.r + rbrep.r, writes=lt.r)
            hi, lo = self.bl[0], self.bl[1]
            ohb = self.ohsb[g % 2]
            S.op("dve", lambda e, ohb=ohb, g=g: e.tensor_copy(ohb.t, self.ohs[g].t[:, 0:132]), reads=self.ohs[g].r, writes=ohb.r)
            S.op("dve", lambda e, lt=lt, hi=hi: e.tensor_copy(hi.t[:, 0:16], lt.t[:, 0:16]), reads=lt.r, writes=hi.r)
            S.op("dve", lambda e, lt=lt, hi=hi: e.tensor_tensor(lt.t[:, 0:16], lt.t[:, 0:16], hi.t[:, 0:16], ALU.subtract), reads=lt.r + hi.r, writes=lt.r)
            S.op("dve", lambda e, lt=lt, lo=lo: e.tensor_copy(lo.t[:, 0:16], lt.t[:, 0:16]), reads=lt.r, writes=lo.r)
            p = self.ps()
            S.op("pe", lambda e, hi=hi, p=p, ohb=ohb: e.matmul(p.t[0:16, 0:132], lhsT=hi.t[:, 0:16], rhs=ohb.t, start=True, stop=False),
                 reads=hi.r + ohb.r, writes=p.r)
            S.op("pe", lambda e, lo=lo, p=p, ohb=ohb: e.matmul(p.t[0:16, 0:132], lhsT=lo.t[:, 0:16], rhs=ohb.t, start=False, stop=True),
                 reads=lo.r + ohb.r, writes=p.r)
            S.op("dve", lambda e, p=p, g=g: e.tensor_tensor(biasS.t[0:16, g, :], p.t[0:16, 0:132], self.masks.t[0:16, g, :], ALU.add),
                 reads=p.r + self.masks.r, writes=biasS.r)
        self.kmask = [view(act.t[:, 19 + k, 0:96].rearrange("p (c t) -> p c t", c=6), "kmask%d" % k) for k in range(2)]
        for k in range(2):
            km = self.kmask[k]
            S.op("dve", lambda e, km=km: e.memset(km.t, 0.0), writes=km.r)
            S.op("dve", lambda e, km=km, k=k: e.tensor_copy(km.t[k * 64:(k + 1) * 64], qk.t[k * 64:(k + 1) * 64, 6:12, :]), reads=qk.r, writes=km.r)
        psY = self.pslong
        ci = 0
        SA = os.environ.get("KDEBUG_SA", "")
        if SA == "a":
            return
        for b in range(4):
            pST = self.ps()
            pSN = self.ps()
            vtiles = {}
            for g, (win, dil) in enumerate(C_PAIRS):
                cin = d[("c128", "c512", "c2048")[g]]
                ncls = 1 if g == 0 else 4
                for cls in range(ncls):
                    kf = kcf[ci % 2]; kb = kcb[ci % 3]; kT = kTc[ci % 2]
                    ci += 1
                    if g == 0:
                        src = cin[b, :, :]
                    else:
                        src = cin[b].rearrange("(i c) f -> c i f", c=dil)[cls]
                    S.dma("sp", out=kf.t, in_=src, writes=kf.r)
                    S.op("dve", lambda e, kf=kf, kb=kb: e.tensor_copy(kb.t, kf.t), reads=kf.r, writes=kb.r)
                    pp = self.pbf()
                    for c in range(2):
                        S.op("pe", lambda e, pp=pp, kb=kb, c=c: e.transpose(pp.t[:, c * 128:(c + 1) * 128], kb.t[:, c * 128:(c + 1) * 128], self.identb.t[:, :]),
                             reads=kb.r + self.identb.r, writes=pp.r, signal=(c == 1))
                    S.op("act", lambda e, pp=pp, kT=kT: e.copy(kT.t.rearrange("p c q -> p (c q)"), pp.t[:, 0:256]), reads=pp.r, writes=kT.r)
                    for h in ([] if SA == "b1" else range(4)):
                        c, pb0 = h // 2, (h % 2) * 64
                        if g == 0:
                            outap = pST.t[:, 0:48].rearrange("p (g h t) -> p g h t", g=3, h=4)[:, 0, h, 0:4]
                            rhs = qk.t[pb0:pb0 + 64, c, b * 4:b * 4 + 4]
                        else:
                            outap = pST.t[:, 0:48].rearrange("p (g h t) -> p g h t", g=3, h=4)[:, g, h, cls:cls + 1]
                            rhs = qk.t[pb0:pb0 + 64, g * 2 + c, b * 4 + cls:b * 4 + cls + 1]
                        S.op("pe", lambda e, outap=outap, kT=kT, c=c, pb0=pb0, rhs=rhs: e.matmul(outap, lhsT=kT.t[pb0:pb0 + 64, c, :], rhs=rhs, start=True, stop=True),
                             reads=kT.r + qk.r, writes=pST.r)
                    vtiles[(g, cls)] = kb
                for h in ([] if SA in ("b1", "b2") else range(4)):
                    c, pb0 = h // 2, (h % 2) * 64
                    outap = pSN.t[0:4, 0:48].rearrange("p (g h t) -> p g h t", g=3, h=4)[:, g, h, 0:4]
                    km = self.kmask[h % 2]
                    S.op("pe", lambda e, outap=outap, g=g, c=c, km=km, b=b: e.matmul(
                        outap, lhsT=km.t[:, g * 2 + c, b * 4:b * 4 + 4], rhs=qk.t[:, g * 2 + c, b * 4:b * 4 + 4], start=True, stop=True),
                         reads=qk.r + km.r, writes=pSN.r)
            if SA in ("b", "b1", "b2"):
                continue
            S.op("act", lambda e, pST=pST: e.copy(STs.t.rearrange("p g h t -> p (g h t)"), pST.t[:, 0:48]), reads=pST.r, writes=STs.r)
            S.op("act", lambda e, pSN=pSN: e.copy(SNs.t[0:4].rearrange("p g h t -> p (g h t)"), pSN.t[0:4, 0:48]), reads=pSN.r, writes=SNs.r)
            pq = self.ps()
            for g in range(3):
                S.op("pe", lambda e, pq=pq, g=g: e.transpose(pq.t[0:16, g * 132:g * 132 + 128], STs.t[:, g, :, :].rearrange("p h t -> p (h t)"), self.ident.t[:, :]),
                     reads=STs.r + self.ident.r, writes=pq.r)
                S.op("pe", lambda e, pq=pq, g=g: e.transpose(pq.t[0:16, g * 132 + 128:g * 132 + 132], SNs.t[0:4, g, :, :].rearrange("p h t -> p (h t)"), self.ident.t[0:4, 0:4]),
                     reads=SNs.r + self.ident.r, writes=pq.r)
            S.op("dve", lambda e, pq=pq: e.tensor_tensor(scs.t[0:16].rearrange("p g k -> p (g k)"), pq.t[0:16, 0:396], biasS.t[0:16].rearrange("p g k -> p (g k)"), ALU.add),
                 reads=pq.r + biasS.r, writes=scs.r)
            S.op("dve", lambda e: e.tensor_reduce(stt.t[0:16, 0:1], scs.t[0:16].rearrange("p g k -> p (g k)"), AX.X, ALU.max), reads=scs.r, writes=stt.r)
            S.op("dve", lambda e: e.tensor_scalar(stt.t[0:16, 1:2], stt.t[0:16, 0:1], -1.0, None, ALU.mult), reads=stt.r, writes=stt.r)
            S.op("act", lambda e: e.activation(scs.t[0:16].rearrange("p g k -> p (g k)"), scs.t[0:16].rearrange("p g k -> p (g k)"), AF.Exp,
                                               bias=stt.t[0:16, 1:2], scale=1.0, accum_out=stt.t[0:16, 2:3]), reads=scs.r + stt.r, writes=scs.r + stt.r)
            S.op("dve", lambda e: e.reciprocal(stt.t[0:16, 3:4], stt.t[0:16, 2:3]), reads=stt.r, writes=stt.r)
            S.op("dve", lambda e: e.tensor_scalar(pn.t[0:16].rearrange("p g k -> p (g k)"), scs.t[0:16].rearrange("p g k -> p (g k)"), stt.t[0:16, 3:4], None, ALU.mult),
                 reads=scs.r + stt.r, writes=pn.r)
            if SA == "c":
                continue
            pp = self.pbf()
            for g in range(3):
                S.op("pe", lambda e, pp=pp, g=g: e.transpose(pp.t[:, g * 16:(g + 1) * 16], pn.t[0:16, g, 0:128], self.identb.t[0:16, 0:16]),
                     reads=pn.r + self.identb.r, writes=pp.r)
                S.op("pe", lambda e, pp=pp, g=g: e.transpose(pp.t[0:4, 64 + g * 16:64 + (g + 1) * 16], pn.t[0:16, g, 128:132], self.identb.t[0:16, 0:16]),
                     reads=pn.r + self.identb.r, writes=pp.r)
            S.op("act", lambda e, pp=pp: e.copy(pTs.t.rearrange("p g q -> p (g q)"), pp.t[:, 0:48]), reads=pp.r, writes=pTs.r)
            S.op("act", lambda e, pp=pp: e.copy(pTn.t[0:4].rearrange("p g q -> p (g q)"), pp.t[0:4, 64:112]), reads=pp.r, writes=pTn.r)
            if SA == "d":
                continue
            for h in range(4):
                c, pb0 = h // 2, (h % 2) * 64
                first = True
                for g, (win, dil) in enumerate(C_PAIRS):
                    outap = psY.t[pb0:pb0 + 64, c * 16 + b * 4:c * 16 + b * 4 + 4]
                    S.op("pe", lambda e, outap=outap, g=g, h=h, b=b, first=first: e.matmul(
                        outap, lhsT=kvnb.t[0:4, b, g, h * 64:(h + 1) * 64], rhs=pTn.t[0:4, g, h * 4:(h + 1) * 4],
                        start=first, stop=False, skip_group_check=True),
                         reads=kvnb.r + pTn.r, writes=psY.r)
                    first = False
            for g, (win, dil) in enumerate(C_PAIRS):
                cin = d[("c128", "c512", "c2048")[g]]
                ncls = 1 if g == 0 else 4
                for cls in range(ncls):
                    kf = kcf[ci % 2]; kb = kcb[ci % 3]
                    ci += 1
                    if g == 0:
                        src = cin[b, :, 256:512]
                    else:
                        src = cin[b].rearrange("(i c) f -> c i f", c=dil)[cls][:, 256:512]
                    S.dma("sp", out=kf.t[:, 0:256], in_=src, writes=kf.r)
                    S.op("dve", lambda e, kf=kf, kb=kb: e.tensor_copy(kb.t[:, 0:256], kf.t[:, 0:256]), reads=kf.r, writes=kb.r)
                    for h in range(4):
                        c, pb0 = h // 2, (h % 2) * 64
                        if g == 0:
                            outap = psY.t[pb0:pb0 + 64, c * 16 + b * 4:c * 16 + b * 4 + 4]
                            rhs = pTs.t[:, 0, h * 4:(h + 1) * 4]
                        else:
                            outap = psY.t[pb0:pb0 + 64, c * 16 + b * 4 + cls:c * 16 + b * 4 + cls + 1]
                            rhs = pTs.t[:, g, h * 4 + cls:h * 4 + cls + 1]
                        S.op("pe", lambda e, outap=outap, kb=kb, h=h, rhs=rhs: e.matmul(outap, lhsT=kb.t[:, h * 64:(h + 1) * 64], rhs=rhs,
                                                                                  start=False, stop=False, skip_group_check=True),
                             reads=kb.r + pTs.r, writes=psY.r)
        if SA:
            return
        for c in range(2):
            S.op("act", lambda e, c=c: e.copy(cat.t[:, c, :], psY.t[:, c * 16:(c + 1) * 16]), reads=psY.r, writes=[cat.r[c]])

    def l1_p3_sample(self):
        o = self.dout
        Bs = self.TB["s"]
        seqcols = [(1 + s, 4 * s, 4) for s in range(4)]
        self.l1_tail(Bs, seqcols, True, lambda: self.store_x_tokmajor(Bs, o["ys"]), prompt=False)


def build_program(cfg):
    b = Builder(cfg)
    return b


_CACHE = {}


def get_nc(cfg_key):
    if cfg_key not in _CACHE:
        cfg = Cfg(*cfg_key)
        b = Builder(cfg)
        b.sstage = None
        nc = build_with_stages(b)
        _CACHE[cfg_key] = (nc, b)
    return _CACHE[cfg_key]


def build_with_stages(b):
    b.sstage = None
    b.sstage2 = None
    b.cpstg = [b.sb("cpstg%d" % k, [128, 128], F32) for k in range(2)]
    b.cpi = 0
    return b.build()


def make_oh():
    oh = np.zeros((3, 33, 384), np.float32)
    for g, (win, dil) in enumerate(C_PAIRS):
        taps = win // dil
        buckets = t5_bucket(dil * np.arange(taps + 1))
        for c in range(384):
            j = 128 - c
            if 0 <= j <= taps:
                oh[g, buckets[j], c] = 1.0
            else:
                oh[g, 32, c] = NEG
    return oh


def make_sample_consts():
    ohs = np.zeros((3, 128, 132), np.float32)
    masks = np.zeros((16, 3, 132), np.float32)
    dmask = np.zeros((128, 4), np.float32)
    for t in range(4):
        dmask[t * 32:(t + 1) * 32, t] = 1.0
    for g, (win, dil) in enumerate(C_PAIRS):
        taps = win // dil
        buckets = t5_bucket(dil * np.arange(taps + 1))
        for t in range(4):
            for col in range(132):
                if col < 128:
                    if g == 0:
                        valid, tap = col >= t, 128 + t - col
                    else:
                        valid, tap = True, 128 - col
                else:
                    j = col - 128
                    if g == 0:
                        valid, tap = j <= t, t - j
                    else:
                        valid, tap = j == t, 0
                if valid:
                    ohs[g, t * 32 + buckets[tap], col] = 1.0
                else:
                    for h in range(4):
                        masks[h * 4 + t, g, col] = NEG
    return ohs, dmask, masks


def make_in_maps(inputs, cfg, n_cores=8):
    f = lambda a: np.ascontiguousarray(np.asarray(a, dtype=np.float32))
    I = {k: f(v) for k, v in inputs.items()}
    oh = make_oh()
    ohs, dmask, masks = make_sample_consts()
    maps = []
    for c in range(n_cores):
        b = c // 2
        ss = slice(4 * c, 4 * c + 4)
        m = {
            "xp": I["x_prompt"][b, :cfg.seq], "cp": I["c_prompt"][b:b + 1],
            "xs": I["x_sample"][ss].reshape(16, D), "cs": I["c_sample"][ss],
            "sa": I["state_a_conv"][0, ss], "sb": I["state_b_pool"][0, ss],
            "c128": I["cache_c_win128"][0, ss].reshape(4, 128, 512),
            "c512": I["cache_c_win512"][0, ss].reshape(4, 512, 512),
            "c2048": I["cache_c_win2048"][0, ss].reshape(4, 2048, 512),
            "sd": I["state_d_conv"][0, ss], "sf": I["state_ffn_conv"][:, ss],
            "ada_w": I["ada_w"], "ada_b": I["ada_b"], "norm_g": I["norm_g"], "rel_bias": I["rel_bias"],
            "ab_w_in": I["ab_w_in"][0], "a_conv_w": I["a_conv_w"][0], "b_w_grp": I["b_w_grp"][0],
            "b_scale": I["b_scale"][0], "ab_w_out": I["ab_w_out"][0], "cd_w_in": I["cd_w_in"][0],
            "d_conv_w": I["d_conv_w"][0], "d_conv_b": I["d_conv_b"][0], "d_ln_g": I["d_ln_g"][0],
            "d_ln_b": I["d_ln_b"][0], "cd_w_out": I["cd_w_out"][0], "ffn_w_up": I["ffn_w_up"],
            "ffn_conv_w": I["ffn_conv_w"], "ffn_conv_b": I["ffn_conv_b"], "ffn_w_down": I["ffn_w_down"],
            "oh": oh, "ohs": ohs, "dmask": dmask, "masks": masks,
        }
        maps.append({k: np.ascontiguousarray(v) for k, v in m.items()})
    return maps


def kernel(**inputs):
    cfg_key = (8, True, 2, False)
    nc, b = get_nc(cfg_key)
    cfg = b.cfg
    maps = make_in_maps(inputs, cfg)
    res = run_bass_kernel_spmd(nc, maps, core_ids=list(range(8)))
    R = res.results
    cat = lambda name, cores: np.stack([np.asarray(R[c][name]) for c in cores])
    even = [0, 2, 4, 6]
    allc = list(range(8))
    y_prompt = cat("yp", even)
    y_sample = np.concatenate([np.asarray(R[c]["ys"]).reshape(4, 4, D) for c in allc])
    a_p = cat("a_p", even)[None]
    a_s = np.concatenate([np.asarray(R[c]["a_s"]) for c in allc])[None]
    b_p = cat("b_p", even)[None]
    b_s = np.concatenate([np.asarray(R[c]["b_s"]) for c in allc])[None]
    outs = [y_prompt, y_sample, a_p, a_s, b_p, b_s]
    for g, wb in enumerate((128, 512, 2048)):
        cp = cat("c%d_p" % g, even).reshape(1, 4, wb, 2, 4, 64)
        cs = np.concatenate([np.asarray(R[c]["c%d_s" % g]) for c in allc]).reshape(1, 32, wb, 2, 4, 64)
        outs += [cp, cs]
    d_p = cat("d_p", even)[None]
    d_s = np.concatenate([np.asarray(R[c]["d_s"]) for c in allc])[None]
    f_p = np.stack([np.asarray(R[c]["f_p"]) for c in even], axis=1)
    f_s = np.concatenate([np.asarray(R[c]["f_s"]) for c in allc], axis=1)
    outs += [d_p, d_s, f_p, f_s]
    return tuple(np.ascontiguousarray(x, dtype=np.float32) for x in outs)
```

```python
import math
import os
import types
from contextlib import ExitStack
import numpy as np
import concourse.bass as bass
import concourse.mybir as mybir
from concourse.bass_utils import run_bass_kernel_spmd

F32 = mybir.dt.float32
BF16 = mybir.dt.bfloat16
AF = mybir.ActivationFunctionType
ALU = mybir.AluOpType
AX = mybir.AxisListType

D = 1024
DFF = 2816
SEQ = 4096
NT = 512
EPS = 1e-6
NEG = -30000.0


class Res:
    __slots__ = ("name", "lw", "rd", "psum")

    def __init__(self, name=""):
        self.name = name
        self.psum = False
        self.lw = None
        self.rd = {}


def _freeze(fn):
    if fn.__closure__ is None:
        return fn
    cells = []
    for c in fn.__closure__:
        try:
            cells.append(types.CellType(c.cell_contents))
        except ValueError:
            cells.append(c)
    return types.FunctionType(fn.__code__, fn.__globals__, fn.__name__, fn.__defaults__, tuple(cells))


class Sched:
    ENG = ("pe", "act", "dve", "pool", "sp")

    def __init__(self, nc, es, n_dma_sems=48):
        self.nc = nc
        self.ops = {e: [] for e in self.ENG}
        self.cnt = {e: 0 for e in self.ENG}
        self.sig = {e: 0 for e in self.ENG}
        self.seen = {e: {} for e in self.ENG}
        self.semh = {}
        for e in self.ENG:
            self.semh[e] = es.enter_context(nc.semaphore("sem_" + e))
        self.ndma = n_dma_sems
        for i in range(n_dma_sems):
            self.semh[("d", i)] = es.enter_context(nc.semaphore("semd%d" % i))
        self.dval = [0] * n_dma_sems
        self.dnext = 0
        self.dnext_pool = 0
        self.out_tokens = []

    def _wait(self, e, tok):
        sem, val = tok
        if self.seen[e].get(sem, 0) >= val:
            return
        self.seen[e][sem] = val
        h = self.semh[sem]
        self.ops[e].append(lambda eng, h=h, v=val: eng.wait_ge(h, v))

    def _deps(self, e, reads, writes):
        toks = {}
        for r in reads:
            if r.lw is not None:
                s, v = r.lw
                toks[s] = max(toks.get(s, 0), v)
        for w in writes:
            if w.lw is not None:
                s, v = w.lw
                toks[s] = max(toks.get(s, 0), v)
            for s, v in w.rd.items():
                toks[s] = max(toks.get(s, 0), v)
        for s, v in toks.items():
            if s == e and e == "pe":
                continue
            self._wait(e, (s, v))

    def _commit(self, tok, reads, writes):
        for w in writes:
            w.lw = tok
            w.rd = {}
        for r in reads:
            if r in writes:
                continue
            s, v = tok
            if r.rd.get(s, 0) < v:
                r.rd[s] = v

    def op(self, e, fn, reads=(), writes=(), signal=True):
        fn = _freeze(fn)
        signal = True
        reads, writes = list(reads), list(writes)
        for r in reads:
            if r.psum and r not in writes:
                writes.append(r)
        self._deps(e, reads, writes)
        self.cnt[e] += 1
        c = self.cnt[e]
        if signal:
            inc = c - self.sig[e]
            self.sig[e] = c
            h = self.semh[e]
            self.ops[e].append(lambda eng, fn=fn, h=h, inc=inc: fn(eng).then_inc(h, inc))
        else:
            self.ops[e].append(lambda eng, fn=fn: fn(eng))
        tok = (e, c)
        self._commit(tok, reads, writes)
        return tok

    def dma(self, q, out, in_, reads=(), writes=(), is_output=False, **kw):
        self._deps(q, reads, writes)
        half = self.ndma // 2
        if q == "pool":
            i = self.dnext_pool
            self.dnext_pool = (i + 1) % half
        else:
            i = half + self.dnext
            self.dnext = (self.dnext + 1) % (self.ndma - half)
        key = ("d", i)
        prev = self.dval[i]
        if prev > 0:
            self._wait(q, (key, prev))
        self.dval[i] = prev + 16
        tok = (key, prev + 16)
        h = self.semh[key]
        self.ops[q].append(lambda eng, o=out, a=in_, h=h, kw=kw: eng.dma_start(out=o, in_=a, **kw).then_inc(h, 16))
        self._commit(tok, reads, writes)
        if is_output:
            self.out_tokens.append(tok)
        return tok

    def barrier(self):
        toks = []
        for e in ("pe", "act", "dve", "pool"):
            if self.sig[e] > 0:
                toks.append((e, self.sig[e]))
        for i in range(self.ndma):
            if self.dval[i] > 0:
                toks.append((("d", i), self.dval[i]))
        for e in self.ENG:
            for t in toks:
                if t[0] != e:
                    self._wait(e, t)

    def finish(self):
        for i in range(self.ndma):
            if self.dval[i] > 0:
                self._wait("sp", (("d", i), self.dval[i]))
        for e in ("pe", "act", "dve", "pool"):
            if self.sig[e] > 0:
                self._wait("sp", (e, self.sig[e]))

    def emit(self, block):
        nc = self.nc
        ops = self.ops

        @block.tensor
        def _(eng):
            for f in ops["pe"]:
                f(eng)

        @block.scalar
        def _(eng):
            for f in ops["act"]:
                f(eng)

        @block.vector
        def _(eng):
            for f in ops["dve"]:
                f(eng)

        @block.gpsimd
        def _(eng):
            for f in ops["pool"]:
                f(eng)

        @block.sync
        def _(eng):
            for f in ops["sp"]:
                f(eng)


class Buf:
    def __init__(self, t, nres=1, name=""):
        self.t = t
        self.r = [Res("%s[%d]" % (name, i)) for i in range(nres)]

    @property
    def all(self):
        return self.r


class Cfg:
    def __init__(self, ntiles=8, do_sample=True, depth=2, debug_x1=False):
        self.ntiles = ntiles
        self.seq = ntiles * NT
        self.do_sample = do_sample
        self.depth = depth
        self.debug_x1 = debug_x1


def t5_bucket(dist):
    dist = np.asarray(dist)
    max_exact = 16
    large = max_exact + (np.log(np.maximum(dist, max_exact) / max_exact) / np.log(2048 / max_exact)
                         * (32 - max_exact)).astype(np.int32)
    large = np.minimum(large, 31)
    return np.where(dist < max_exact, dist, large).astype(np.int32)


C_PAIRS = ((128, 1), (512, 4), (2048, 16))


class Builder:
    def __init__(self, cfg):
        self.cfg = cfg
        self.nc = bass.Bass("TRN2", target_bir_lowering=False)
        self.es = ExitStack()
        self.S = None
        self.din = {}
        self.dout = {}
        self.scr = {}

    def inp(self, name, shape, dt=F32):
        t = self.nc.dram_tensor(name, list(shape), dt, kind="ExternalInput").ap()
        self.din[name] = t
        return t

    def outp(self, name, shape, dt=F32):
        t = self.nc.dram_tensor(name, list(shape), dt, kind="ExternalOutput").ap()
        self.dout[name] = t
        return t

    def scratch(self, name, shape, dt):
        t = self.nc.dram_tensor(name, list(shape), dt, kind="Internal").ap()
        self.scr[name] = (t, Res(name))
        return t

    def sb(self, name, shape, dt=F32, nres=1):
        t = self.es.enter_context(self.nc.sbuf_tensor(name, list(shape), dt))
        return Buf(t, nres, name)

    def psb(self, name, shape, dt=F32):
        t = self.es.enter_context(self.nc.psum_tensor(name, list(shape), dt))
        b = Buf(t, 1, name)
        b.r[0].psum = True
        return b

    def ps(self):
        b = self.psum[self.psi]
        self.psi = (self.psi + 1) % len(self.psum)
        return b

    def wslot(self):
        b = self.wsl[self.wsi]
        self.wsi = (self.wsi + 1) % len(self.wsl)
        return b

    def declare_io(self):
        cfg = self.cfg
        i = self.inp
        i("xp", [cfg.seq, D])
        i("cp", [1, D])
        i("xs", [16, D])
        i("cs", [4, D])
        i("sa", [4, 2, 512])
        i("sb", [4, 15, 512])
        i("c128", [4, 128, 512])
        i("c512", [4, 512, 512])
        i("c2048", [4, 2048, 512])
        i("sd", [4, 30, 512])
        i("sf", [2, 4, 2, 2 * DFF])
        i("ada_w", [2, D, 6 * D])
        i("ada_b", [2, 6 * D])
        i("norm_g", [2, 4, D])
        i("rel_bias", [32, 12])
        i("ab_w_in", [D, 2048])
        i("a_conv_w", [3, 512])
        i("b_w_grp", [4, 128, 128])
        i("b_scale", [512])
        i("ab_w_out", [D, D])
        i("cd_w_in", [D, 3328])
        i("d_conv_w", [31, 512])
        i("d_conv_b", [512])
        i("d_ln_g", [512])
        i("d_ln_b", [512])
        i("cd_w_out", [768, D])
        i("ffn_w_up", [2, D, 2 * DFF])
        i("ffn_conv_w", [2, 3, 2 * DFF])
        i("ffn_conv_b", [2, 2 * DFF])
        i("ffn_w_down", [2, DFF, D])
        i("oh", [3, 33, 384])
        i("ohs", [3, 128, 132])
        i("dmask", [128, 4])
        i("masks", [16, 3, 132])
        o = self.outp
        o("yp", [cfg.seq, D])
        o("ys", [16, D])
        o("a_p", [2, 512])
        o("a_s", [4, 2, 512])
        o("b_p", [15, 512])
        o("b_s", [4, 15, 512])
        o("c0_p", [min(128, cfg.seq), 512])
        o("c0_s", [4, 128, 512])
        o("c1_p", [min(512, cfg.seq), 512])
        o("c1_s", [4, 512, 512])
        o("c2_p", [min(2048, cfg.seq), 512])
        o("c2_s", [4, 2048, 512])
        o("d_p", [30, 512])
        o("d_s", [4, 30, 512])
        o("f_p", [2, 2, 2 * DFF])
        o("f_s", [2, 4, 2, 2 * DFF])

    def build(self):
        cfg = self.cfg
        nc = self.nc
        es = self.es
        self.declare_io()
        S = self.S = Sched(nc, es)
        self.ident = self.sb("ident", [128, 128], F32)
        self.identb = self.sb("identb", [128, 128], BF16)
        self.onesb = self.sb("onesb", [128, 128], BF16)
        self.onesf = self.sb("onesf", [128, 128], F32)
        self.psum = [self.psb("ps%d" % k, [128, 512], F32) for k in range(5)]
        self.pslong = self.psb("pslong", [128, 512], F32)
        self.psi = 0
        self.psbf = [self.psb("psbf%d" % k, [128, 1024], BF16) for k in range(2)]
        self.psbi = 0
        self.wsl = [self.sb("wsl%d" % k, [128, 4096], BF16) for k in range(3)]
        self.wsi = 0
        self.vstage = self.sb("vstage", [128, 128], F32)
        ident, identb, onesb, onesf = self.ident, self.identb, self.onesb, self.onesf
        S.op("pool", lambda e: e.iota(ident.t[:], pattern=[[1, 128]], base=0, channel_multiplier=-1,
                                      allow_small_or_imprecise_dtypes=True), writes=ident.r)
        S.op("pool", lambda e: e.tensor_single_scalar(ident.t[:], ident.t[:], 0.0, ALU.is_equal),
             reads=ident.r, writes=ident.r)
        S.op("pool", lambda e: e.tensor_copy(identb.t[:], ident.t[:]), reads=ident.r, writes=identb.r)
        S.op("pool", lambda e: e.memset(onesb.t[:], 1.0), writes=onesb.r)
        S.op("pool", lambda e: e.memset(onesf.t[:], 1.0), writes=onesf.r)
        self.epsb = self.sb("epsb", [128, 2], F32)
        S.op("pool", lambda e: e.memset(self.epsb.t[:, 0:1], float(D * EPS)), writes=self.epsb.r)
        S.op("pool", lambda e: e.memset(self.epsb.t[:, 1:2], float(EPS)), writes=self.epsb.r)

        self.W = {}
        self.load_small()
        self.compute_mod()
        self.prep_weights(0)
        self.alloc_tile_bufs()
        for l in range(cfg.depth):
            if l == 0:
                self.layer0()
            else:
                self.layer1()
        S.finish()
        with nc.Block() as block:
            S.emit(block)
        es.close()
        return nc

    def cast_group(self, name, kc, ncols, blocks):
        t = self.scratch(name, [128, kc, ncols], BF16)
        res = self.scr[name][1]
        for src, c0 in blocks:
            w = src.shape[1]
            self.S.dma("pool", out=t[:, :, c0:c0 + w], in_=src.rearrange("(k p) c -> p k c", p=128),
                       writes=[res])
        return (t, res, kc, ncols)

    def prep_weights(self, which):
        d = self.din
        W = self.W
        cfg = self.cfg
        if which == 0:
            self.prep_l0(d, W)
            self.prep_ffn(d, W, 0)
        elif cfg.depth > 1:
            self.prep_l1(d, W)
            self.prep_ffn(d, W, 1)

    def prep_l0(self, d, W):
        wi = d["ab_w_in"]
        W["in0"] = [self.cast_group("w_in0_%d" % i, 8, 512,
                                    [(wi[:, (b * 4 + i) * 128:(b * 4 + i + 1) * 128], b * 128) for b in range(4)])
                    for i in range(4)]
        wo = d["ab_w_out"]
        W["out0"] = [self.cast_group("w_out0_%d" % g, 8, 512, [(wo[:, g * 512:(g + 1) * 512], 0)]) for g in range(2)]

    def prep_ffn(self, d, W, l):
        if True:
            up = d["ffn_w_up"][l]
            W["up%d" % l] = [self.cast_group("w_up%d_%d" % (l, j), 8, 512,
                                             [(up[:, (2 * j) * 128:(2 * j + 1) * 128], 0),
                                              (up[:, DFF + (2 * j) * 128:DFF + (2 * j + 1) * 128], 128),
                                              (up[:, (2 * j + 1) * 128:(2 * j + 2) * 128], 256),
                                              (up[:, DFF + (2 * j + 1) * 128:DFF + (2 * j + 2) * 128], 384)])
                             for j in range(11)]
            dn = d["ffn_w_down"][l]
            W["down%d" % l] = [self.cast_group("w_dn%d_%d" % (l, g), 22, 128, [(dn[:, g * 128:(g + 1) * 128], 0)])
                               for g in range(8)]

    def prep_l1(self, d, W):
        if True:
            ci = d["cd_w_in"]
            W["qk1"] = [self.cast_group("w_qk1_%d" % g, 8, 512, [(ci[:, g * 512:(g + 1) * 512], 0)]) for g in range(3)]
            W["kv1"] = [self.cast_group("w_kv1_%d" % g, 8, 512,
                                        [(ci[:, 768 + g * 256:768 + (g + 1) * 256], 0),
                                         (ci[:, 1536 + g * 256:1536 + (g + 1) * 256], 256)]) for g in range(3)]
            W["d1"] = [self.cast_group("w_d1_%d" % g, 8, 512,
                                       [(ci[:, 2304 + (2 * g) * 128:2304 + (2 * g + 1) * 128], 0),
                                        (ci[:, 2816 + (2 * g) * 128:2816 + (2 * g + 1) * 128], 128),
                                        (ci[:, 2304 + (2 * g + 1) * 128:2304 + (2 * g + 2) * 128], 256),
                                        (ci[:, 2816 + (2 * g + 1) * 128:2816 + (2 * g + 2) * 128], 384)])
                       for g in range(2)]
            co = d["cd_w_out"]
            W["out1"] = [self.cast_group("w_out1_%d" % g, 6, 512, [(co[:, g * 512:(g + 1) * 512], 0)]) for g in range(2)]

    def load_w(self, grp):
        t, res, kc, ncols = grp
        sl = self.wslot()
        self.S.dma("sp", out=sl.t[:, 0:kc * ncols], in_=t.rearrange("p k c -> p (k c)"), reads=[res], writes=sl.r)
        return sl, sl.t[:, 0:kc * ncols].rearrange("p (k c) -> p k c", k=kc)

    def load_vec_T(self, src_rows, dst_ap, dst_res, R):
        S = self.S
        vs = self.vstage
        S.dma("sp", out=vs.t[0:R, :], in_=src_rows, writes=vs.r)
        p = self.ps()
        S.op("pe", lambda e: e.transpose(p.t[:, 0:R], vs.t[0:R, :], self.ident.t[0:R, 0:R]),
             reads=vs.r + self.ident.r, writes=p.r)
        S.op("dve", lambda e: e.tensor_copy(dst_ap, p.t[:, 0:R]), reads=p.r, writes=dst_res)

    def load_small(self):
        d = self.din
        cfg = self.cfg
        V = self.V = {}

        def vec(name, src, R):
            b = self.sb("v_" + name, [128, R], F32)
            for r0 in range(0, R, 128):
                rr = min(128, R - r0)
                self.load_vec_T(src[r0:r0 + rr, :], b.t[:, r0:r0 + rr], b.r, rr)
            V[name] = b
            return b

        for l in range(cfg.depth):
            vec("ada_b%d" % l, d["ada_b"][l].rearrange("(r p) -> r p", p=128), 48)
            vec("ng%d" % l, d["norm_g"][l].rearrange("k (r p) -> (k r) p", p=128), 32)
            vec("fcw%d" % l, d["ffn_conv_w"][l].rearrange("k (r p) -> (k r) p", p=128), 132)
            vec("fcb%d" % l, d["ffn_conv_b"][l].rearrange("(r p) -> r p", p=128), 44)
        vec("acw", d["a_conv_w"].rearrange("k (r p) -> (k r) p", p=128), 12)
        vec("bsc", d["b_scale"].rearrange("(r p) -> r p", p=128), 4)
        if cfg.depth > 1:
            vec("dcw", d["d_conv_w"].rearrange("k (r p) -> (k r) p", p=128), 124)
            vec("dcb", d["d_conv_b"].rearrange("(r p) -> r p", p=128), 4)
            vec("dlg", d["d_ln_g"].rearrange("(r p) -> r p", p=128), 4)
            vec("dlb", d["d_ln_b"].rearrange("(r p) -> r p", p=128), 4)
        self.bw = self.sb("bw", [128, 4, 128], BF16)
        self.S.dma("pool", out=self.bw.t[:], in_=d["b_w_grp"].rearrange("g c d -> c g d"), writes=self.bw.r)

    def compute_mod(self):
        S = self.S
        d = self.din
        cfg = self.cfg
        NS = 5
        crow = self.sb("crow", [8, D], F32)
        S.dma("sp", out=crow.t[0:1, :], in_=d["cp"], writes=crow.r)
        S.dma("sp", out=crow.t[1:5, :], in_=d["cs"], writes=crow.r)
        cT = self.sb("cT", [128, 8, NS], BF16)
        p = self.ps()
        for kc in range(8):
            S.op("pe", lambda e, kc=kc: e.transpose(p.t[:, kc * NS:(kc + 1) * NS], crow.t[0:NS, kc * 128:(kc + 1) * 128],
                                                    self.ident.t[0:NS, 0:NS]),
                 reads=crow.r + self.ident.r, writes=p.r)
        S.op("act", lambda e: e.activation(cT.t[:].rearrange("p k s -> p (k s)"), p.t[:, 0:8 * NS], AF.Silu),
             reads=p.r, writes=cT.r)
        self.modp = []
        for l in range(cfg.depth):
            mod = self.sb("mod%d" % l, [128, 48, NS], F32)
            pm = self.pslong
            aw = d["ada_w"][l]
            for g in range(12):
                sl = self.wslot()
                S.dma("pool", out=sl.t[:, 0:4096].rearrange("p (k c) -> p k c", k=8),
                      in_=aw[:, g * 512:(g + 1) * 512].rearrange("(k p) c -> p k c", p=128), writes=sl.r)
                wv = sl.t[:, 0:4096].rearrange("p (k c) -> p k c", k=8)
                for c4 in range(4):
                    oc = g * 4 + c4
                    for kc in range(8):
                        S.op("pe", lambda e, oc=oc, kc=kc, c4=c4, wv=wv: e.matmul(
                            pm.t[:, oc * NS:(oc + 1) * NS], lhsT=wv[:, kc, c4 * 128:(c4 + 1) * 128], rhs=cT.t[:, kc, :],
                            start=(kc == 0), stop=(kc == 7)),
                             reads=sl.r + cT.r, writes=pm.r, signal=(kc == 7))
            ab = self.V["ada_b%d" % l]
            for s in range(NS):
                S.op("dve", lambda e, s=s: e.tensor_tensor(mod.t[:, :, s], pm.t[:, 0:48 * NS].rearrange("p (c s) -> p c s", s=NS)[:, :, s],
                                                          ab.t[:, :], ALU.add),
                     reads=pm.r + ab.r, writes=mod.r)
            ng = self.V["ng%d" % l]
            mp = {}
            for nm in ("A1", "G1", "A2", "G2"):
                mp[nm] = self.sb("mp%d%s" % (l, nm), [128, 8, NS], F32)
            for s in range(NS):
                S.op("dve", lambda e, s=s: e.scalar_tensor_tensor(mp["A1"].t[:, :, s], mod.t[:, 8:16, s], 1.0, ng.t[:, 0:8], ALU.add, ALU.mult),
                     reads=mod.r + ng.r, writes=mp["A1"].r)
                S.op("dve", lambda e, s=s: e.scalar_tensor_tensor(mp["A2"].t[:, :, s], mod.t[:, 32:40, s], 1.0, ng.t[:, 16:24], ALU.add, ALU.mult),
                     reads=mod.r + ng.r, writes=mp["A2"].r)
                S.op("dve", lambda e, s=s: e.tensor_tensor(mp["G1"].t[:, :, s], mod.t[:, 16:24, s], ng.t[:, 8:16], ALU.mult),
                     reads=mod.r + ng.r, writes=mp["G1"].r)
                S.op("dve", lambda e, s=s: e.tensor_tensor(mp["G2"].t[:, :, s], mod.t[:, 40:48, s], ng.t[:, 24:32], ALU.mult),
                     reads=mod.r + ng.r, writes=mp["G2"].r)
            for nm in ("A1", "G1", "A2", "G2"):
                S.op("dve", lambda e, nm=nm: e.tensor_scalar(mp[nm].t[:], mp[nm].t[:], 32.0, None, ALU.mult),
                     reads=mp[nm].r, writes=mp[nm].r)
            mp["mod"] = mod
            self.modp.append(mp)

    def alloc_tile_bufs(self):
        self.TB = {}
        for kind, nseg, sl in (("p", 1, NT), ("s", 4, 4)):
            if kind == "s" and not self.cfg.do_sample:
                continue
            n = nseg * sl
            B = {}
            B["nseg"], B["sl"], B["n"] = nseg, sl, n
            if kind == "p":
                B["xtok"] = self.sb(kind + "xtok", [128, max(1, n // 128), D], F32)
            else:
                B["xtok"] = Buf(self.TB["p"]["xtok"].t[:, 3:4, :], 1, "sxtok")
                B["xtok"].r = self.TB["p"]["xtok"].r
            B["xT"] = self.sb(kind + "xT", [128, 8, n], F32, nres=8)
            B["hT"] = self.sb(kind + "hT", [128, 8, n], BF16)
            B["rstd"] = self.sb(kind + "rstd", [128, n], F32)
            B["tmp"] = [self.sb(kind + "tmp%d" % k, [128, n], F32) for k in range(4)]
            B["tmpi"] = 0
            B["y"] = self.sb(kind + "y", [128, 8, n], F32, nres=8)
            B["cat"] = self.sb(kind + "cat", [128, 8, n], BF16, nres=8)
            B["act"] = self.sb(kind + "act", [128, 22, n], BF16, nres=22)
            B["sq"] = Buf(B["act"].t, 1, kind + "sq")
            B["sq"].r = B["act"].r[0:8]
            B["pbuf"] = [self.sb(kind + "pbuf%d" % i, [128, nseg, 2 + sl], F32) for i in range(4)]
            B["ud"] = [self.sb(kind + "ud%d" % i, [128, nseg, 15 + sl], F32) for i in range(4)]
            B["ubuf"] = [Buf(u.t[:, :, 0:15 + sl], 1, "ub") for u in B["ud"]]
            for u, v in zip(B["ubuf"], B["ud"]):
                u.r = v.r
            B["sbuf"] = [self.sb(kind + "sbuf%d" % i, [128, nseg, 15 + sl], F32) for i in range(2)]
            B["pool"] = self.sb(kind + "pool", [128, 4, n], BF16, nres=4)
            B["fbuf"] = [self.sb(kind + "fbuf%d" % i, [128, nseg, 2 + sl], F32) for i in range(4)]
            B["fhalo"] = [self.sb(kind + "fhalo%d" % l, [128, 44, nseg, 2], F32) for l in range(self.cfg.depth)]
            B["ostg"] = B["xtok"]
            self.TB[kind] = B
        xk = self.TB["p"]["xtok"]
        Bp_ = self.TB["p"]
        Bp_["ffn_xt"] = [Buf(xk.t[:, k // 2, (k % 2) * 512:(k % 2) * 512 + 512], 1, "xtmp%d" % k) for k in range(4)]
        flat = xk.t[:, 2:4, :].rearrange("p a b -> p (a b)")
        Bp_["ffn_xf"] = [Buf(flat[:, k * 514:(k + 1) * 514].rearrange("p (s c) -> p s c", s=1), 1, "xfb%d" % k) for k in range(3)]
        self.sstage = Buf(xk.t[:, 0, :], 1, "sstage"); self.sstage.r = xk.r
        self.sstage2 = Buf(xk.t[:, 1, 0:512], 1, "sstage2"); self.sstage2.r = xk.r
        self.rowstg = Buf(xk.t[:, 2, 0:512], 1, "rowstg"); self.rowstg.r = xk.r
        self.rcnt = self.sb("rcnt", [128, 4, 16], F32)
        S = self.S
        S.op("pool", lambda e: e.iota(self.rcnt.t[:, 0, :], pattern=[[1, 16]], base=1, channel_multiplier=0,
                                      allow_small_or_imprecise_dtypes=True), writes=self.rcnt.r)
        for g in range(1, 4):
            S.op("pool", lambda e, g=g: e.tensor_copy(self.rcnt.t[:, g, :], self.rcnt.t[:, 0, :]),
                 reads=self.rcnt.r, writes=self.rcnt.r)
        for g in range(4):
            S.op("pool", lambda e, g=g: e.tensor_scalar(self.rcnt.t[:, g, :], self.rcnt.t[:, g, :], float(2 ** (g + 1)), None, ALU.min),
                 reads=self.rcnt.r, writes=self.rcnt.r)
        S.op("dve", lambda e: e.reciprocal(self.rcnt.t[:], self.rcnt.t[:]), reads=self.rcnt.r, writes=self.rcnt.r)

    def tmp(self, B):
        t = B["tmp"][B["tmpi"]]
        B["tmpi"] = (B["tmpi"] + 1) % len(B["tmp"])
        return t

    def load_x_tokmajor(self, B, src_rows):
        S = self.S
        n = B["n"]
        xt = B["xtok"]
        xT = B["xT"]
        if n >= 128:
            S.dma("sp", out=xt.t[:, :, :], in_=src_rows.rearrange("(s p) f -> p s f", p=128), writes=xt.r)
            ns, rows = n // 128, 128
        else:
            S.dma("sp", out=xt.t[0:n, 0, :], in_=src_rows, writes=xt.r)
            ns, rows = 1, n
        for fc in range(8):
            p = self.ps()
            for s in range(ns):
                S.op("pe", lambda e, fc=fc, s=s, p=p: e.transpose(p.t[:, s * rows:(s + 1) * rows], xt.t[0:rows, s, fc * 128:(fc + 1) * 128],
                                                                  self.ident.t[0:rows, 0:rows]),
                     reads=xt.r + self.ident.r, writes=p.r, signal=(s == ns - 1))
            S.op("act", lambda e, fc=fc, p=p: e.copy(xT.t[:, fc, :], p.t[:, 0:n]), reads=p.r, writes=[xT.r[fc]])

    def store_x_tokmajor(self, B, dst_rows):
        S = self.S
        n = B["n"]
        xT = B["xT"]
        og = B["ostg"]
        ns, rows = (n // 128, 128) if n >= 128 else (1, n)
        for s in range(ns):
            for half in range(2):
                p = self.ps()
                for c4 in range(4):
                    fc = half * 4 + c4
                    S.op("pe", lambda e, fc=fc, s=s, c4=c4, p=p: e.transpose(p.t[0:rows, c4 * 128:(c4 + 1) * 128],
                                                                             xT.t[:, fc, s * rows:(s + 1) * rows], self.ident.t[:, :]),
                         reads=[xT.r[fc]] + self.ident.r, writes=p.r, signal=(c4 == 3))
                S.op("act", lambda e, s=s, half=half, p=p: e.copy(og.t[0:rows, s, half * 512:(half + 1) * 512], p.t[0:rows, :]),
                     reads=p.r, writes=og.r)
        if n >= 128:
            S.dma("act", out=dst_rows.rearrange("(s p) f -> p s f", p=128), in_=og.t[:, :, :], reads=og.r, is_output=True)
        else:
            S.dma("act", out=dst_rows, in_=og.t[0:n, 0, :], reads=og.r, is_output=True)

    def rms_stats(self, B, src, src_res):
        S = self.S
        n = B["n"]
        sq = B["sq"]
        for kc in range(8):
            S.op("act", lambda e, kc=kc: e.activation(sq.t[:, kc, :], src.t[:, kc, :], AF.Square),
                 reads=[src_res[kc]] if len(src_res) == 8 else src_res, writes=sq.r)
        p = self.ps()
        for kc in range(8):
            S.op("pe", lambda e, kc=kc: e.matmul(p.t[:, 0:n], lhsT=self.onesb.t[:, :], rhs=sq.t[:, kc, :], start=(kc == 0), stop=(kc == 7)),
                 reads=sq.r + self.onesb.r, writes=p.r, signal=(kc == 7))
        rstd = B["rstd"]
        S.op("act", lambda e: e.activation(rstd.t[:, :], p.t[:, 0:n], AF.Sqrt, bias=self.epsb.t[:, 0:1], scale=1.0),
             reads=p.r + self.epsb.r, writes=rstd.r)
        S.op("dve", lambda e: e.reciprocal(rstd.t[:, :], rstd.t[:, :]), reads=rstd.r, writes=rstd.r)
        return rstd

    def norm_mod(self, B, seqcols, A, Bsh_mod, bchunk0):
        S = self.S
        xT, hT = B["xT"], B["hT"]
        rstd = self.rms_stats(B, xT, xT.r)
        for kc in range(8):
            for (sc, c0, w) in seqcols:
                t = self.tmp(B)
                S.op("dve", lambda e, kc=kc, sc=sc, c0=c0, w=w, t=t: e.scalar_tensor_tensor(
                    t.t[:, c0:c0 + w], xT.t[:, kc, c0:c0 + w], A.t[:, kc, sc:sc + 1], rstd.t[:, c0:c0 + w], ALU.mult, ALU.mult),
                     reads=[xT.r[kc]] + A.r + rstd.r, writes=t.r)
                S.op("act", lambda e, kc=kc, sc=sc, c0=c0, w=w, t=t: e.activation(
                    hT.t[:, kc, c0:c0 + w], t.t[:, c0:c0 + w], AF.Identity, bias=Bsh_mod.t[:, bchunk0 + kc, sc:sc + 1], scale=1.0),
                     reads=t.r + Bsh_mod.r, writes=hT.r)

    def resid_update(self, B, seqcols, G):
        S = self.S
        xT, y = B["xT"], B["y"]
        rstd = self.rms_stats(B, y, y.r)
        for kc in range(8):
            for (sc, c0, w) in seqcols:
                t = self.tmp(B)
                S.op("dve", lambda e, kc=kc, sc=sc, c0=c0, w=w, t=t: e.scalar_tensor_tensor(
                    t.t[:, c0:c0 + w], y.t[:, kc, c0:c0 + w], G.t[:, kc, sc:sc + 1], rstd.t[:, c0:c0 + w], ALU.mult, ALU.mult),
                     reads=[y.r[kc]] + G.r + rstd.r, writes=t.r)
                S.op("pool", lambda e, kc=kc, c0=c0, w=w, t=t: e.tensor_tensor(
                    xT.t[:, kc, c0:c0 + w], xT.t[:, kc, c0:c0 + w], t.t[:, c0:c0 + w], ALU.add),
                     reads=t.r + [xT.r[kc]], writes=[xT.r[kc]])

    def mm(self, p, n, wv, col0, rhs_fn, nk, reads):
        S = self.S
        for kc in range(nk):
            rhs, rres = rhs_fn(kc)
            S.op("pe", lambda e, kc=kc, rhs=rhs: e.matmul(p.t[:, 0:n], lhsT=wv[:, kc, col0:col0 + 128], rhs=rhs,
                                                          start=(kc == 0), stop=(kc == nk - 1)),
                 reads=reads + rres, writes=p.r, signal=(kc == nk - 1))

    def ffn(self, B, l, seqcols, first, state_src=None):
        S = self.S
        n, nseg, sl = B["n"], B["nseg"], B["sl"]
        hT, act, y = B["hT"], B["act"], B["y"]
        fh = B["fhalo"][l]
        fcw, fcb = self.V["fcw%d" % l], self.V["fcb%d" % l]
        rhs_h = lambda kc: (hT.t[:, kc, :], hT.r)

        def v3(ap):
            return ap.rearrange("p (s t) -> p s t", s=nseg)

        extras = B.get("ffn_xt", []) + B.get("ffn_xf", [])
        owner = B["xtok"].r[0]
        for xb in extras:
            xb.r[0].lw = owner.lw
            xb.r[0].rd = dict(owner.rd)
        tmps = B["tmp"] + B.get("ffn_xt", [])
        fbufs = B["fbuf"] + B.get("ffn_xf", [])
        rot = {"t": 0, "f": 0}

        def next_tmp():
            t_ = tmps[rot["t"] % len(tmps)]
            rot["t"] += 1
            return t_

        def next_fb():
            f_ = fbufs[rot["f"] % len(fbufs)]
            rot["f"] += 1
            return f_

        def back(j, ta, tg):
            S.op("act", lambda e, tg=tg: e.activation(tg.t[:, 0:n], tg.t[:, 0:n], AF.Silu), reads=tg.r, writes=tg.r)
            S.op("dve", lambda e, ta=ta, tg=tg, j=j: e.tensor_tensor(act.t[:, j, :], ta.t[:, 0:n], tg.t[:, 0:n], ALU.mult),
                 reads=ta.r + tg.r, writes=[act.r[j]])

        prev = None
        for jj in range(11):
            sl_w, wv = self.load_w(self.W["up%d" % l][jj])
            for j2 in range(2):
                j = jj * 2 + j2
                res = []
                for part in range(2):
                    ch = part * 22 + j
                    p = self.ps()
                    self.mm(p, n, wv, j2 * 256 + part * 128, rhs_h, 8, sl_w.r)
                    fb = next_fb()
                    S.op("act", lambda e, p=p, fb=fb: e.copy(fb.t[:, :, 2:2 + sl], v3(p.t[:, 0:n])), reads=p.r, writes=fb.r)
                    S.op("pool", lambda e, fb=fb, ch=ch: e.tensor_copy(fb.t[:, :, 0:2], fh.t[:, ch, :, :]), reads=fh.r, writes=fb.r)
                    t = next_tmp()
                    S.op("act", lambda e, fb=fb, t=t, ch=ch: e.activation(v3(t.t[:, 0:n]), fb.t[:, :, 0:sl], AF.Identity,
                                                                         bias=fcb.t[:, ch:ch + 1], scale=fcw.t[:, ch:ch + 1]),
                         reads=fb.r + fcw.r + fcb.r, writes=t.r)
                    S.op("dve", lambda e, fb=fb, t=t, ch=ch: e.scalar_tensor_tensor(v3(t.t[:, 0:n]), fb.t[:, :, 1:1 + sl], fcw.t[:, 44 + ch:44 + ch + 1],
                                                                                   v3(t.t[:, 0:n]), ALU.mult, ALU.add),
                         reads=fb.r + fcw.r + t.r, writes=t.r)
                    S.op("dve", lambda e, fb=fb, t=t, ch=ch: e.scalar_tensor_tensor(v3(t.t[:, 0:n]), fb.t[:, :, 2:2 + sl], fcw.t[:, 88 + ch:88 + ch + 1],
                                                                                   v3(t.t[:, 0:n]), ALU.mult, ALU.add),
                         reads=fb.r + fcw.r + t.r, writes=t.r)
                    S.op("pool", lambda e, fb=fb, ch=ch: e.tensor_copy(fh.t[:, ch, :, :], fb.t[:, :, sl:sl + 2]), reads=fb.r, writes=fh.r)
                    res.append(t)
                if prev is not None:
                    back(*prev)
                prev = (j, res[0], res[1])
        back(*prev)
        for xb in extras:
            r_ = xb.r[0]
            toks_ = dict(r_.rd)
            if r_.lw is not None:
                toks_[r_.lw[0]] = max(toks_.get(r_.lw[0], 0), r_.lw[1])
            for k_, v_ in toks_.items():
                if owner.rd.get(k_, 0) < v_:
                    owner.rd[k_] = v_
        for oc in range(8):
            sl_w, wv = self.load_w(self.W["down%d" % l][oc])
            p = self.ps()
            self.mm(p, n, wv, 0, lambda kc: (act.t[:, kc, :], [act.r[kc]]), 22, sl_w.r)
            S.op("act", lambda e, oc=oc, p=p: e.copy(y.t[:, oc, :], p.t[:, 0:n]), reads=p.r, writes=[y.r[oc]])

    def mixer_ab(self, B, seqcols, first):
        S = self.S
        n, nseg, sl = B["n"], B["nseg"], B["sl"]
        hT, cat, y = B["hT"], B["cat"], B["y"]
        acw, bsc = self.V["acw"], self.V["bsc"]
        rhs_h = lambda kc: (hT.t[:, kc, :], hT.r)

        def v3(ap):
            return ap.rearrange("p (s t) -> p s t", s=nseg)

        for i in range(4):
            sl_w, wv = self.load_w(self.W["in0"][i])
            pA = self.ps(); self.mm(pA, n, wv, 0, rhs_h, 8, sl_w.r)
            pC = self.ps(); self.mm(pC, n, wv, 256, rhs_h, 8, sl_w.r)
            pB = self.ps(); self.mm(pB, n, wv, 128, rhs_h, 8, sl_w.r)
            pU = self.ps(); self.mm(pU, n, wv, 384, rhs_h, 8, sl_w.r)
            t1 = self.tmp(B)
            S.op("act", lambda e, pA=pA, t1=t1: e.copy(t1.t[:, 0:n], pA.t[:, 0:n]), reads=pA.r, writes=t1.r)
            pb = B["pbuf"][i]
            S.op("dve", lambda e, pC=pC, t1=t1, pb=pb: e.tensor_tensor(pb.t[:, :, 2:2 + sl], v3(pC.t[:, 0:n]), v3(t1.t[:, 0:n]), ALU.mult),
                 reads=pC.r + t1.r, writes=pb.r)
            z = self.tmp(B)
            S.op("pool", lambda e, pb=pb, z=z, i=i: e.tensor_scalar(v3(z.t[:, 0:n]), pb.t[:, :, 0:sl], acw.t[:, i:i + 1], None, ALU.mult),
                 reads=pb.r + acw.r, writes=z.r)
            S.op("dve", lambda e, pb=pb, z=z, i=i: e.scalar_tensor_tensor(v3(z.t[:, 0:n]), pb.t[:, :, 1:1 + sl], acw.t[:, 4 + i:5 + i], v3(z.t[:, 0:n]), ALU.mult, ALU.add),
                 reads=pb.r + acw.r + z.r, writes=z.r)
            S.op("dve", lambda e, pb=pb, z=z, i=i: e.scalar_tensor_tensor(v3(z.t[:, 0:n]), pb.t[:, :, 2:2 + sl], acw.t[:, 8 + i:9 + i], v3(z.t[:, 0:n]), ALU.mult, ALU.add),
                 reads=pb.r + acw.r + z.r, writes=z.r)
            S.op("dve", lambda e, pB=pB, z=z, i=i: e.tensor_tensor(cat.t[:, i, :], pB.t[:, 0:n], z.t[:, 0:n], ALU.mult),
                 reads=pB.r + z.r, writes=[cat.r[i]])
            ub = B["ubuf"][i]
            S.op("act", lambda e, pU=pU, ub=ub: e.copy(ub.t[:, :, 15:15 + sl], v3(pU.t[:, 0:n])), reads=pU.r, writes=ub.r)
            cur = ub
            L = 15 + sl
            sh = 1
            for lev in range(i + 1):
                nxt = B["sbuf"][lev % 2]
                lo = 2 * sh - 1
                eng = "dve" if lev % 2 == 0 else "pool"
                S.op(eng, lambda e, cur=cur, nxt=nxt, lo=lo, sh=sh: e.tensor_tensor(nxt.t[:, :, lo:L], cur.t[:, :, lo:L], cur.t[:, :, lo - sh:L - sh], ALU.add),
                     reads=cur.r, writes=nxt.r)
                cur = nxt
                sh *= 2
            win = 2 ** (i + 1)
            pl = B["pool"]
            S.op("dve", lambda e, cur=cur, ub=ub, i=i, win=win: e.scalar_tensor_tensor(
                v3(pl.t[:, i, :]), cur.t[:, :, 15:15 + sl], 1.0 / win, ub.t[:, :, 15:15 + sl], ALU.mult, ALU.subtract),
                 reads=cur.r + ub.r, writes=[pl.r[i]])
            if first:
                S.op("dve", lambda e, cur=cur, i=i: e.tensor_tensor(cur.t[:, 0, 15:30], cur.t[:, 0, 15:30], self.rcnt.t[:, i, 0:15], ALU.mult),
                     reads=cur.r + self.rcnt.r, writes=cur.r)
                S.op("dve", lambda e, cur=cur, ub=ub, i=i: e.tensor_tensor(pl.t[:, i, 0:15], cur.t[:, 0, 15:30], ub.t[:, 0, 15:30], ALU.subtract),
                     reads=cur.r + ub.r, writes=[pl.r[i]])
            pG = self.ps()
            S.op("pe", lambda e, pG=pG, i=i: e.matmul(pG.t[:, 0:n], lhsT=self.bw.t[:, i, :], rhs=pl.t[:, i, :], start=True, stop=True),
                 reads=self.bw.r + [pl.r[i]], writes=pG.r)
            S.op("act", lambda e, pG=pG, i=i: e.activation(cat.t[:, 4 + i, :], pG.t[:, 0:n], AF.Copy, scale=bsc.t[:, i:i + 1]),
                 reads=pG.r + bsc.r, writes=[cat.r[4 + i]])
        for g in range(2):
            sl_w, wv = self.load_w(self.W["out0"][g])
            for c4 in range(4):
                oc = g * 4 + c4
                p = self.ps()
                self.mm(p, n, wv, c4 * 128, lambda kc: (cat.t[:, kc, :], [cat.r[kc]]), 8, sl_w.r)
                S.op("act", lambda e, oc=oc, p=p: e.copy(y.t[:, oc, :], p.t[:, 0:n]), reads=p.r, writes=[y.r[oc]])

    def halo_shift_ab(self, B):
        S = self.S
        sl = B["sl"]
        for i in range(4):
            pb, ub = B["pbuf"][i], B["ubuf"][i]
            S.op("pool", lambda e, pb=pb: e.tensor_copy(pb.t[:, :, 0:2], pb.t[:, :, sl:sl + 2]), reads=pb.r, writes=pb.r)
            S.op("pool", lambda e, ub=ub: e.tensor_copy(ub.t[:, :, 0:15], ub.t[:, :, sl:sl + 15]), reads=ub.r, writes=ub.r)

    def store_rows_T(self, src_fn, nchunks, R, dst_rows_fn, src_res):
        S = self.S
        for c0 in range(0, nchunks, 4):
            nn = min(4, nchunks - c0)
            p = self.ps()
            for c in range(nn):
                S.op("pe", lambda e, c=c, c0=c0, p=p: e.transpose(p.t[0:R, c * 128:(c + 1) * 128], src_fn(c0 + c), self.ident.t[:, :]),
                     reads=src_res + self.ident.r, writes=p.r, signal=(c == nn - 1))
            og = self.rowstg
            S.op("act", lambda e, p=p, nn=nn: e.copy(og.t[0:R, 0:nn * 128], p.t[0:R, 0:nn * 128]), reads=p.r, writes=og.r)
            S.dma("pool", out=dst_rows_fn(c0 * 128, nn * 128), in_=og.t[0:R, 0:nn * 128], reads=og.r, is_output=True)

    def layer0(self):
        S = self.S
        cfg = self.cfg
        d, o = self.din, self.dout
        mp = self.modp[0]
        Bp = self.TB["p"]
        for i in range(4):
            S.op("pool", lambda e, i=i: e.memset(Bp["pbuf"][i].t[:, :, 0:2], 0.0), writes=Bp["pbuf"][i].r)
            S.op("pool", lambda e, i=i: e.memset(Bp["ubuf"][i].t[:, :, 0:15], 0.0), writes=Bp["ubuf"][i].r)
        for i in range(2):
            S.op("pool", lambda e, i=i: e.memset(Bp["sbuf"][i].t[:, :, :], 0.0), writes=Bp["sbuf"][i].r)
        S.op("pool", lambda e: e.memset(Bp["fhalo"][0].t[:], 0.0), writes=Bp["fhalo"][0].r)
        if cfg.depth > 1:
            self.x1 = self.scratch("x1", [D, cfg.seq], F32)
        for ti in range(cfg.ntiles):
            first, last = ti == 0, ti == cfg.ntiles - 1
            t0 = ti * NT
            seqcols = [(0, 0, NT)]
            self.load_x_tokmajor(Bp, d["xp"][t0:t0 + NT, :])
            self.norm_mod(Bp, seqcols, mp["A1"], mp["mod"], 0)
            self.mixer_ab(Bp, seqcols, first)
            if last:
                self.store_rows_T(lambda c: Bp["pbuf"][c].t[:, 0, NT:NT + 2], 4, 2, lambda c0, nc_: o["a_p"][:, c0:c0 + nc_],
                                  sum([Bp["pbuf"][c].r for c in range(4)], []))
                self.store_rows_T(lambda c: Bp["ubuf"][c].t[:, 0, NT:NT + 15], 4, 15, lambda c0, nc_: o["b_p"][:, c0:c0 + nc_],
                                  sum([Bp["ubuf"][c].r for c in range(4)], []))
            else:
                self.halo_shift_ab(Bp)
            self.resid_update(Bp, seqcols, mp["G1"])
            self.norm_mod(Bp, seqcols, mp["A2"], mp["mod"], 24)
            self.ffn(Bp, 0, seqcols, first)
            if last:
                fh = Bp["fhalo"][0]
                for r in range(2):
                    self.store_rows_T(lambda c, r=r: fh.t[:, c, 0, r:r + 1], 44, 1,
                                      lambda c0, nc_, r=r: o["f_p"][0, r:r + 1, c0:c0 + nc_], fh.r)
            self.resid_update(Bp, seqcols, mp["G2"])
            if cfg.depth > 1:
                x1r = self.scr["x1"][1]
                for kc in range(8):
                    S.dma("pool", out=self.x1[kc * 128:(kc + 1) * 128, t0:t0 + NT], in_=Bp["xT"].t[:, kc, :],
                          reads=[Bp["xT"].r[kc]], writes=[x1r])
            if cfg.depth == 1 or cfg.debug_x1:
                self.store_x_tokmajor(Bp, o["yp"][t0:t0 + NT, :])
            if ti == 0:
                self.prep_weights(1)
        if cfg.do_sample:
            self.layer0_sample()

    def layer0_sample(self):
        S = self.S
        cfg = self.cfg
        d, o = self.din, self.dout
        mp = self.modp[0]
        Bs = self.TB["s"]
        seqcols = [(1 + s, 4 * s, 4) for s in range(4)]
        for i in range(4):
            pass
        self.load_state_T(d["sa"].rearrange("s r f -> (s r) f"), 8, 4, lambda c: Bs["pbuf"][c].t[:, :, 0:2], [Bs["pbuf"][c].r for c in range(4)], 4, 2)
        self.load_state_T(d["sb"].rearrange("s r f -> (s r) f"), 60, 4, lambda c: Bs["ubuf"][c].t[:, :, 0:15], [Bs["ubuf"][c].r for c in range(4)], 4, 15)
        fh = Bs["fhalo"][0]
        self.load_state_T(d["sf"][0].rearrange("s r f -> (s r) f"), 8, 44, lambda c: fh.t[:, c, :, :], [fh.r] * 44, 4, 2)
        self.load_x_tokmajor(Bs, d["xs"])
        self.norm_mod(Bs, seqcols, mp["A1"], mp["mod"], 0)
        self.mixer_ab(Bs, seqcols, False)
        self.store_state_T(lambda c: Bs["pbuf"][c].t[:, :, 4:6], sum([Bs["pbuf"][c].r for c in range(4)], []), 4, 4, 2,
                           lambda c0, nc_: o["a_s"].rearrange("s r f -> (s r) f")[:, c0:c0 + nc_])
        self.store_state_T(lambda c: Bs["ubuf"][c].t[:, :, 4:19], sum([Bs["ubuf"][c].r for c in range(4)], []), 4, 4, 15,
                           lambda c0, nc_: o["b_s"].rearrange("s r f -> (s r) f")[:, c0:c0 + nc_])
        self.resid_update(Bs, seqcols, mp["G1"])
        self.norm_mod(Bs, seqcols, mp["A2"], mp["mod"], 24)
        self.ffn(Bs, 0, seqcols, False)
        self.store_state_T(lambda c: fh.t[:, c, :, :], fh.r, 44, 4, 2,
                           lambda c0, nc_: o["f_s"][0].rearrange("s r f -> (s r) f")[:, c0:c0 + nc_])
        self.resid_update(Bs, seqcols, mp["G2"])
        if cfg.depth == 1 or cfg.debug_x1:
            self.store_x_tokmajor(Bs, o["ys"])

    def load_state_T(self, src_rows, R, nchunks, dst_fn, dst_res, nseg, nr):
        S = self.S
        stg = self.sstage
        for c0 in range(0, nchunks, 8):
            nn = min(8, nchunks - c0)
            S.dma("sp", out=stg.t[0:R, 0:nn * 128], in_=src_rows[:, c0 * 128:(c0 + nn) * 128], writes=stg.r)
            for c in range(nn):
                p = self.ps()
                S.op("pe", lambda e, c=c, p=p: e.transpose(p.t[:, 0:R], stg.t[0:R, c * 128:(c + 1) * 128], self.ident.t[0:R, 0:R]),
                     reads=stg.r + self.ident.r, writes=p.r)
                S.op("dve", lambda e, c=c, c0=c0, p=p: e.tensor_copy(dst_fn(c0 + c), p.t[:, 0:R].rearrange("p (s r) -> p s r", s=nseg)),
                     reads=p.r, writes=dst_res[c0 + c])

    def store_state_T(self, src_fn, src_res, nchunks, nseg, nr, dst_rows_fn):
        S = self.S
        R = nseg * nr
        stg = self.sstage2
        for c0 in range(0, nchunks, 4):
            nn = min(4, nchunks - c0)
            p = self.ps()
            for c in range(nn):
                cp = self.cpstg[self.cpi]
                self.cpi = (self.cpi + 1) % 2
                S.op("dve", lambda e, c=c, c0=c0, cp=cp: e.tensor_copy(cp.t[:, 0:R].rearrange("p (s r) -> p s r", s=nseg), src_fn(c0 + c)),
                     reads=src_res, writes=cp.r)
                S.op("pe", lambda e, c=c, p=p, cp=cp: e.transpose(p.t[0:R, c * 128:(c + 1) * 128], cp.t[:, 0:R], self.ident.t[:, :]),
                     reads=cp.r + self.ident.r, writes=p.r)
            S.op("act", lambda e, p=p, nn=nn: e.copy(stg.t[0:R, 0:nn * 128], p.t[0:R, 0:nn * 128]), reads=p.r, writes=stg.r)
            S.dma("pool", out=dst_rows_fn(c0 * 128, nn * 128), in_=stg.t[0:R, 0:nn * 128], reads=stg.r, is_output=True)

    def pbf(self):
        b = self.psbf[self.psbi]
        self.psbi = (self.psbi + 1) % 2
        return b

    def layer1(self):
        S = self.S
        cfg = self.cfg
        d, o = self.din, self.dout
        seq = cfg.seq
        S.barrier()
        self.qsc, self.ksc = [], []
        for g, (win, dil) in enumerate(C_PAIRS):
            self.qsc.append(self.scratch("qsc%d" % g, [2, 128, dil, seq // dil], BF16))
            self.ksc.append(self.scratch("ksc%d" % g, [2, 128, dil, seq // dil], BF16))
        self.vsc = self.scratch("vsc", [seq, 768], BF16)
        self.osc = self.scratch("osc", [seq, 768], F32)
        self.lsc = self.scratch("lsc", [seq, 12], F32)
        self.ydsc = self.scratch("ydsc", [512, seq], BF16)
        self.bsc = self.scratch("bsc", [12, 128, 384], F32)
        Bp = self.TB["p"]
        self.halfb = self.sb("halfb", [128, 1], F32)
        S.op("pool", lambda e: e.memset(self.halfb.t[:], 0.5), writes=self.halfb.r)
        self.qstg = [self.sb("qstg%d" % k, [128, NT], BF16) for k in range(2)]
        self.qsi = 0
        self.kvrow = []
        for k in range(2):
            b_ = Buf(Bp["pbuf"][k].t[:, 0, 0:512], 1, "kvrow%d" % k)
            b_.r = Bp["pbuf"][k].r
            self.kvrow.append(b_)
        self.kvi = 0
        self.vbf = [self.sb("vbf%d" % k, [128, 256], BF16) for k in range(2)]
        self.vbi = 0
        self.rb = self.sb("rb", [32, 12], F32)
        S.dma("sp", out=self.rb.t[:, :], in_=d["rel_bias"], writes=self.rb.r)
        self.ohg = []
        for g in range(3):
            t_ = Buf(Bp["fbuf"][g].t[0:33, 0, 0:384], 1, "ohg%d" % g)
            t_.r = Bp["fbuf"][g].r
            S.dma("sp", out=t_.t, in_=d["oh"][g], writes=t_.r)
            self.ohg.append(t_)
        self.mst = [self.sb("mst0", [128, 4, 4], F32), self.sb("mst1", [128, 4, 3, 4], F32), self.sb("mst2", [128, 4, 4], F32)]
        self.ycf = [self.sb("ycf%d" % k, [128, 256], F32) for k in range(2)]
        self.ycb = [self.sb("ycb%d" % k, [128, 256], BF16) for k in range(2)]
        self.ltile = self.sb("ltile", [128, 4, 3, 4], F32)
        if cfg.do_sample:
            self.qkT_s = self.sb("qkT_s", [128, 12, 16], BF16)
            self.kvn = []
            for k in range(2):
                kv_ = Buf(Bp["xT"].t[0:4, k, 0:512], 1, "kvn%d" % k)
                kv_.r = [Bp["xT"].r[k]]
                self.kvn.append(kv_)
            self.kvni = 0
            self.kvnb = self.sb("kvnb", [4, 4, 3, 256], BF16)
            self.rbrep = self.sb("rbrep", [128, 12], F32)
            for t_ in range(4):
                S.dma("sp", out=self.rbrep.t[t_ * 32:(t_ + 1) * 32, :], in_=d["rel_bias"], writes=self.rbrep.r)
            self.dmask = self.sb("dmask_sb", [128, 4], F32)
            S.dma("sp", out=self.dmask.t[:, :], in_=d["dmask"], writes=self.dmask.r)
            self.ohs = []
            for g in range(3):
                t_ = self.sb("ohs%d" % g, [128, 132], F32)
                S.dma("sp", out=t_.t[:, :], in_=d["ohs"][g], writes=t_.r)
                self.ohs.append(t_)
            self.masks = self.sb("masks_sb", [16, 3, 132], F32)
            S.dma("sp", out=self.masks.t[:, :, :], in_=d["masks"], writes=self.masks.r)
        S.op("pool", lambda e: e.memset(Bp["fhalo"][1].t[:], 0.0), writes=Bp["fhalo"][1].r)
        self.dg = [self.sb("dg%d" % k, [128, 128], BF16) for k in range(8)]
        self.dgi = 0
        for kind, B_ in self.TB.items():
            B_["dbb"] = [self.sb(kind + "dbb%d" % i, [128, B_["nseg"], 30 + B_["sl"]], BF16) for i in range(4)]
            B_["dbo"] = [self.sb(kind + "dbo%d" % i, [128, B_["nseg"], 30 + B_["sl"]], BF16) for i in range(4)]
            B_["dst"] = self.sb(kind + "dst", [128, 4, B_["nseg"], 30], F32)
        for i in range(4):
            S.op("pool", lambda e, i=i: e.memset(Bp["dbb"][i].t[:, :, 0:30], 0.0), writes=Bp["dbb"][i].r)
        if os.environ.get("KDEBUG_STOP") == "l1start":
            return
        for ti in range(cfg.ntiles):
            self.l1_p1(Bp, ti * NT, [(0, 0, NT)], last=(ti == cfg.ntiles - 1))
        if os.environ.get("KDEBUG_STOP") in ("p1a", "p1b", "p1p"):
            return
        if cfg.do_sample:
            self.l1_p1_sample()
        if os.environ.get("KDEBUG_STOP") in ("p1", "s1", "s2", "s3", "s4"):
            return
        S.barrier()
        self.l1_attn_setup()
        if os.environ.get("KDEBUG_STOP") == "p2setup":
            return
        self.l1_attn_prompt()
        if os.environ.get("KDEBUG_STOP") == "p2p":
            return
        if cfg.do_sample:
            self.l1_attn_sample()
        if os.environ.get("KDEBUG_STOP") == "p2":
            return
        S.barrier()
        for ti in range(cfg.ntiles):
            self.l1_p3(Bp, ti * NT, [(0, 0, NT)], last=(ti == cfg.ntiles - 1))
        if cfg.do_sample:
            self.l1_p3_sample()

    def l1_dpath(self, B, seqcols, ydst_fn, want_state):
        S = self.S
        n, nseg, sl = B["n"], B["nseg"], B["sl"]
        hT, y = B["hT"], B["y"]
        dcw, dcb, dlg, dlb = self.V["dcw"], self.V["dcb"], self.V["dlg"], self.V["dlb"]
        rhs_h = lambda kc: (hT.t[:, kc, :], hT.r)

        def v3(ap):
            return ap.rearrange("p (s t) -> p s t", s=nseg)

        zb = y
        sq = B["sq"]
        dstate = B["dst"]
        for g2 in range(2):
            sl_w, wv = self.load_w(self.W["d1"][g2])
            for i2 in range(2):
                i = g2 * 2 + i2
                pV = self.ps(); self.mm(pV, n, wv, i2 * 256, rhs_h, 8, sl_w.r)
                pG = self.ps(); self.mm(pG, n, wv, i2 * 256 + 128, rhs_h, 8, sl_w.r)
                t1 = self.tmp(B)
                S.op("act", lambda e, pG=pG, t1=t1: e.activation(t1.t[:, 0:n], pG.t[:, 0:n], AF.Tanh, scale=0.5), reads=pG.r, writes=t1.r)
                S.op("act", lambda e, t1=t1: e.activation(t1.t[:, 0:n], t1.t[:, 0:n], AF.Identity, bias=self.halfb.t[:, 0:1], scale=0.5), reads=t1.r + self.halfb.r, writes=t1.r)
                db = B["dbb"][i]
                S.op("dve", lambda e, pV=pV, t1=t1, db=db: e.tensor_tensor(db.t[:, :, 30:30 + sl], v3(pV.t[:, 0:n]), v3(t1.t[:, 0:n]), ALU.mult),
                     reads=pV.r + t1.r, writes=db.r)
                if want_state:
                    k = min(30, sl)
                    S.op("dve", lambda e, pV=pV, t1=t1, i=i, k=k: e.tensor_tensor(dstate.t[:, i, :, 30 - k:30], v3(pV.t[:, 0:n])[:, :, sl - k:sl], v3(t1.t[:, 0:n])[:, :, sl - k:sl], ALU.mult),
                         reads=pV.r + t1.r, writes=dstate.r)
                dbo = B["dbo"][i]
                S.op("pool", lambda e, db=db, dbo=dbo: e.tensor_copy(dbo.t[:, :, 0:29 + sl], db.t[:, :, 1:30 + sl]), reads=db.r, writes=dbo.r)
                pc = self.ps()
                for j in range(31):
                    dg = self.dg[self.dgi]
                    self.dgi = (self.dgi + 1) % len(self.dg)
                    S.op("dve", lambda e, dg=dg, j=j, i=i: e.tensor_scalar(dg.t[:, :], self.identb.t[:, :], dcw.t[:, j * 4 + i:j * 4 + i + 1], None, ALU.mult),
                         reads=self.identb.r + dcw.r, writes=dg.r)
                    src, jj = (db, j) if j % 2 == 0 else (dbo, j - 1)
                    rhs = src.t[:, 0, jj:jj + sl] if nseg == 1 else src.t[:, :, jj:jj + sl]
                    S.op("pe", lambda e, dg=dg, rhs=rhs, pc=pc, j=j: e.matmul(pc.t[:, 0:n], lhsT=dg.t[:, :], rhs=rhs, start=(j == 0), stop=(j == 30)),
                         reads=dg.r + src.r, writes=pc.r)
                S.op("act", lambda e, pc=pc, i=i: e.activation(zb.t[:, i, :], pc.t[:, 0:n], AF.Identity, bias=dcb.t[:, i:i + 1], scale=1.0),
                     reads=pc.r + dcb.r, writes=zb.r)
                S.op("act", lambda e, pc=pc, i=i: e.activation(sq.t[:, 4 + i, :], pc.t[:, 0:n], AF.Square, bias=dcb.t[:, i:i + 1], scale=1.0),
                     reads=pc.r + dcb.r, writes=sq.r)
                S.op("act", lambda e, pc=pc, i=i: e.activation(sq.t[:, i, :], pc.t[:, 0:n], AF.Identity, bias=dcb.t[:, i:i + 1], scale=1.0),
                     reads=pc.r + dcb.r, writes=sq.r)
        p1 = self.ps()
        for i in range(4):
            S.op("pe", lambda e, i=i: e.matmul(p1.t[:, 0:n], lhsT=self.onesb.t[:, :], rhs=sq.t[:, i, :], start=(i == 0), stop=(i == 3)),
                 reads=sq.r + self.onesb.r, writes=p1.r, signal=(i == 3))
        p2 = self.ps()
        for i in range(4):
            S.op("pe", lambda e, i=i: e.matmul(p2.t[:, 0:n], lhsT=self.onesb.t[:, :], rhs=sq.t[:, 4 + i, :], start=(i == 0), stop=(i == 3)),
                 reads=sq.r + self.onesb.r, writes=p2.r, signal=(i == 3))
        mean = Buf(zb.t[:, 4, :], 1, "ln_mean"); mean.r = zb.r
        var = Buf(zb.t[:, 5, :], 1, "ln_var"); var.r = zb.r
        S.op("dve", lambda e: e.tensor_scalar(mean.t[:, 0:n], p1.t[:, 0:n], 1.0 / 512, None, ALU.mult), reads=p1.r, writes=mean.r)
        S.op("dve", lambda e: e.tensor_tensor(var.t[:, 0:n], mean.t[:, 0:n], mean.t[:, 0:n], ALU.mult), reads=mean.r, writes=var.r)
        S.op("dve", lambda e: e.scalar_tensor_tensor(var.t[:, 0:n], p2.t[:, 0:n], 1.0 / 512, var.t[:, 0:n], ALU.mult, ALU.subtract),
             reads=p2.r + var.r, writes=var.r)
        S.op("act", lambda e: e.activation(var.t[:, 0:n], var.t[:, 0:n], AF.Sqrt, bias=self.epsb.t[:, 1:2], scale=1.0), reads=var.r + self.epsb.r, writes=var.r)
        S.op("dve", lambda e: e.reciprocal(var.t[:, 0:n], var.t[:, 0:n]), reads=var.r, writes=var.r)
        for i in range(4):
            t = self.tmp(B)
            S.op("dve", lambda e, i=i, t=t: e.tensor_tensor(t.t[:, 0:n], zb.t[:, i, :], mean.t[:, 0:n], ALU.subtract), reads=zb.r + mean.r, writes=t.r)
            S.op("dve", lambda e, t=t: e.tensor_tensor(t.t[:, 0:n], t.t[:, 0:n], var.t[:, 0:n], ALU.mult), reads=t.r + var.r, writes=t.r)
            dst, dres = ydst_fn(i)
            S.op("act", lambda e, i=i, t=t, dst=dst: e.activation(dst, t.t[:, 0:n], AF.Silu, bias=dlb.t[:, i:i + 1], scale=dlg.t[:, i:i + 1]),
                 reads=t.r + dlb.r + dlg.r, writes=dres)

    def l1_p1(self, B, t0, seqcols, last):
        S = self.S
        cfg = self.cfg
        d, o = self.din, self.dout
        mp = self.modp[1]
        n = NT
        xT, hT, cat = B["xT"], B["hT"], B["cat"]
        x1r = self.scr["x1"][1]
        for kc in range(8):
            S.dma("sp", out=xT.t[:, kc, :], in_=self.x1[kc * 128:(kc + 1) * 128, t0:t0 + NT], reads=[x1r], writes=[xT.r[kc]])
        self.norm_mod(B, seqcols, mp["A1"], mp["mod"], 0)
        rhs_h = lambda kc: (hT.t[:, kc, :], hT.r)
        for grp in range(3):
            sl_w, wv = self.load_w(self.W["qk1"][grp])
            for c4 in range(4):
                cc = grp * 4 + c4
                isq = cc < 6
                g = (cc // 2) if isq else ((cc - 6) // 2)
                half = cc % 2
                dil = C_PAIRS[g][1]
                M = NT // dil
                p = self.ps()
                self.mm(p, n, wv, c4 * 128, rhs_h, 8, sl_w.r)
                stg = self.qstg[self.qsi]
                self.qsi = (self.qsi + 1) % len(self.qstg)
                S.op("act", lambda e, p=p, stg=stg, dil=dil, isq=isq: e.activation(
                    stg.t[:, 0:n].rearrange("p (r m) -> p m r", r=dil), p.t[:, 0:n].rearrange("p (m r) -> p m r", r=dil),
                    AF.Copy, scale=(0.125 if isq else 1.0)), reads=p.r, writes=stg.r)
                dst = (self.qsc if isq else self.ksc)[g]
                dres = self.scr[("qsc%d" if isq else "ksc%d") % g][1]
                S.dma("act", out=dst[half, :, :, t0 // dil:t0 // dil + M], in_=stg.t[:, 0:n].rearrange("p (r m) -> p r m", r=dil),
                      reads=stg.r, writes=[dres])
        if os.environ.get("KDEBUG_STOP") == "p1a":
            return
        vres = self.scr["vsc"][1]
        for g in range(3):
            wb = C_PAIRS[g][0]
            sl_w, wv = self.load_w(self.W["kv1"][g])
            for s in range(4):
                p = self.ps()
                for kc in range(8):
                    S.op("pe", lambda e, kc=kc, s=s, p=p, wv=wv: e.matmul(p.t[:, 0:512], lhsT=hT.t[:, kc, s * 128:(s + 1) * 128], rhs=wv[:, kc, 0:512],
                                                                        start=(kc == 0), stop=(kc == 7)),
                         reads=hT.r + sl_w.r, writes=p.r, signal=(kc == 7))
                kv = self.kvrow[self.kvi]
                self.kvi = (self.kvi + 1) % 2
                S.op("act", lambda e, p=p, kv=kv: e.copy(kv.t[:, 0:512], p.t[:, 0:512]), reads=p.r, writes=kv.r)
                tok = t0 + s * 128
                row = tok - (cfg.seq - min(wb, cfg.seq))
                vb = self.vbf[self.vbi]
                self.vbi = (self.vbi + 1) % 2
                S.op("act", lambda e, p=p, vb=vb: e.copy(vb.t[:, :], p.t[:, 256:512]), reads=p.r, writes=vb.r)
                if row >= 0:
                    S.dma("act", out=o["c%d_p" % g][row:row + 128, :], in_=kv.t[:, 0:512], reads=kv.r, is_output=True)
                S.dma("act", out=self.vsc[tok:tok + 128, g * 256:(g + 1) * 256], in_=vb.t[:, :], reads=vb.r, writes=[vres])
        if os.environ.get("KDEBUG_STOP") == "p1b":
            return
        self.l1_dpath(B, seqcols, lambda i: (cat.t[:, 2 + i, :], [cat.r[2 + i]]), want_state=last)
        ydr = self.scr["ydsc"][1]
        for i in range(4):
            S.dma("act", out=self.ydsc[i * 128:(i + 1) * 128, t0:t0 + NT], in_=cat.t[:, 2 + i, :], reads=[cat.r[2 + i]], writes=[ydr])
        if last:
            self.store_rows_T(lambda c: B["dst"].t[:, c, 0, :], 4, 30, lambda c0, nc_: o["d_p"][:, c0:c0 + nc_], B["dst"].r)
        else:
            for i in range(4):
                db = B["dbb"][i]
                S.op("pool", lambda e, db=db: e.tensor_copy(db.t[:, :, 0:30], db.t[:, :, NT:NT + 30]), reads=db.r, writes=db.r)

    def l1_attn_setup(self):
        S = self.S
        d = self.din
        Bp = self.TB["p"]
        y = Bp["y"]
        self.biasmat = []
        for gh in range(12):
            bm = Buf(y.t[:, gh // 2, (gh % 2) * 256:(gh % 2) * 256 + 256], 1, "biasmat%d" % gh)
            self.biasmat.append(bm)
        rb = self.rb
        self.bl = [self.sb("bl%d" % k, [128, 128], BF16) for k in range(2)]
        hT_, act_ = Bp["hT"], Bp["act"]
        self.ohgb = [Buf(hT_.t[0:33, 6, 0:384], 1, "ohgb0"), Buf(hT_.t[0:33, 7, 0:384], 1, "ohgb1"), Buf(act_.t[0:33, 16, 0:384], 1, "ohgb2")]
        for g in range(3):
            S.op("dve", lambda e, g=g: e.tensor_copy(self.ohgb[g].t, self.ohg[g].t), reads=self.ohg[g].r, writes=self.ohgb[g].r)
        bres = self.scr["bsc"][1]
        for g in range(3):
            ohg = self.ohg[g]
            for h in range(4):
                gh = g * 4 + h
                lt = self.tmp(Bp)
                S.op("dve", lambda e, lt=lt: e.memset(lt.t[0:33, 0:128], 1.0), writes=lt.r)
                S.op("dve", lambda e, lt=lt, gh=gh: e.tensor_scalar(lt.t[0:32, 0:128], lt.t[0:32, 0:128], rb.t[0:32, gh:gh + 1], None, ALU.mult),
                     reads=lt.r + rb.r, writes=lt.r)
                hi, lo = self.bl[0], self.bl[1]
                S.op("dve", lambda e, lt=lt, hi=hi: e.tensor_copy(hi.t[0:33, :], lt.t[0:33, 0:128]), reads=lt.r, writes=hi.r)
                S.op("dve", lambda e, lt=lt, hi=hi: e.tensor_tensor(lt.t[0:33, 0:128], lt.t[0:33, 0:128], hi.t[0:33, :], ALU.subtract), reads=lt.r + hi.r, writes=lt.r)
                S.op("dve", lambda e, lt=lt, lo=lo: e.tensor_copy(lo.t[0:33, :], lt.t[0:33, 0:128]), reads=lt.r, writes=lo.r)
                p = self.ps()
                S.op("pe", lambda e, hi=hi, p=p, g=g: e.matmul(p.t[:, 0:384], lhsT=hi.t[0:33, :], rhs=self.ohgb[g].t, start=True, stop=False),
                     reads=hi.r + self.ohgb[g].r, writes=p.r)
                S.op("pe", lambda e, lo=lo, p=p, g=g: e.matmul(p.t[:, 0:384], lhsT=lo.t[0:33, :], rhs=self.ohgb[g].t, start=False, stop=True),
                     reads=lo.r + self.ohgb[g].r, writes=p.r)
                rt = self.tmp(Bp)
                S.op("act", lambda e, rt=rt, p=p: e.copy(rt.t[:, 0:384], p.t[:, 0:384]), reads=p.r, writes=rt.r)
                S.dma("sp", out=self.bsc[gh], in_=rt.t[:, 0:384], reads=rt.r, writes=[bres])
                skew = bass.AP(tensor=self.bsc.tensor, offset=gh * 128 * 384, ap=[[383, 128], [1, 256]])
                S.dma("sp", out=self.biasmat[gh].t, in_=skew, reads=[bres], writes=self.biasmat[gh].r)

    def l1_attn_prompt(self):
        S = self.S
        cfg = self.cfg
        Bp = self.TB["p"]
        hT, act = Bp["hT"], Bp["act"]
        seq = cfg.seq
        def view(ap, name):
            return Buf(ap, 1, name)
        qt = [view(hT.t[:, k, 0:256].rearrange("p (c q) -> p c q", c=2), "qt%d" % k) for k in range(2)]
        kt = [view(hT.t[:, 2 + k, 0:512].rearrange("p (c q) -> p c q", c=2), "kt%d" % k) for k in range(2)]
        vt = [view(hT.t[:, 4 + k, 0:512].rearrange("p (c q) -> p c q", c=2), "vt%d" % k) for k in range(2)]
        pb_ = [view(act.t[:, k, 0:256], "pbf%d" % k) for k in range(4)]
        pT = [view(act.t[:, 4 + k, 0:256].rearrange("p (c q) -> p c q", c=2), "pT%d" % k) for k in range(4)]
        sc = [view(Bp["pbuf"][k].t[:, 0, 0:256], "sc%d" % k) for k in range(4)]
        ost = [view(Bp["xT"].t[:, k, 0:256].rearrange("p (h e) -> p h e", h=4), "ost%d" % k) for k in range(2)]
        lst = [view(Bp["xT"].t[:, 2 + k, 0:4], "lst%d" % k) for k in range(2)]
        st = [view(Bp["xT"].t[:, 4 + k, 0:8], "stat%d" % k) for k in range(4)]
        osr, lsr, vres = self.scr["osc"][1], self.scr["lsc"][1], self.scr["vsc"][1]
        ui = 0
        hi = 0
        for g, (win, dil) in enumerate(C_PAIRS):
            L = seq // dil
            vview = self.vsc.rearrange("(m r) f -> r m f", r=dil)
            oview = self.osc.rearrange("(m r) f -> r m f", r=dil)
            lview = self.lsc.rearrange("(m r) f -> r m f", r=dil)
            qres, kres = self.scr["qsc%d" % g][1], self.scr["ksc%d" % g][1]
            for r in range(dil):
                for Bk in range(L // 128):
                    m0 = Bk * 128
                    nk = 128 if Bk == 0 else 256
                    k0 = 256 - nk
                    q_, k_, v_ = qt[ui % 2], kt[ui % 2], vt[ui % 2]
                    o_, l_ = ost[ui % 2], lst[ui % 2]
                    ui += 1
                    S.dma("sp", out=q_.t, in_=self.qsc[g][:, :, r, m0:m0 + 128].rearrange("c p q -> p c q"), reads=[qres], writes=q_.r)
                    S.dma("sp", out=k_.t[:, :, k0:256], in_=self.ksc[g][:, :, r, m0 + 128 - nk:m0 + 128].rearrange("c p q -> p c q"),
                          reads=[kres], writes=k_.r)
                    for hh in range(2 - nk // 128, 2):
                        ms = m0 - 128 + hh * 128
                        S.dma("sp", out=v_.t[:, hh, :], in_=vview[r, ms:ms + 128, g * 256:(g + 1) * 256], reads=[vres], writes=v_.r)
                    hs = []
                    for h in range(4):
                        c, pb0 = h // 2, (h % 2) * 64
                        gh = g * 4 + h
                        p = self.ps()
                        S.op("pe", lambda e, p=p, q_=q_, k_=k_, c=c, pb0=pb0, k0=k0: e.matmul(
                            p.t[:, k0:256], lhsT=q_.t[pb0:pb0 + 64, c, :], rhs=k_.t[pb0:pb0 + 64, c, k0:256], start=True, stop=True),
                             reads=q_.r + k_.r, writes=p.r)
                        s_ = sc[hi % 4]; pb = pb_[hi % 4]; pt = pT[hi % 4]; stt = st[hi % 4]
                        hi += 1
                        hs.append((h, pb, pt, stt))
                        bm = self.biasmat[gh]
                        S.op("dve", lambda e, p=p, s_=s_, bm=bm, k0=k0: e.tensor_tensor(s_.t[:, k0:256], p.t[:, k0:256], bm.t[:, k0:256], ALU.add),
                             reads=p.r + bm.r, writes=s_.r)
                        S.op("dve", lambda e, s_=s_, stt=stt, k0=k0: e.tensor_reduce(stt.t[:, 0:1], s_.t[:, k0:256], AX.X, ALU.max),
                             reads=s_.r, writes=stt.r)
                        S.op("dve", lambda e, stt=stt: e.tensor_scalar(stt.t[:, 1:2], stt.t[:, 0:1], -1.0, None, ALU.mult), reads=stt.r, writes=stt.r)
                        S.op("act", lambda e, s_=s_, pb=pb, stt=stt, k0=k0: e.activation(pb.t[:, k0:256], s_.t[:, k0:256], AF.Exp, bias=stt.t[:, 1:2], scale=1.0,
                                                                                       accum_out=stt.t[:, 2:3]),
                             reads=s_.r + stt.r, writes=pb.r + stt.r)
                        S.op("dve", lambda e, stt=stt: e.reciprocal(stt.t[:, 3:4], stt.t[:, 2:3]), reads=stt.r, writes=stt.r)
                    for (h, pb, pt, stt) in hs:
                        pp = self.pbf()
                        for hh in range(2 - nk // 128, 2):
                            S.op("pe", lambda e, pp=pp, pb=pb, hh=hh: e.transpose(pp.t[:, hh * 128:(hh + 1) * 128], pb.t[:, hh * 128:(hh + 1) * 128], self.identb.t[:, :]),
                                 reads=pb.r + self.identb.r, writes=pp.r, signal=(hh == 1))
                        S.op("act", lambda e, pp=pp, pt=pt, k0=k0: e.copy(pt.t[:, :, :].rearrange("p c q -> p (c q)")[:, k0:256], pp.t[:, k0:256]),
                             reads=pp.r, writes=pt.r)
                        po = self.ps()
                        for hh in range(2 - nk // 128, 2):
                            S.op("pe", lambda e, po=po, pt=pt, v_=v_, hh=hh, h=h, nk=nk: e.matmul(
                                po.t[:, 0:64], lhsT=pt.t[:, hh, :], rhs=v_.t[:, hh, h * 64:(h + 1) * 64], start=(hh == 2 - nk // 128), stop=(hh == 1)),
                                 reads=pt.r + v_.r, writes=po.r, signal=(hh == 1))
                        S.op("act", lambda e, po=po, o_=o_, stt=stt, h=h: e.activation(o_.t[:, h, :], po.t[:, 0:64], AF.Copy, scale=stt.t[:, 3:4]),
                             reads=po.r + stt.r, writes=o_.r)
                        S.op("act", lambda e, stt=stt: e.activation(stt.t[:, 4:5], stt.t[:, 2:3], AF.Ln), reads=stt.r, writes=stt.r)
                        S.op("act", lambda e, stt=stt, l_=l_, h=h: e.activation(l_.t[:, h:h + 1], stt.t[:, 4:5], AF.Identity, bias=stt.t[:, 0:1], scale=1.0),
                             reads=stt.r, writes=l_.r)
                    S.dma("act", out=oview[r, m0:m0 + 128, g * 256:(g + 1) * 256], in_=o_.t.rearrange("p h e -> p (h e)"), reads=o_.r, writes=[osr])
                    S.dma("act", out=lview[r, m0:m0 + 128, g * 4:(g + 1) * 4], in_=l_.t, reads=l_.r, writes=[lsr])

    def l1_merge(self, B, n, otile, ltile):
        S = self.S
        ns = max(1, n // 128)
        rows = min(n, 128)
        cat = B["cat"]
        mx, ex, sm = self.mst[0], self.mst[1], self.mst[2]
        lt = ltile.t
        R = slice(0, rows)
        S.op("dve", lambda e: e.tensor_tensor(mx.t[R, 0:ns, :], lt[R, :, 0, :], lt[R, :, 1, :], ALU.max), reads=ltile.r, writes=mx.r)
        S.op("dve", lambda e: e.tensor_tensor(mx.t[R, 0:ns, :], mx.t[R, 0:ns, :], lt[R, :, 2, :], ALU.max), reads=ltile.r + mx.r, writes=mx.r)
        for g in range(3):
            S.op("dve", lambda e, g=g: e.tensor_tensor(ex.t[R, 0:ns, g, :], lt[R, :, g, :], mx.t[R, 0:ns, :], ALU.subtract), reads=ltile.r + mx.r, writes=ex.r)
        S.op("act", lambda e: e.activation(ex.t[R, 0:ns, :, :].rearrange("p s g h -> p (s g h)"), ex.t[R, 0:ns, :, :].rearrange("p s g h -> p (s g h)"), AF.Exp),
             reads=ex.r, writes=ex.r)
        S.op("dve", lambda e: e.tensor_tensor(sm.t[R, 0:ns, :], ex.t[R, 0:ns, 0, :], ex.t[R, 0:ns, 1, :], ALU.add), reads=ex.r, writes=sm.r)
        S.op("dve", lambda e: e.tensor_tensor(sm.t[R, 0:ns, :], sm.t[R, 0:ns, :], ex.t[R, 0:ns, 2, :], ALU.add), reads=ex.r + sm.r, writes=sm.r)
        S.op("dve", lambda e: e.reciprocal(sm.t[R, 0:ns, :], sm.t[R, 0:ns, :]), reads=sm.r, writes=sm.r)
        for g in range(3):
            S.op("dve", lambda e, g=g: e.tensor_tensor(ex.t[R, 0:ns, g, :], ex.t[R, 0:ns, g, :], sm.t[R, 0:ns, :], ALU.mult), reads=ex.r + sm.r, writes=ex.r)
        for s in range(ns):
            yc = self.ycf[s % 2]
            for h in range(4):
                for g in range(3):
                    src = otile.t[R, s, g, h * 64:(h + 1) * 64]
                    wcol = ex.t[R, s, g, h:h + 1]
                    if g == 0:
                        S.op("dve", lambda e, yc=yc, src=src, wcol=wcol, h=h: e.tensor_scalar(yc.t[R, h * 64:(h + 1) * 64], src, wcol, None, ALU.mult),
                             reads=otile.r + ex.r, writes=yc.r)
                    else:
                        S.op("dve", lambda e, yc=yc, src=src, wcol=wcol, h=h: e.scalar_tensor_tensor(
                            yc.t[R, h * 64:(h + 1) * 64], src, wcol, yc.t[R, h * 64:(h + 1) * 64], ALU.mult, ALU.add),
                             reads=otile.r + ex.r + yc.r, writes=yc.r)
            ycb = self.ycb[s % 2]
            S.op("act", lambda e, yc=yc, ycb=ycb: e.copy(ycb.t[R, :], yc.t[R, :]), reads=yc.r, writes=ycb.r)
            pp = self.pbf()
            for c in range(2):
                S.op("pe", lambda e, pp=pp, ycb=ycb, c=c: e.transpose(pp.t[:, c * 128:c * 128 + rows], ycb.t[R, c * 128:(c + 1) * 128], self.identb.t[R, R]),
                     reads=ycb.r + self.identb.r, writes=pp.r, signal=(c == 1))
            for c in range(2):
                S.op("act", lambda e, pp=pp, c=c, s=s: e.copy(cat.t[:, c, s * rows:(s + 1) * rows], pp.t[:, c * 128:c * 128 + rows]),
                     reads=pp.r, writes=[cat.r[c]])

    def l1_p3(self, B, t0, seqcols, last):
        S = self.S
        cfg = self.cfg
        d, o = self.din, self.dout
        mp = self.modp[1]
        n = NT
        xT, cat, y = B["xT"], B["cat"], B["y"]
        x1r = self.scr["x1"][1]
        for kc in range(8):
            S.dma("sp", out=xT.t[:, kc, :], in_=self.x1[kc * 128:(kc + 1) * 128, t0:t0 + NT], reads=[x1r], writes=[xT.r[kc]])
        xt = B["xtok"]
        otile = Buf(xt.t[:, :, 0:768].rearrange("p s (g f) -> p s g f", g=3), 1, "otile")
        otile.r = xt.r
        S.dma("sp", out=xt.t[:, :, 0:768], in_=self.osc[t0:t0 + NT, :].rearrange("(s p) f -> p s f", p=128), reads=[self.scr["osc"][1]], writes=xt.r)
        lt = self.ltile
        S.dma("sp", out=lt.t[:, :, :, :].rearrange("p s g h -> p s (g h)"), in_=self.lsc[t0:t0 + NT, :].rearrange("(s p) f -> p s f", p=128),
              reads=[self.scr["lsc"][1]], writes=lt.r)
        self.l1_merge(B, n, otile, lt)
        ydr = self.scr["ydsc"][1]
        for i in range(4):
            S.dma("sp", out=cat.t[:, 2 + i, :], in_=self.ydsc[i * 128:(i + 1) * 128, t0:t0 + NT], reads=[ydr], writes=[cat.r[2 + i]])
        self.l1_tail(B, seqcols, last, lambda: self.store_x_tokmajor(B, o["yp"][t0:t0 + NT, :]), prompt=True)

    def l1_tail(self, B, seqcols, last, store_fn, prompt):
        S = self.S
        o = self.dout
        mp = self.modp[1]
        n = B["n"]
        cat, y = B["cat"], B["y"]
        for g in range(2):
            sl_w, wv = self.load_w(self.W["out1"][g])
            for c4 in range(4):
                oc = g * 4 + c4
                p = self.ps()
                self.mm(p, n, wv, c4 * 128, lambda kc: (cat.t[:, kc, :], [cat.r[kc]]), 6, sl_w.r)
                S.op("act", lambda e, oc=oc, p=p: e.copy(y.t[:, oc, :], p.t[:, 0:n]), reads=p.r, writes=[y.r[oc]])
        self.resid_update(B, seqcols, mp["G1"])
        self.norm_mod(B, seqcols, mp["A2"], mp["mod"], 24)
        self.ffn(B, 1, seqcols, False)
        fh = B["fhalo"][1]
        if prompt and last:
            for r in range(2):
                self.store_rows_T(lambda c, r=r: fh.t[:, c, 0, r:r + 1], 44, 1,
                                  lambda c0, nc_, r=r: o["f_p"][1, r:r + 1, c0:c0 + nc_], fh.r)
        if not prompt:
            self.store_state_T(lambda c: fh.t[:, c, :, :], fh.r, 44, 4, 2,
                               lambda c0, nc_: o["f_s"][1].rearrange("s r f -> (s r) f")[:, c0:c0 + nc_])
        self.resid_update(B, seqcols, mp["G2"])
        store_fn()

    def l1_p1_sample(self):
        S = self.S
        d, o = self.din, self.dout
        mp = self.modp[1]
        Bs = self.TB["s"]
        n = 16
        seqcols = [(1 + s, 4 * s, 4) for s in range(4)]
        hT, cat = Bs["hT"], Bs["cat"]
        Bp_ = self.TB["p"]
        sdin = Buf(Bp_["xT"].t[:, 2, 0:480].rearrange("p (c s r) -> p c s r", c=4, s=4), 1, "sdin")
        sdin.r = [Bp_["xT"].r[2]]
        self.load_state_T(d["sd"].rearrange("s r f -> (s r) f"), 120, 4, lambda c: sdin.t[:, c, :, :], [sdin.r] * 4, 4, 30)
        for i in range(4):
            S.op("dve", lambda e, i=i: e.tensor_copy(Bs["dbb"][i].t[:, :, 0:30], sdin.t[:, i, :, :]), reads=sdin.r, writes=Bs["dbb"][i].r)
            S.op("dve", lambda e, i=i: e.tensor_copy(Bs["dst"].t[:, i, :, 0:26], sdin.t[:, i, :, 4:30]), reads=sdin.r, writes=Bs["dst"].r)
        fh = Bs["fhalo"][1]
        self.load_state_T(d["sf"][1].rearrange("s r f -> (s r) f"), 8, 44, lambda c: fh.t[:, c, :, :], [fh.r] * 44, 4, 2)
        self.norm_mod(Bs, seqcols, mp["A1"], mp["mod"], 0)
        if os.environ.get("KDEBUG_STOP") == "s1":
            return
        rhs_h = lambda kc: (hT.t[:, kc, :], hT.r)
        qk = self.qkT_s
        for grp in range(3):
            sl_w, wv = self.load_w(self.W["qk1"][grp])
            for c4 in range(4):
                cc = grp * 4 + c4
                p = self.ps()
                self.mm(p, n, wv, c4 * 128, rhs_h, 8, sl_w.r)
                S.op("act", lambda e, p=p, cc=cc: e.activation(qk.t[:, cc, :], p.t[:, 0:n], AF.Copy, scale=(0.125 if cc < 6 else 1.0)),
                     reads=p.r, writes=qk.r)
        if os.environ.get("KDEBUG_STOP") == "s2":
            return
        kvnb = self.kvnb
        for g in range(3):
            wb = C_PAIRS[g][0]
            sl_w, wv = self.load_w(self.W["kv1"][g])
            for b in range(4):
                p = self.ps()
                for kc in range(8):
                    S.op("pe", lambda e, kc=kc, b=b, p=p, wv=wv: e.matmul(p.t[0:4, 0:512], lhsT=hT.t[:, kc, b * 4:(b + 1) * 4], rhs=wv[:, kc, 0:512],
                                                                        start=(kc == 0), stop=(kc == 7)),
                         reads=hT.r + sl_w.r, writes=p.r, signal=(kc == 7))
                kvn = self.kvn[self.kvni]
                self.kvni = (self.kvni + 1) % 2
                flags = os.environ.get("KDEBUG_S3", "")
                if "a" not in flags:
                    S.op("act", lambda e, p=p, kvn=kvn: e.copy(kvn.t[0:4, :], p.t[0:4, 0:512]), reads=p.r, writes=kvn.r)
                if "d" not in flags:
                    S.op("dve", lambda e, p=p, b=b, g=g: e.tensor_copy(kvnb.t[0:4, b, g, :], p.t[0:4, 256:512]), reads=p.r, writes=kvnb.r)
                cin = d[("c128", "c512", "c2048")[g]]
                for r0 in ([] if os.environ.get("KDEBUG_NOCOPY") else range(0, wb - 4, 256)):
                    r1 = min(wb - 4, r0 + 256)
                    S.dma("pool", out=o["c%d_s" % g][b, r0:r1, :], in_=cin[b, 4 + r0:4 + r1, :], is_output=True)
                if "o" not in flags:
                    S.dma("pool", out=o["c%d_s" % g][b, wb - 4:wb, :], in_=kvn.t[0:4, :], reads=kvn.r, is_output=True)
        if os.environ.get("KDEBUG_STOP") == "s3":
            return
        self.l1_dpath(Bs, seqcols, lambda i: (cat.t[:, 2 + i, :], [cat.r[2 + i]]), want_state=True)
        if os.environ.get("KDEBUG_STOP") == "s4":
            return
        self.store_state_T(lambda c: Bs["dst"].t[:, c, :, :], Bs["dst"].r, 4, 4, 30,
                           lambda c0, nc_: o["d_s"].rearrange("s r f -> (s r) f")[:, c0:c0 + nc_])

    def l1_attn_sample(self):
        S = self.S
        d = self.din
        Bp, Bs = self.TB["p"], self.TB["s"]
        act = Bp["act"]
        cat = Bs["cat"]
        qk = self.qkT_s
        kvnb = self.kvnb

        def view(ap, name):
            return Buf(ap, 1, name)
        y = Bp["y"]
        kcf = [view(y.t[:, 6 + k, :], "kcf%d" % k) for k in range(2)]
        kcb = [view(act.t[:, 8 + k, :], "kcb%d" % k) for k in range(3)]
        kTc = [view(act.t[:, 12 + k, 0:256].rearrange("p (c q) -> p c q", c=2), "kTc%d" % k) for k in range(2)]
        xk = Bp["xtok"]
        STs = view(xk.t[:, 1, 0:48].rearrange("p (g h t) -> p g h t", g=3, h=4), "STs")
        SNs = view(xk.t[:, 1, 64:112].rearrange("p (g h t) -> p g h t", g=3, h=4), "SNs")
        scs = view(xk.t[:, 2, 0:396].rearrange("p (g k) -> p g k", g=3), "scs")
        stt = view(xk.t[:, 1, 128:136], "stts")
        pn = view(act.t[:, 14, 0:396].rearrange("p (g k) -> p g k", g=3), "pn")
        pTs = view(act.t[:, 15, 0:48].rearrange("p (g q) -> p g q", g=3), "pTs")
        pTn = view(act.t[:, 15, 64:112].rearrange("p (g q) -> p g q", g=3), "pTn")
        biasS = view(xk.t[:, 0, 0:396].rearrange("p (g k) -> p g k", g=3), "biasS")
        rbrep = self.rbrep
        self.ohsb = [view(act.t[:, 17 + k, 0:132], "ohsb%d" % k) for k in range(2)]
        for g in range(3):
            lt = self.tmp(Bp)
            for h in range(4):
                gh = g * 4 + h
                S.op("dve", lambda e, lt=lt, h=h, gh=gh: e.tensor_scalar(lt.t[:, h * 4:(h + 1) * 4], self.dmask.t[:, 0:4], rbrep.t[:, gh:gh + 1], None, ALU.mult),
                     reads=self.dmask.r + rbrep.r, writes=lt.r)
            hi, lo = self.bl[0], self.bl[1]
            ohb = self.ohsb[g % 2]
            S.op("dve", lambda e, ohb=ohb, g=g: e.tensor_copy(ohb.t, self.ohs[g].t[:, 0:132]), reads=self.ohs[g].r, writes=ohb.r)
            S.op("dve", lambda e, lt=lt, hi=hi: e.tensor_copy(hi.t[:, 0:16], lt.t[:, 0:16]), reads=lt.r, writes=hi.r)
            S.op("dve", lambda e, lt=lt, hi=hi: e.tensor_tensor(lt.t[:, 0:16], lt.t[:, 0:16], hi.t[:, 0:16], ALU.subtract), reads=lt.r + hi.r, writes=lt.r)
            S.op("dve", lambda e, lt=lt, lo=lo: e.tensor_copy(lo.t[:, 0:16], lt.t[:, 0:16]), reads=lt.r, writes=lo.r)
            p = self.ps()
            S.op("pe", lambda e, hi=hi, p=p, ohb=ohb: e.matmul(p.t[0:16, 0:132], lhsT=hi.t[:, 0:16], rhs=ohb.t, start=True, stop=False),
                 reads=hi.r + ohb.r, writes=p.r)
            S.op("pe", lambda e, lo=lo, p=p, ohb=ohb: e.matmul(p.t[0:16, 0:132], lhsT=lo.t[:, 0:16], rhs=ohb.t, start=False, stop=True),
                 reads=lo.r + ohb.r, writes=p.r)
            S.op("dve", lambda e, p=p, g=g: e.tensor_tensor(biasS.t[0:16, g, :], p.t[0:16, 0:132], self.masks.t[0:16, g, :], ALU.add),
                 reads=p.r + self.masks.r, writes=biasS.r)
        self.kmask = [view(act.t[:, 19 + k, 0:96].rearrange("p (c t) -> p c t", c=6), "kmask%d" % k) for k in range(2)]
        for k in range(2):
            km = self.kmask[k]
            S.op("dve", lambda e, km=km: e.memset(km.t, 0.0), writes=km.r)
            S.op("dve", lambda e, km=km, k=k: e.tensor_copy(km.t[k * 64:(k + 1) * 64], qk.t[k * 64:(k + 1) * 64, 6:12, :]), reads=qk.r, writes=km.r)
        psY = self.pslong
        ci = 0
        SA = os.environ.get("KDEBUG_SA", "")
        if SA == "a":
            return
        for b in range(4):
            pST = self.ps()
            pSN = self.ps()
            vtiles = {}
            for g, (win, dil) in enumerate(C_PAIRS):
                cin = d[("c128", "c512", "c2048")[g]]
                ncls = 1 if g == 0 else 4
                for cls in range(ncls):
                    kf = kcf[ci % 2]; kb = kcb[ci % 3]; kT = kTc[ci % 2]
                    ci += 1
                    if g == 0:
                        src = cin[b, :, :]
                    else:
                        src = cin[b].rearrange("(i c) f -> c i f", c=dil)[cls]
                    S.dma("sp", out=kf.t, in_=src, writes=kf.r)
                    S.op("dve", lambda e, kf=kf, kb=kb: e.tensor_copy(kb.t, kf.t), reads=kf.r, writes=kb.r)
                    pp = self.pbf()
                    for c in range(2):
                        S.op("pe", lambda e, pp=pp, kb=kb, c=c: e.transpose(pp.t[:, c * 128:(c + 1) * 128], kb.t[:, c * 128:(c + 1) * 128], self.identb.t[:, :]),
                             reads=kb.r + self.identb.r, writes=pp.r, signal=(c == 1))
                    S.op("act", lambda e, pp=pp, kT=kT: e.copy(kT.t.rearrange("p c q -> p (c q)"), pp.t[:, 0:256]), reads=pp.r, writes=kT.r)
                    for h in ([] if SA == "b1" else range(4)):
                        c, pb0 = h // 2, (h % 2) * 64
                        if g == 0:
                            outap = pST.t[:, 0:48].rearrange("p (g h t) -> p g h t", g=3, h=4)[:, 0, h, 0:4]
                            rhs = qk.t[pb0:pb0 + 64, c, b * 4:b * 4 + 4]
                        else:
                            outap = pST.t[:, 0:48].rearrange("p (g h t) -> p g h t", g=3, h=4)[:, g, h, cls:cls + 1]
                            rhs = qk.t[pb0:pb0 + 64, g * 2 + c, b * 4 + cls:b * 4 + cls + 1]
                        S.op("pe", lambda e, outap=outap, kT=kT, c=c, pb0=pb0, rhs=rhs: e.matmul(outap, lhsT=kT.t[pb0:pb0 + 64, c, :], rhs=rhs, start=True, stop=True),
                             reads=kT.r + qk.r, writes=pST.r)
                    vtiles[(g, cls)] = kb
                for h in ([] if SA in ("b1", "b2") else range(4)):
                    c, pb0 = h // 2, (h % 2) * 64
                    outap = pSN.t[0:4, 0:48].rearrange("p (g h t) -> p g h t", g=3, h=4)[:, g, h, 0:4]
                    km = self.kmask[h % 2]
                    S.op("pe", lambda e, outap=outap, g=g, c=c, km=km, b=b: e.matmul(
                        outap, lhsT=km.t[:, g * 2 + c, b * 4:b * 4 + 4], rhs=qk.t[:, g * 2 + c, b * 4:b * 4 + 4], start=True, stop=True),
                         reads=qk.r + km.r, writes=pSN.r)
            if SA in ("b", "b1", "b2"):
                continue
            S.op("act", lambda e, pST=pST: e.copy(STs.t.rearrange("p g h t -> p (g h t)"), pST.t[:, 0:48]), reads=pST.r, writes=STs.r)
            S.op("act", lambda e, pSN=pSN: e.copy(SNs.t[0:4].rearrange("p g h t -> p (g h t)"), pSN.t[0:4, 0:48]), reads=pSN.r, writes=SNs.r)
            pq = self.ps()
            for g in range(3):
                S.op("pe", lambda e, pq=pq, g=g: e.transpose(pq.t[0:16, g * 132:g * 132 + 128], STs.t[:, g, :, :].rearrange("p h t -> p (h t)"), self.ident.t[:, :]),
                     reads=STs.r + self.ident.r, writes=pq.r)
                S.op("pe", lambda e, pq=pq, g=g: e.transpose(pq.t[0:16, g * 132 + 128:g * 132 + 132], SNs.t[0:4, g, :, :].rearrange("p h t -> p (h t)"), self.ident.t[0:4, 0:4]),
                     reads=SNs.r + self.ident.r, writes=pq.r)
            S.op("dve", lambda e, pq=pq: e.tensor_tensor(scs.t[0:16].rearrange("p g k -> p (g k)"), pq.t[0:16, 0:396], biasS.t[0:16].rearrange("p g k -> p (g k)"), ALU.add),
                 reads=pq.r + biasS.r, writes=scs.r)
            S.op("dve", lambda e: e.tensor_reduce(stt.t[0:16, 0:1], scs.t[0:16].rearrange("p g k -> p (g k)"), AX.X, ALU.max), reads=scs.r, writes=stt.r)
            S.op("dve", lambda e: e.tensor_scalar(stt.t[0:16, 1:2], stt.t[0:16, 0:1], -1.0, None, ALU.mult), reads=stt.r, writes=stt.r)
            S.op("act", lambda e: e.activation(scs.t[0:16].rearrange("p g k -> p (g k)"), scs.t[0:16].rearrange("p g k -> p (g k)"), AF.Exp,
                                               bias=stt.t[0:16, 1:2], scale=1.0, accum_out=stt.t[0:16, 2:3]), reads=scs.r + stt.r, writes=scs.r + stt.r)
            S.op("dve", lambda e: e.reciprocal(stt.t[0:16, 3:4], stt.t[0:16, 2:3]), reads=stt.r, writes=stt.r)
            S.op("dve", lambda e: e.tensor_scalar(pn.t[0:16].rearrange("p g k -> p (g k)"), scs.t[0:16].rearrange("p g k -> p (g k)"), stt.t[0:16, 3:4], None, ALU.mult),
                 reads=scs.r + stt.r, writes=pn.r)
            if SA == "c":
                continue
            pp = self.pbf()
            for g in range(3):
                S.op("pe", lambda e, pp=pp, g=g: e.transpose(pp.t[:, g * 16:(g + 1) * 16], pn.t[0:16, g, 0:128], self.identb.t[0:16, 0:16]),
                     reads=pn.r + self.identb.r, writes=pp.r)
                S.op("pe", lambda e, pp=pp, g=g: e.transpose(pp.t[0:4, 64 + g * 16:64 + (g + 1) * 16], pn.t[0:16, g, 128:132], self.identb.t[0:16, 0:16]),
                     reads=pn.r + self.identb.r, writes=pp.r)
            S.op("act", lambda e, pp=pp: e.copy(pTs.t.rearrange("p g q -> p (g q)"), pp.t[:, 0:48]), reads=pp.r, writes=pTs.r)
            S.op("act", lambda e, pp=pp: e.copy(pTn.t[0:4].rearrange("p g q -> p (g q)"), pp.t[0:4, 64:112]), reads=pp.r, writes=pTn.r)
            if SA == "d":
                continue
            for h in range(4):
                c, pb0 = h // 2, (h % 2) * 64
                first = True
                for g, (win, dil) in enumerate(C_PAIRS):
                    outap = psY.t[pb0:pb0 + 64, c * 16 + b * 4:c * 16 + b * 4 + 4]
                    S.op("pe", lambda e, outap=outap, g=g, h=h, b=b, first=first: e.matmul(
                        outap, lhsT=kvnb.t[0:4, b, g, h * 64:(h + 1) * 64], rhs=pTn.t[0:4, g, h * 4:(h + 1) * 4],
                        start=first, stop=False, skip_group_check=True),
                         reads=kvnb.r + pTn.r, writes=psY.r)
                    first = False
            for g, (win, dil) in enumerate(C_PAIRS):
                cin = d[("c128", "c512", "c2048")[g]]
                ncls = 1 if g == 0 else 4
                for cls in range(ncls):
                    kf = kcf[ci % 2]; kb = kcb[ci % 3]
                    ci += 1
                    if g == 0:
                        src = cin[b, :, 256:512]
                    else:
                        src = cin[b].rearrange("(i c) f -> c i f", c=dil)[cls][:, 256:512]
                    S.dma("sp", out=kf.t[:, 0:256], in_=src, writes=kf.r)
                    S.op("dve", lambda e, kf=kf, kb=kb: e.tensor_copy(kb.t[:, 0:256], kf.t[:, 0:256]), reads=kf.r, writes=kb.r)
                    for h in range(4):
                        c, pb0 = h // 2, (h % 2) * 64
                        if g == 0:
                            outap = psY.t[pb0:pb0 + 64, c * 16 + b * 4:c * 16 + b * 4 + 4]
                            rhs = pTs.t[:, 0, h * 4:(h + 1) * 4]
                        else:
                            outap = psY.t[pb0:pb0 + 64, c * 16 + b * 4 + cls:c * 16 + b * 4 + cls + 1]
                            rhs = pTs.t[:, g, h * 4 + cls:h * 4 + cls + 1]
                        S.op("pe", lambda e, outap=outap, kb=kb, h=h, rhs=rhs: e.matmul(outap, lhsT=kb.t[:, h * 64:(h + 1) * 64], rhs=rhs,
                                                                                  start=False, stop=False, skip_group_check=True),
                             reads=kb.r + pTs.r, writes=psY.r)
        if SA:
            return
        for c in range(2):
            S.op("act", lambda e, c=c: e.copy(cat.t[:, c, :], psY.t[:, c * 16:(c + 1) * 16]), reads=psY.r, writes=[cat.r[c]])

    def l1_p3_sample(self):
        o = self.dout
        Bs = self.TB["s"]
        seqcols = [(1 + s, 4 * s, 4) for s in range(4)]
        self.l1_tail(Bs, seqcols, True, lambda: self.store_x_tokmajor(Bs, o["ys"]), prompt=False)


def build_program(cfg):
    b = Builder(cfg)
    return b


_CACHE = {}


def get_nc(cfg_key):
    if cfg_key not in _CACHE:
        cfg = Cfg(*cfg_key)
        b = Builder(cfg)
        b.sstage = None
        nc = build_with_stages(b)
        _CACHE[cfg_key] = (nc, b)
    return _CACHE[cfg_key]


def build_with_stages(b):
    b.sstage = None
    b.sstage2 = None
    b.cpstg = [b.sb("cpstg%d" % k, [128, 128], F32) for k in range(2)]
    b.cpi = 0
    return b.build()


def make_oh():
    oh = np.zeros((3, 33, 384), np.float32)
    for g, (win, dil) in enumerate(C_PAIRS):
        taps = win // dil
        buckets = t5_bucket(dil * np.arange(taps + 1))
        for c in range(384):
            j = 128 - c
            if 0 <= j <= taps:
                oh[g, buckets[j], c] = 1.0
            else:
                oh[g, 32, c] = NEG
    return oh


def make_sample_consts():
    ohs = np.zeros((3, 128, 132), np.float32)
    masks = np.zeros((16, 3, 132), np.float32)
    dmask = np.zeros((128, 4), np.float32)
    for t in range(4):
        dmask[t * 32:(t + 1) * 32, t] = 1.0
    for g, (win, dil) in enumerate(C_PAIRS):
        taps = win // dil
        buckets = t5_bucket(dil * np.arange(taps + 1))
        for t in range(4):
            for col in range(132):
                if col < 128:
                    if g == 0:
                        valid, tap = col >= t, 128 + t - col
                    else:
                        valid, tap = True, 128 - col
                else:
                    j = col - 128
                    if g == 0:
                        valid, tap = j <= t, t - j
                    else:
                        valid, tap = j == t, 0
                if valid:
                    ohs[g, t * 32 + buckets[tap], col] = 1.0
                else:
                    for h in range(4):
                        masks[h * 4 + t, g, col] = NEG
    return ohs, dmask, masks


def make_in_maps(inputs, cfg, n_cores=8):
    f = lambda a: np.ascontiguousarray(np.asarray(a, dtype=np.float32))
    I = {k: f(v) for k, v in inputs.items()}
    oh = make_oh()
    ohs, dmask, masks = make_sample_consts()
    maps = []
    for c in range(n_cores):
        b = c // 2
        ss = slice(4 * c, 4 * c + 4)
        m = {
            "xp": I["x_prompt"][b, :cfg.seq], "cp": I["c_prompt"][b:b + 1],
            "xs": I["x_sample"][ss].reshape(16, D), "cs": I["c_sample"][ss],
            "sa": I["state_a_conv"][0, ss], "sb": I["state_b_pool"][0, ss],
            "c128": I["cache_c_win128"][0, ss].reshape(4, 128, 512),
            "c512": I["cache_c_win512"][0, ss].reshape(4, 512, 512),
            "c2048": I["cache_c_win2048"][0, ss].reshape(4, 2048, 512),
            "sd": I["state_d_conv"][0, ss], "sf": I["state_ffn_conv"][:, ss],
            "ada_w": I["ada_w"], "ada_b": I["ada_b"], "norm_g": I["norm_g"], "rel_bias": I["rel_bias"],
            "ab_w_in": I["ab_w_in"][0], "a_conv_w": I["a_conv_w"][0], "b_w_grp": I["b_w_grp"][0],
            "b_scale": I["b_scale"][0], "ab_w_out": I["ab_w_out"][0], "cd_w_in": I["cd_w_in"][0],
            "d_conv_w": I["d_conv_w"][0], "d_conv_b": I["d_conv_b"][0], "d_ln_g": I["d_ln_g"][0],
            "d_ln_b": I["d_ln_b"][0], "cd_w_out": I["cd_w_out"][0], "ffn_w_up": I["ffn_w_up"],
            "ffn_conv_w": I["ffn_conv_w"], "ffn_conv_b": I["ffn_conv_b"], "ffn_w_down": I["ffn_w_down"],
            "oh": oh, "ohs": ohs, "dmask": dmask, "masks": masks,
        }
        maps.append({k: np.ascontiguousarray(v) for k, v in m.items()})
    return maps


def kernel(**inputs):
    cfg_key = (8, True, 2, False)
    nc, b = get_nc(cfg_key)
    cfg = b.cfg
    maps = make_in_maps(inputs, cfg)
    res = run_bass_kernel_spmd(nc, maps, core_ids=list(range(8)))
    R = res.results
    cat = lambda name, cores: np.stack([np.asarray(R[c][name]) for c in cores])
    even = [0, 2, 4, 6]
    allc = list(range(8))
    y_prompt = cat("yp", even)
    y_sample = np.concatenate([np.asarray(R[c]["ys"]).reshape(4, 4, D) for c in allc])
    a_p = cat("a_p", even)[None]
    a_s = np.concatenate([np.asarray(R[c]["a_s"]) for c in allc])[None]
    b_p = cat("b_p", even)[None]
    b_s = np.concatenate([np.asarray(R[c]["b_s"]) for c in allc])[None]
    outs = [y_prompt, y_sample, a_p, a_s, b_p, b_s]
    for g, wb in enumerate((128, 512, 2048)):
        cp = cat("c%d_p" % g, even).reshape(1, 4, wb, 2, 4, 64)
        cs = np.concatenate([np.asarray(R[c]["c%d_s" % g]) for c in allc]).reshape(1, 32, wb, 2, 4, 64)
        outs += [cp, cs]
    d_p = cat("d_p", even)[None]
    d_s = np.concatenate([np.asarray(R[c]["d_s"]) for c in allc])[None]
    f_p = np.stack([np.asarray(R[c]["f_p"]) for c in even], axis=1)
    f_s = np.concatenate([np.asarray(R[c]["f_s"]) for c in allc], axis=1)
    outs += [d_p, d_s, f_p, f_s]
    return tuple(np.ascontiguousarray(x, dtype=np.float32) for x in outs)
```

```python
import math
import os
import types
from contextlib import ExitStack
import numpy as np
import concourse.bass as bass
import concourse.mybir as mybir
from concourse.bass_utils import run_bass_kernel_spmd

F32 = mybir.dt.float32
BF16 = mybir.dt.bfloat16
AF = mybir.ActivationFunctionType
ALU = mybir.AluOpType
AX = mybir.AxisListType

D = 1024
DFF = 2816
SEQ = 4096
NT = 512
EPS = 1e-6
NEG = -30000.0


class Res:
    __slots__ = ("name", "lw", "rd", "psum")

    def __init__(self, name=""):
        self.name = name
        self.psum = False
        self.lw = None
        self.rd = {}


def _freeze(fn):
    if fn.__closure__ is None:
        return fn
    cells = []
    for c in fn.__closure__:
        try:
            cells.append(types.CellType(c.cell_contents))
        except ValueError:
            cells.append(c)
    return types.FunctionType(fn.__code__, fn.__globals__, fn.__name__, fn.__defaults__, tuple(cells))


class Sched:
    ENG = ("pe", "act", "dve", "pool", "sp")

    def __init__(self, nc, es, n_dma_sems=48):
        self.nc = nc
        self.ops = {e: [] for e in self.ENG}
        self.cnt = {e: 0 for e in self.ENG}
        self.sig = {e: 0 for e in self.ENG}
        self.seen = {e: {} for e in self.ENG}
        self.semh = {}
        for e in self.ENG:
            self.semh[e] = es.enter_context(nc.semaphore("sem_" + e))
        self.ndma = n_dma_sems
        for i in range(n_dma_sems):
            self.semh[("d", i)] = es.enter_context(nc.semaphore("semd%d" % i))
        self.dval = [0] * n_dma_sems
        self.dnext = 0
        self.dnext_pool = 0
        self.out_tokens = []

    def _wait(self, e, tok):
        sem, val = tok
        if self.seen[e].get(sem, 0) >= val:
            return
        self.seen[e][sem] = val
        h = self.semh[sem]
        self.ops[e].append(lambda eng, h=h, v=val: eng.wait_ge(h, v))

    def _deps(self, e, reads, writes):
        toks = {}
        for r in reads:
            if r.lw is not None:
                s, v = r.lw
                toks[s] = max(toks.get(s, 0), v)
        for w in writes:
            if w.lw is not None:
                s, v = w.lw
                toks[s] = max(toks.get(s, 0), v)
            for s, v in w.rd.items():
                toks[s] = max(toks.get(s, 0), v)
        for s, v in toks.items():
            if s == e and e == "pe":
                continue
            self._wait(e, (s, v))

    def _commit(self, tok, reads, writes):
        for w in writes:
            w.lw = tok
            w.rd = {}
        for r in reads:
            if r in writes:
                continue
            s, v = tok
            if r.rd.get(s, 0) < v:
                r.rd[s] = v

    def op(self, e, fn, reads=(), writes=(), signal=True):
        fn = _freeze(fn)
        signal = True
        reads, writes = list(reads), list(writes)
        for r in reads:
            if r.psum and r not in writes:
                writes.append(r)
        self._deps(e, reads, writes)
        self.cnt[e] += 1
        c = self.cnt[e]
        if signal:
            inc = c - self.sig[e]
            self.sig[e] = c
            h = self.semh[e]
            self.ops[e].append(lambda eng, fn=fn, h=h, inc=inc: fn(eng).then_inc(h, inc))
        else:
            self.ops[e].append(lambda eng, fn=fn: fn(eng))
        tok = (e, c)
        self._commit(tok, reads, writes)
        return tok

    def dma(self, q, out, in_, reads=(), writes=(), is_output=False, **kw):
        self._deps(q, reads, writes)
        half = self.ndma // 2
        if q == "pool":
            i = self.dnext_pool
            self.dnext_pool = (i + 1) % half
        else:
            i = half + self.dnext
            self.dnext = (self.dnext + 1) % (self.ndma - half)
        key = ("d", i)
        prev = self.dval[i]
        if prev > 0:
            self._wait(q, (key, prev))
        self.dval[i] = prev + 16
        tok = (key, prev + 16)
        h = self.semh[key]
        self.ops[q].append(lambda eng, o=out, a=in_, h=h, kw=kw: eng.dma_start(out=o, in_=a, **kw).then_inc(h, 16))
        self._commit(tok, reads, writes)
        if is_output:
            self.out_tokens.append(tok)
        return tok

    def barrier(self):
        toks = []
        for e in ("pe", "act", "dve", "pool"):
            if self.sig[e] > 0:
                toks.append((e, self.sig[e]))
        for i in range(self.ndma):
            if self.dval[i] > 0:
                toks.append((("d", i), self.dval[i]))
        for e in self.ENG:
            for t in toks:
                if t[0] != e:
                    self._wait(e, t)

    def finish(self):
        for i in range(self.ndma):
            if self.dval[i] > 0:
                self._wait("sp", (("d", i), self.dval[i]))
        for e in ("pe", "act", "dve", "pool"):
            if self.sig[e] > 0:
                self._wait("sp", (e, self.sig[e]))

    def emit(self, block):
        nc = self.nc
        ops = self.ops

        @block.tensor
        def _(eng):
            for f in ops["pe"]:
                f(eng)

        @block.scalar
        def _(eng):
            for f in ops["act"]:
                f(eng)

        @block.vector
        def _(eng):
            for f in ops["dve"]:
                f(eng)

        @block.gpsimd
        def _(eng):
            for f in ops["pool"]:
                f(eng)

        @block.sync
        def _(eng):
            for f in ops["sp"]:
                f(eng)


class Buf:
    def __init__(self, t, nres=1, name=""):
        self.t = t
        self.r = [Res("%s[%d]" % (name, i)) for i in range(nres)]

    @property
    def all(self):
        return self.r


class Cfg:
    def __init__(self, ntiles=8, do_sample=True, depth=2, debug_x1=False):
        self.ntiles = ntiles
        self.seq = ntiles * NT
        self.do_sample = do_sample
        self.depth = depth
        self.debug_x1 = debug_x1


def t5_bucket(dist):
    dist = np.asarray(dist)
    max_exact = 16
    large = max_exact + (np.log(np.maximum(dist, max_exact) / max_exact) / np.log(2048 / max_exact)
                         * (32 - max_exact)).astype(np.int32)
    large = np.minimum(large, 31)
    return np.where(dist < max_exact, dist, large).astype(np.int32)


C_PAIRS = ((128, 1), (512, 4), (2048, 16))


class Builder:
    def __init__(self, cfg):
        self.cfg = cfg
        self.nc = bass.Bass("TRN2", target_bir_lowering=False)
        self.es = ExitStack()
        self.S = None
        self.din = {}
        self.dout = {}
        self.scr = {}

    def inp(self, name, shape, dt=F32):
        t = self.nc.dram_tensor(name, list(shape), dt, kind="ExternalInput").ap()
        self.din[name] = t
        return t

    def outp(self, name, shape, dt=F32):
        t = self.nc.dram_tensor(name, list(shape), dt, kind="ExternalOutput").ap()
        self.dout[name] = t
        return t

    def scratch(self, name, shape, dt):
        t = self.nc.dram_tensor(name, list(shape), dt, kind="Internal").ap()
        self.scr[name] = (t, Res(name))
        return t

    def sb(self, name, shape, dt=F32, nres=1):
        t = self.es.enter_context(self.nc.sbuf_tensor(name, list(shape), dt))
        return Buf(t, nres, name)

    def psb(self, name, shape, dt=F32):
        t = self.es.enter_context(self.nc.psum_tensor(name, list(shape), dt))
        b = Buf(t, 1, name)
        b.r[0].psum = True
        return b

    def ps(self):
        b = self.psum[self.psi]
        self.psi = (self.psi + 1) % len(self.psum)
        return b

    def wslot(self):
        b = self.wsl[self.wsi]
        self.wsi = (self.wsi + 1) % len(self.wsl)
        return b

    def declare_io(self):
        cfg = self.cfg
        i = self.inp
        i("xp", [cfg.seq, D])
        i("cp", [1, D])
        i("xs", [16, D])
        i("cs", [4, D])
        i("sa", [4, 2, 512])
        i("sb", [4, 15, 512])
        i("c128", [4, 128, 512])
        i("c512", [4, 512, 512])
        i("c2048", [4, 2048, 512])
        i("sd", [4, 30, 512])
        i("sf", [2, 4, 2, 2 * DFF])
        i("ada_w", [2, D, 6 * D])
        i("ada_b", [2, 6 * D])
        i("norm_g", [2, 4, D])
        i("rel_bias", [32, 12])
        i("ab_w_in", [D, 2048])
        i("a_conv_w", [3, 512])
        i("b_w_grp", [4, 128, 128])
        i("b_scale", [512])
        i("ab_w_out", [D, D])
        i("cd_w_in", [D, 3328])
        i("d_conv_w", [31, 512])
        i("d_conv_b", [512])
        i("d_ln_g", [512])
        i("d_ln_b", [512])
        i("cd_w_out", [768, D])
        i("ffn_w_up", [2, D, 2 * DFF])
        i("ffn_conv_w", [2, 3, 2 * DFF])
        i("ffn_conv_b", [2, 2 * DFF])
        i("ffn_w_down", [2, DFF, D])
        i("oh", [3, 33, 384])
        i("ohs", [3, 128, 132])
        i("dmask", [128, 4])
        i("masks", [16, 3, 132])
        o = self.outp
        o("yp", [cfg.seq, D])
        o("ys", [16, D])
        o("a_p", [2, 512])
        o("a_s", [4, 2, 512])
        o("b_p", [15, 512])
        o("b_s", [4, 15, 512])
        o("c0_p", [min(128, cfg.seq), 512])
        o("c0_s", [4, 128, 512])
        o("c1_p", [min(512, cfg.seq), 512])
        o("c1_s", [4, 512, 512])
        o("c2_p", [min(2048, cfg.seq), 512])
        o("c2_s", [4, 2048, 512])
        o("d_p", [30, 512])
        o("d_s", [4, 30, 512])
        o("f_p", [2, 2, 2 * DFF])
        o("f_s", [2, 4, 2, 2 * DFF])

    def build(self):
        cfg = self.cfg
        nc = self.nc
        es = self.es
        self.declare_io()
        S = self.S = Sched(nc, es)
        self.ident = self.sb("ident", [128, 128], F32)
        self.identb = self.sb("identb", [128, 128], BF16)
        self.onesb = self.sb("onesb", [128, 128], BF16)
        self.onesf = self.sb("onesf", [128, 128], F32)
        self.psum = [self.psb("ps%d" % k, [128, 512], F32) for k in range(5)]
        self.pslong = self.psb("pslong", [128, 512], F32)
        self.psi = 0
        self.psbf = [self.psb("psbf%d" % k, [128, 1024], BF16) for k in range(2)]
        self.psbi = 0
        self.wsl = [self.sb("wsl%d" % k, [128, 4096], BF16) for k in range(3)]
        self.wsi = 0
        self.vstage = self.sb("vstage", [128, 128], F32)
        ident, identb, onesb, onesf = self.ident, self.identb, self.onesb, self.onesf
        S.op("pool", lambda e: e.iota(ident.t[:], pattern=[[1, 128]], base=0, channel_multiplier=-1,
                                      allow_small_or_imprecise_dtypes=True), writes=ident.r)
        S.op("pool", lambda e: e.tensor_single_scalar(ident.t[:], ident.t[:], 0.0, ALU.is_equal),
             reads=ident.r, writes=ident.r)
        S.op("pool", lambda e: e.tensor_copy(identb.t[:], ident.t[:]), reads=ident.r, writes=identb.r)
        S.op("pool", lambda e: e.memset(onesb.t[:], 1.0), writes=onesb.r)
        S.op("pool", lambda e: e.memset(onesf.t[:], 1.0), writes=onesf.r)
        self.epsb = self.sb("epsb", [128, 2], F32)
        S.op("pool", lambda e: e.memset(self.epsb.t[:, 0:1], float(D * EPS)), writes=self.epsb.r)
        S.op("pool", lambda e: e.memset(self.epsb.t[:, 1:2], float(EPS)), writes=self.epsb.r)

        self.W = {}
        self.load_small()
        self.compute_mod()
        self.prep_weights(0)
        self.alloc_tile_bufs()
        for l in range(cfg.depth):
            if l == 0:
                self.layer0()
            else:
                self.layer1()
        S.finish()
        with nc.Block() as block:
            S.emit(block)
        es.close()
        return nc

    def cast_group(self, name, kc, ncols, blocks):
        t = self.scratch(name, [128, kc, ncols], BF16)
        res = self.scr[name][1]
        for src, c0 in blocks:
            w = src.shape[1]
            self.S.dma("pool", out=t[:, :, c0:c0 + w], in_=src.rearrange("(k p) c -> p k c", p=128),
                       writes=[res])
        return (t, res, kc, ncols)

    def prep_weights(self, which):
        d = self.din
        W = self.W
        cfg = self.cfg
        if which == 0:
            self.prep_l0(d, W)
            self.prep_ffn(d, W, 0)
        elif cfg.depth > 1:
            self.prep_l1(d, W)
            self.prep_ffn(d, W, 1)

    def prep_l0(self, d, W):
        wi = d["ab_w_in"]
        W["in0"] = [self.cast_group("w_in0_%d" % i, 8, 512,
                                    [(wi[:, (b * 4 + i) * 128:(b * 4 + i + 1) * 128], b * 128) for b in range(4)])
                    for i in range(4)]
        wo = d["ab_w_out"]
        W["out0"] = [self.cast_group("w_out0_%d" % g, 8, 512, [(wo[:, g * 512:(g + 1) * 512], 0)]) for g in range(2)]

    def prep_ffn(self, d, W, l):
        if True:
            up = d["ffn_w_up"][l]
            W["up%d" % l] = [self.cast_group("w_up%d_%d" % (l, j), 8, 512,
                                             [(up[:, (2 * j) * 128:(2 * j + 1) * 128], 0),
                                              (up[:, DFF + (2 * j) * 128:DFF + (2 * j + 1) * 128], 128),
                                              (up[:, (2 * j + 1) * 128:(2 * j + 2) * 128], 256),
                                              (up[:, DFF + (2 * j + 1) * 128:DFF + (2 * j + 2) * 128], 384)])
                             for j in range(11)]
            dn = d["ffn_w_down"][l]
            W["down%d" % l] = [self.cast_group("w_dn%d_%d" % (l, g), 22, 128, [(dn[:, g * 128:(g + 1) * 128], 0)])
                               for g in range(8)]

    def prep_l1(self, d, W):
        if True:
            ci = d["cd_w_in"]
            W["qk1"] = [self.cast_group("w_qk1_%d" % g, 8, 512, [(ci[:, g * 512:(g + 1) * 512], 0)]) for g in range(3)]
            W["kv1"] = [self.cast_group("w_kv1_%d" % g, 8, 512,
                                        [(ci[:, 768 + g * 256:768 + (g + 1) * 256], 0),
                                         (ci[:, 1536 + g * 256:1536 + (g + 1) * 256], 256)]) for g in range(3)]
            W["d1"] = [self.cast_group("w_d1_%d" % g, 8, 512,
                                       [(ci[:, 2304 + (2 * g) * 128:2304 + (2 * g + 1) * 128], 0),
                                        (ci[:, 2816 + (2 * g) * 128:2816 + (2 * g + 1) * 128], 128),
                                        (ci[:, 2304 + (2 * g + 1) * 128:2304 + (2 * g + 2) * 128], 256),
                                        (ci[:, 2816 + (2 * g + 1) * 128:2816 + (2 * g + 2) * 128], 384)])
                       for g in range(2)]
            co = d["cd_w_out"]
            W["out1"] = [self.cast_group("w_out1_%d" % g, 6, 512, [(co[:, g * 512:(g + 1) * 512], 0)]) for g in range(2)]

    def load_w(self, grp):
        t, res, kc, ncols = grp
        sl = self.wslot()
        self.S.dma("sp", out=sl.t[:, 0:kc * ncols], in_=t.rearrange("p k c -> p (k c)"), reads=[res], writes=sl.r)
        return sl, sl.t[:, 0:kc * ncols].rearrange("p (k c) -> p k c", k=kc)

    def load_vec_T(self, src_rows, dst_ap, dst_res, R):
        S = self.S
        vs = self.vstage
        S.dma("sp", out=vs.t[0:R, :], in_=src_rows, writes=vs.r)
        p = self.ps()
        S.op("pe", lambda e: e.transpose(p.t[:, 0:R], vs.t[0:R, :], self.ident.t[0:R, 0:R]),
             reads=vs.r + self.ident.r, writes=p.r)
        S.op("dve", lambda e: e.tensor_copy(dst_ap, p.t[:, 0:R]), reads=p.r, writes=dst_res)

    def load_small(self):
        d = self.din
        cfg = self.cfg
        V = self.V = {}

        def vec(name, src, R):
            b = self.sb("v_" + name, [128, R], F32)
            for r0 in range(0, R, 128):
                rr = min(128, R - r0)
                self.load_vec_T(src[r0:r0 + rr, :], b.t[:, r0:r0 + rr], b.r, rr)
            V[name] = b
            return b

        for l in range(cfg.depth):
            vec("ada_b%d" % l, d["ada_b"][l].rearrange("(r p) -> r p", p=128), 48)
            vec("ng%d" % l, d["norm_g"][l].rearrange("k (r p) -> (k r) p", p=128), 32)
            vec("fcw%d" % l, d["ffn_conv_w"][l].rearrange("k (r p) -> (k r) p", p=128), 132)
            vec("fcb%d" % l, d["ffn_conv_b"][l].rearrange("(r p) -> r p", p=128), 44)
        vec("acw", d["a_conv_w"].rearrange("k (r p) -> (k r) p", p=128), 12)
        vec("bsc", d["b_scale"].rearrange("(r p) -> r p", p=128), 4)
        if cfg.depth > 1:
            vec("dcw", d["d_conv_w"].rearrange("k (r p) -> (k r) p", p=128), 124)
            vec("dcb", d["d_conv_b"].rearrange("(r p) -> r p", p=128), 4)
            vec("dlg", d["d_ln_g"].rearrange("(r p) -> r p", p=128), 4)
            vec("dlb", d["d_ln_b"].rearrange("(r p) -> r p", p=128), 4)
        self.bw = self.sb("bw", [128, 4, 128], BF16)
        self.S.dma("pool", out=self.bw.t[:], in_=d["b_w_grp"].rearrange("g c d -> c g d"), writes=self.bw.r)

    def compute_mod(self):
        S = self.S
        d = self.din
        cfg = self.cfg
        NS = 5
        crow = self.sb("crow", [8, D], F32)
        S.dma("sp", out=crow.t[0:1, :], in_=d["cp"], writes=crow.r)
        S.dma("sp", out=crow.t[1:5, :], in_=d["cs"], writes=crow.r)
        cT = self.sb("cT", [128, 8, NS], BF16)
        p = self.ps()
        for kc in range(8):
            S.op("pe", lambda e, kc=kc: e.transpose(p.t[:, kc * NS:(kc + 1) * NS], crow.t[0:NS, kc * 128:(kc + 1) * 128],
                                                    self.ident.t[0:NS, 0:NS]),
                 reads=crow.r + self.ident.r, writes=p.r)
        S.op("act", lambda e: e.activation(cT.t[:].rearrange("p k s -> p (k s)"), p.t[:, 0:8 * NS], AF.Silu),
             reads=p.r, writes=cT.r)
        self.modp = []
        for l in range(cfg.depth):
            mod = self.sb("mod%d" % l, [128, 48, NS], F32)
            pm = self.pslong
            aw = d["ada_w"][l]
            for g in range(12):
                sl = self.wslot()
                S.dma("pool", out=sl.t[:, 0:4096].rearrange("p (k c) -> p k c", k=8),
                      in_=aw[:, g * 512:(g + 1) * 512].rearrange("(k p) c -> p k c", p=128), writes=sl.r)
                wv = sl.t[:, 0:4096].rearrange("p (k c) -> p k c", k=8)
                for c4 in range(4):
                    oc = g * 4 + c4
                    for kc in range(8):
                        S.op("pe", lambda e, oc=oc, kc=kc, c4=c4, wv=wv: e.matmul(
                            pm.t[:, oc * NS:(oc + 1) * NS], lhsT=wv[:, kc, c4 * 128:(c4 + 1) * 128], rhs=cT.t[:, kc, :],
                            start=(kc == 0), stop=(kc == 7)),
                             reads=sl.r + cT.r, writes=pm.r, signal=(kc == 7))
            ab = self.V["ada_b%d" % l]
            for s in range(NS):
                S.op("dve", lambda e, s=s: e.tensor_tensor(mod.t[:, :, s], pm.t[:, 0:48 * NS].rearrange("p (c s) -> p c s", s=NS)[:, :, s],
                                                          ab.t[:, :], ALU.add),
                     reads=pm.r + ab.r, writes=mod.r)
            ng = self.V["ng%d" % l]
            mp = {}
            for nm in ("A1", "G1", "A2", "G2"):
                mp[nm] = self.sb("mp%d%s" % (l, nm), [128, 8, NS], F32)
            for s in range(NS):
                S.op("dve", lambda e, s=s: e.scalar_tensor_tensor(mp["A1"].t[:, :, s], mod.t[:, 8:16, s], 1.0, ng.t[:, 0:8], ALU.add, ALU.mult),
                     reads=mod.r + ng.r, writes=mp["A1"].r)
                S.op("dve", lambda e, s=s: e.scalar_tensor_tensor(mp["A2"].t[:, :, s], mod.t[:, 32:40, s], 1.0, ng.t[:, 16:24], ALU.add, ALU.mult),
                     reads=mod.r + ng.r, writes=mp["A2"].r)
                S.op("dve", lambda e, s=s: e.tensor_tensor(mp["G1"].t[:, :, s], mod.t[:, 16:24, s], ng.t[:, 8:16], ALU.mult),
                     reads=mod.r + ng.r, writes=mp["G1"].r)
                S.op("dve", lambda e, s=s: e.tensor_tensor(mp["G2"].t[:, :, s], mod.t[:, 40:48, s], ng.t[:, 24:32], ALU.mult),
                     reads=mod.r + ng.r, writes=mp["G2"].r)
            for nm in ("A1", "G1", "A2", "G2"):
                S.op("dve", lambda e, nm=nm: e.tensor_scalar(mp[nm].t[:], mp[nm].t[:], 32.0, None, ALU.mult),
                     reads=mp[nm].r, writes=mp[nm].r)
            mp["mod"] = mod
            self.modp.append(mp)

    def alloc_tile_bufs(self):
        self.TB = {}
        for kind, nseg, sl in (("p", 1, NT), ("s", 4, 4)):
            if kind == "s" and not self.cfg.do_sample:
                continue
            n = nseg * sl
            B = {}
            B["nseg"], B["sl"], B["n"] = nseg, sl, n
            if kind == "p":
                B["xtok"] = self.sb(kind + "xtok", [128, max(1, n // 128), D], F32)
            else:
                B["xtok"] = Buf(self.TB["p"]["xtok"].t[:, 3:4, :], 1, "sxtok")
                B["xtok"].r = self.TB["p"]["xtok"].r
            B["xT"] = self.sb(kind + "xT", [128, 8, n], F32, nres=8)
            B["hT"] = self.sb(kind + "hT", [128, 8, n], BF16)
            B["rstd"] = self.sb(kind + "rstd", [128, n], F32)
            B["tmp"] = [self.sb(kind + "tmp%d" % k, [128, n], F32) for k in range(4)]
            B["tmpi"] = 0
            B["y"] = self.sb(kind + "y", [128, 8, n], F32, nres=8)
            B["cat"] = self.sb(kind + "cat", [128, 8, n], BF16, nres=8)
            B["act"] = self.sb(kind + "act", [128, 22, n], BF16, nres=22)
            B["sq"] = Buf(B["act"].t, 1, kind + "sq")
            B["sq"].r = B["act"].r[0:8]
            B["pbuf"] = [self.sb(kind + "pbuf%d" % i, [128, nseg, 2 + sl], F32) for i in range(4)]
            B["ud"] = [self.sb(kind + "ud%d" % i, [128, nseg, 15 + sl], F32) for i in range(4)]
            B["ubuf"] = [Buf(u.t[:, :, 0:15 + sl], 1, "ub") for u in B["ud"]]
            for u, v in zip(B["ubuf"], B["ud"]):
                u.r = v.r
            B["sbuf"] = [self.sb(kind + "sbuf%d" % i, [128, nseg, 15 + sl], F32) for i in range(2)]
            B["pool"] = self.sb(kind + "pool", [128, 4, n], BF16, nres=4)
            B["fbuf"] = [self.sb(kind + "fbuf%d" % i, [128, nseg, 2 + sl], F32) for i in range(4)]
            B["fhalo"] = [self.sb(kind + "fhalo%d" % l, [128, 44, nseg, 2], F32, nres=44) for l in range(self.cfg.depth)]
            B["ostg"] = B["xtok"]
            self.TB[kind] = B
        xk = self.TB["p"]["xtok"]
        Bp_ = self.TB["p"]
        Bp_["ffn_xt"] = [Buf(xk.t[:, k // 2, (k % 2) * 512:(k % 2) * 512 + 512], 1, "xtmp%d" % k) for k in range(4)]
        flat = xk.t[:, 2:4, :].rearrange("p a b -> p (a b)")
        Bp_["ffn_xf"] = [Buf(flat[:, k * 514:(k + 1) * 514].rearrange("p (s c) -> p s c", s=1), 1, "xfb%d" % k) for k in range(3)]
        self.sstage = Buf(xk.t[:, 0, :], 1, "sstage"); self.sstage.r = xk.r
        self.sstage2 = Buf(xk.t[:, 1, 0:512], 1, "sstage2"); self.sstage2.r = xk.r
        self.rowstg = Buf(xk.t[:, 2, 0:512], 1, "rowstg"); self.rowstg.r = xk.r
        self.rcnt = self.sb("rcnt", [128, 4, 16], F32)
        S = self.S
        S.op("pool", lambda e: e.iota(self.rcnt.t[:, 0, :], pattern=[[1, 16]], base=1, channel_multiplier=0,
                                      allow_small_or_imprecise_dtypes=True), writes=self.rcnt.r)
        for g in range(1, 4):
            S.op("pool", lambda e, g=g: e.tensor_copy(self.rcnt.t[:, g, :], self.rcnt.t[:, 0, :]),
                 reads=self.rcnt.r, writes=self.rcnt.r)
        for g in range(4):
            S.op("pool", lambda e, g=g: e.tensor_scalar(self.rcnt.t[:, g, :], self.rcnt.t[:, g, :], float(2 ** (g + 1)), None, ALU.min),
                 reads=self.rcnt.r, writes=self.rcnt.r)
        S.op("dve", lambda e: e.reciprocal(self.rcnt.t[:], self.rcnt.t[:]), reads=self.rcnt.r, writes=self.rcnt.r)

    def tmp(self, B):
        t = B["tmp"][B["tmpi"]]
        B["tmpi"] = (B["tmpi"] + 1) % len(B["tmp"])
        return t

    def load_x_tokmajor(self, B, src_rows):
        S = self.S
        n = B["n"]
        xt = B["xtok"]
        xT = B["xT"]
        if n >= 128:
            S.dma("sp", out=xt.t[:, :, :], in_=src_rows.rearrange("(s p) f -> p s f", p=128), writes=xt.r)
            ns, rows = n // 128, 128
        else:
            S.dma("sp", out=xt.t[0:n, 0, :], in_=src_rows, writes=xt.r)
            ns, rows = 1, n
        for fc in range(8):
            p = self.ps()
            for s in range(ns):
                S.op("pe", lambda e, fc=fc, s=s, p=p: e.transpose(p.t[:, s * rows:(s + 1) * rows], xt.t[0:rows, s, fc * 128:(fc + 1) * 128],
                                                                  self.ident.t[0:rows, 0:rows]),
                     reads=xt.r + self.ident.r, writes=p.r, signal=(s == ns - 1))
            S.op("act", lambda e, fc=fc, p=p: e.copy(xT.t[:, fc, :], p.t[:, 0:n]), reads=p.r, writes=[xT.r[fc]])

    def store_x_tokmajor(self, B, dst_rows):
        S = self.S
        n = B["n"]
        xT = B["xT"]
        og = B["ostg"]
        ns, rows = (n // 128, 128) if n >= 128 else (1, n)
        for s in range(ns):
            for half in range(2):
                p = self.ps()
                for c4 in range(4):
                    fc = half * 4 + c4
                    S.op("pe", lambda e, fc=fc, s=s, c4=c4, p=p: e.transpose(p.t[0:rows, c4 * 128:(c4 + 1) * 128],
                                                                             xT.t[:, fc, s * rows:(s + 1) * rows], self.ident.t[:, :]),
                         reads=[xT.r[fc]] + self.ident.r, writes=p.r, signal=(c4 == 3))
                S.op("act", lambda e, s=s, half=half, p=p: e.copy(og.t[0:rows, s, half * 512:(half + 1) * 512], p.t[0:rows, :]),
                     reads=p.r, writes=og.r)
        if n >= 128:
            S.dma("act", out=dst_rows.rearrange("(s p) f -> p s f", p=128), in_=og.t[:, :, :], reads=og.r, is_output=True)
        else:
            S.dma("act", out=dst_rows, in_=og.t[0:n, 0, :], reads=og.r, is_output=True)

    def rms_stats(self, B, src, src_res):
        S = self.S
        n = B["n"]
        sq = B["sq"]
        for kc in range(8):
            S.op("act", lambda e, kc=kc: e.activation(sq.t[:, kc, :], src.t[:, kc, :], AF.Square),
                 reads=[src_res[kc]] if len(src_res) == 8 else src_res, writes=sq.r)
        p = self.ps()
        for kc in range(8):
            S.op("pe", lambda e, kc=kc: e.matmul(p.t[:, 0:n], lhsT=self.onesb.t[:, :], rhs=sq.t[:, kc, :], start=(kc == 0), stop=(kc == 7)),
                 reads=sq.r + self.onesb.r, writes=p.r, signal=(kc == 7))
        rstd = B["rstd"]
        S.op("act", lambda e: e.activation(rstd.t[:, :], p.t[:, 0:n], AF.Sqrt, bias=self.epsb.t[:, 0:1], scale=1.0),
             reads=p.r + self.epsb.r, writes=rstd.r)
        S.op("dve", lambda e: e.reciprocal(rstd.t[:, :], rstd.t[:, :]), reads=rstd.r, writes=rstd.r)
        return rstd

    def norm_mod(self, B, seqcols, A, Bsh_mod, bchunk0):
        S = self.S
        xT, hT = B["xT"], B["hT"]
        rstd = self.rms_stats(B, xT, xT.r)
        for kc in range(8):
            for (sc, c0, w) in seqcols:
                t = self.tmp(B)
                S.op("dve", lambda e, kc=kc, sc=sc, c0=c0, w=w, t=t: e.scalar_tensor_tensor(
                    t.t[:, c0:c0 + w], xT.t[:, kc, c0:c0 + w], A.t[:, kc, sc:sc + 1], rstd.t[:, c0:c0 + w], ALU.mult, ALU.mult),
                     reads=[xT.r[kc]] + A.r + rstd.r, writes=t.r)
                S.op("act", lambda e, kc=kc, sc=sc, c0=c0, w=w, t=t: e.activation(
                    hT.t[:, kc, c0:c0 + w], t.t[:, c0:c0 + w], AF.Identity, bias=Bsh_mod.t[:, bchunk0 + kc, sc:sc + 1], scale=1.0),
                     reads=t.r + Bsh_mod.r, writes=hT.r)

    def resid_update(self, B, seqcols, G):
        S = self.S
        xT, y = B["xT"], B["y"]
        rstd = self.rms_stats(B, y, y.r)
        for kc in range(8):
            for (sc, c0, w) in seqcols:
                t = self.tmp(B)
                S.op("dve", lambda e, kc=kc, sc=sc, c0=c0, w=w, t=t: e.scalar_tensor_tensor(
                    t.t[:, c0:c0 + w], y.t[:, kc, c0:c0 + w], G.t[:, kc, sc:sc + 1], rstd.t[:, c0:c0 + w], ALU.mult, ALU.mult),
                     reads=[y.r[kc]] + G.r + rstd.r, writes=t.r)
                S.op("pool", lambda e, kc=kc, c0=c0, w=w, t=t: e.tensor_tensor(
                    xT.t[:, kc, c0:c0 + w], xT.t[:, kc, c0:c0 + w], t.t[:, c0:c0 + w], ALU.add),
                     reads=t.r + [xT.r[kc]], writes=[xT.r[kc]])

    def mm(self, p, n, wv, col0, rhs_fn, nk, reads):
        S = self.S
        for kc in range(nk):
            rhs, rres = rhs_fn(kc)
            S.op("pe", lambda e, kc=kc, rhs=rhs: e.matmul(p.t[:, 0:n], lhsT=wv[:, kc, col0:col0 + 128], rhs=rhs,
                                                          start=(kc == 0), stop=(kc == nk - 1)),
                 reads=reads + rres, writes=p.r, signal=(kc == nk - 1))

    def ffn(self, B, l, seqcols, first, state_src=None):
        S = self.S
        n, nseg, sl = B["n"], B["nseg"], B["sl"]
        hT, act, y = B["hT"], B["act"], B["y"]
        fh = B["fhalo"][l]
        fcw, fcb = self.V["fcw%d" % l], self.V["fcb%d" % l]
        rhs_h = lambda kc: (hT.t[:, kc, :], hT.r)

        def v3(ap):
            return ap.rearrange("p (s t) -> p s t", s=nseg)

        extras = B.get("ffn_xt", []) + B.get("ffn_xf", [])
        owner = B["xtok"].r[0]
        for xb in extras:
            xb.r[0].lw = owner.lw
            xb.r[0].rd = dict(owner.rd)
        tmps = B["tmp"] + B.get("ffn_xt", [])
        fbufs = B["fbuf"] + B.get("ffn_xf", [])
        rot = {"t": 0, "f": 0}

        def next_tmp():
            t_ = tmps[rot["t"] % len(tmps)]
            rot["t"] += 1
            return t_

        def next_fb():
            f_ = fbufs[rot["f"] % len(fbufs)]
            rot["f"] += 1
            return f_

        def back(j, ta, tg):
            S.op("act", lambda e, tg=tg: e.activation(tg.t[:, 0:n], tg.t[:, 0:n], AF.Silu), reads=tg.r, writes=tg.r)
            S.op("dve", lambda e, ta=ta, tg=tg, j=j: e.tensor_tensor(act.t[:, j, :], ta.t[:, 0:n], tg.t[:, 0:n], ALU.mult),
                 reads=ta.r + tg.r, writes=[act.r[j]])

        prev = None
        for jj in range(11):
            sl_w, wv = self.load_w(self.W["up%d" % l][jj])
            for j2 in range(2):
                j = jj * 2 + j2
                res = []
                for part in range(2):
                    ch = part * 22 + j
                    p = self.ps()
                    self.mm(p, n, wv, j2 * 256 + part * 128, rhs_h, 8, sl_w.r)
                    fb = next_fb()
                    S.op("act", lambda e, p=p, fb=fb: e.copy(fb.t[:, :, 2:2 + sl], v3(p.t[:, 0:n])), reads=p.r, writes=fb.r)
                    S.op("act", lambda e, fb=fb, ch=ch: e.copy(fb.t[:, :, 0:2], fh.t[:, ch, :, :]), reads=[fh.r[ch]], writes=fb.r)
                    t = next_tmp()
                    S.op("act", lambda e, fb=fb, t=t, ch=ch: e.activation(v3(t.t[:, 0:n]), fb.t[:, :, 0:sl], AF.Identity,
                                                                         bias=fcb.t[:, ch:ch + 1], scale=fcw.t[:, ch:ch + 1]),
                         reads=fb.r + fcw.r + fcb.r, writes=t.r)
                    S.op("dve", lambda e, fb=fb, t=t, ch=ch: e.scalar_tensor_tensor(v3(t.t[:, 0:n]), fb.t[:, :, 1:1 + sl], fcw.t[:, 44 + ch:44 + ch + 1],
                                                                                   v3(t.t[:, 0:n]), ALU.mult, ALU.add),
                         reads=fb.r + fcw.r + t.r, writes=t.r)
                    S.op("dve", lambda e, fb=fb, t=t, ch=ch: e.scalar_tensor_tensor(v3(t.t[:, 0:n]), fb.t[:, :, 2:2 + sl], fcw.t[:, 88 + ch:88 + ch + 1],
                                                                                   v3(t.t[:, 0:n]), ALU.mult, ALU.add),
                         reads=fb.r + fcw.r + t.r, writes=t.r)
                    S.op("pool", lambda e, fb=fb, ch=ch: e.tensor_copy(fh.t[:, ch, :, :], fb.t[:, :, sl:sl + 2]), reads=fb.r, writes=[fh.r[ch]])
                    res.append(t)
                if prev is not None:
                    back(*prev)
                prev = (j, res[0], res[1])
        back(*prev)
        for xb in extras:
            r_ = xb.r[0]
            toks_ = dict(r_.rd)
            if r_.lw is not None:
                toks_[r_.lw[0]] = max(toks_.get(r_.lw[0], 0), r_.lw[1])
            for k_, v_ in toks_.items():
                if owner.rd.get(k_, 0) < v_:
                    owner.rd[k_] = v_
        for oc in range(8):
            sl_w, wv = self.load_w(self.W["down%d" % l][oc])
            p = self.ps()
            self.mm(p, n, wv, 0, lambda kc: (act.t[:, kc, :], [act.r[kc]]), 22, sl_w.r)
            S.op("act", lambda e, oc=oc, p=p: e.copy(y.t[:, oc, :], p.t[:, 0:n]), reads=p.r, writes=[y.r[oc]])

    def mixer_ab(self, B, seqcols, first):
        S = self.S
        n, nseg, sl = B["n"], B["nseg"], B["sl"]
        hT, cat, y = B["hT"], B["cat"], B["y"]
        acw, bsc = self.V["acw"], self.V["bsc"]
        rhs_h = lambda kc: (hT.t[:, kc, :], hT.r)

        def v3(ap):
            return ap.rearrange("p (s t) -> p s t", s=nseg)

        for i in range(4):
            sl_w, wv = self.load_w(self.W["in0"][i])
            pA = self.ps(); self.mm(pA, n, wv, 0, rhs_h, 8, sl_w.r)
            pC = self.ps(); self.mm(pC, n, wv, 256, rhs_h, 8, sl_w.r)
            pB = self.ps(); self.mm(pB, n, wv, 128, rhs_h, 8, sl_w.r)
            pU = self.ps(); self.mm(pU, n, wv, 384, rhs_h, 8, sl_w.r)
            t1 = self.tmp(B)
            S.op("act", lambda e, pA=pA, t1=t1: e.copy(t1.t[:, 0:n], pA.t[:, 0:n]), reads=pA.r, writes=t1.r)
            pb = B["pbuf"][i]
            S.op("dve", lambda e, pC=pC, t1=t1, pb=pb: e.tensor_tensor(pb.t[:, :, 2:2 + sl], v3(pC.t[:, 0:n]), v3(t1.t[:, 0:n]), ALU.mult),
                 reads=pC.r + t1.r, writes=pb.r)
            z = self.tmp(B)
            S.op("pool", lambda e, pb=pb, z=z, i=i: e.tensor_scalar(v3(z.t[:, 0:n]), pb.t[:, :, 0:sl], acw.t[:, i:i + 1], None, ALU.mult),
                 reads=pb.r + acw.r, writes=z.r)
            S.op("dve", lambda e, pb=pb, z=z, i=i: e.scalar_tensor_tensor(v3(z.t[:, 0:n]), pb.t[:, :, 1:1 + sl], acw.t[:, 4 + i:5 + i], v3(z.t[:, 0:n]), ALU.mult, ALU.add),
                 reads=pb.r + acw.r + z.r, writes=z.r)
            S.op("dve", lambda e, pb=pb, z=z, i=i: e.scalar_tensor_tensor(v3(z.t[:, 0:n]), pb.t[:, :, 2:2 + sl], acw.t[:, 8 + i:9 + i], v3(z.t[:, 0:n]), ALU.mult, ALU.add),
                 reads=pb.r + acw.r + z.r, writes=z.r)
            S.op("dve", lambda e, pB=pB, z=z, i=i: e.tensor_tensor(cat.t[:, i, :], pB.t[:, 0:n], z.t[:, 0:n], ALU.mult),
                 reads=pB.r + z.r, writes=[cat.r[i]])
            ub = B["ubuf"][i]
            S.op("act", lambda e, pU=pU, ub=ub: e.copy(ub.t[:, :, 15:15 + sl], v3(pU.t[:, 0:n])), reads=pU.r, writes=ub.r)
            cur = ub
            L = 15 + sl
            sh = 1
            for lev in range(i + 1):
                nxt = B["sbuf"][lev % 2]
                lo = 2 * sh - 1
                eng = "dve" if lev % 2 == 0 else "pool"
                S.op(eng, lambda e, cur=cur, nxt=nxt, lo=lo, sh=sh: e.tensor_tensor(nxt.t[:, :, lo:L], cur.t[:, :, lo:L], cur.t[:, :, lo - sh:L - sh], ALU.add),
                     reads=cur.r, writes=nxt.r)
                cur = nxt
                sh *= 2
            win = 2 ** (i + 1)
            pl = B["pool"]
            S.op("dve", lambda e, cur=cur, ub=ub, i=i, win=win: e.scalar_tensor_tensor(
                v3(pl.t[:, i, :]), cur.t[:, :, 15:15 + sl], 1.0 / win, ub.t[:, :, 15:15 + sl], ALU.mult, ALU.subtract),
                 reads=cur.r + ub.r, writes=[pl.r[i]])
            if first:
                S.op("dve", lambda e, cur=cur, i=i: e.tensor_tensor(cur.t[:, 0, 15:30], cur.t[:, 0, 15:30], self.rcnt.t[:, i, 0:15], ALU.mult),
                     reads=cur.r + self.rcnt.r, writes=cur.r)
                S.op("dve", lambda e, cur=cur, ub=ub, i=i: e.tensor_tensor(pl.t[:, i, 0:15], cur.t[:, 0, 15:30], ub.t[:, 0, 15:30], ALU.subtract),
                     reads=cur.r + ub.r, writes=[pl.r[i]])
            pG = self.ps()
            S.op("pe", lambda e, pG=pG, i=i: e.matmul(pG.t[:, 0:n], lhsT=self.bw.t[:, i, :], rhs=pl.t[:, i, :], start=True, stop=True),
                 reads=self.bw.r + [pl.r[i]], writes=pG.r)
            S.op("act", lambda e, pG=pG, i=i: e.activation(cat.t[:, 4 + i, :], pG.t[:, 0:n], AF.Copy, scale=bsc.t[:, i:i + 1]),
                 reads=pG.r + bsc.r, writes=[cat.r[4 + i]])
        for g in range(2):
            sl_w, wv = self.load_w(self.W["out0"][g])
            for c4 in range(4):
                oc = g * 4 + c4
                p = self.ps()
                self.mm(p, n, wv, c4 * 128, lambda kc: (cat.t[:, kc, :], [cat.r[kc]]), 8, sl_w.r)
                S.op("act", lambda e, oc=oc, p=p: e.copy(y.t[:, oc, :], p.t[:, 0:n]), reads=p.r, writes=[y.r[oc]])

    def halo_shift_ab(self, B):
        S = self.S
        sl = B["sl"]
        for i in range(4):
            pb, ub = B["pbuf"][i], B["ubuf"][i]
            S.op("pool", lambda e, pb=pb: e.tensor_copy(pb.t[:, :, 0:2], pb.t[:, :, sl:sl + 2]), reads=pb.r, writes=pb.r)
            S.op("pool", lambda e, ub=ub: e.tensor_copy(ub.t[:, :, 0:15], ub.t[:, :, sl:sl + 15]), reads=ub.r, writes=ub.r)

    def store_rows_T(self, src_fn, nchunks, R, dst_rows_fn, src_res):
        S = self.S
        for c0 in range(0, nchunks, 4):
            nn = min(4, nchunks - c0)
            p = self.ps()
            for c in range(nn):
                S.op("pe", lambda e, c=c, c0=c0, p=p: e.transpose(p.t[0:R, c * 128:(c + 1) * 128], src_fn(c0 + c), self.ident.t[:, :]),
                     reads=src_res + self.ident.r, writes=p.r, signal=(c == nn - 1))
            og = self.rowstg
            S.op("act", lambda e, p=p, nn=nn: e.copy(og.t[0:R, 0:nn * 128], p.t[0:R, 0:nn * 128]), reads=p.r, writes=og.r)
            S.dma("pool", out=dst_rows_fn(c0 * 128, nn * 128), in_=og.t[0:R, 0:nn * 128], reads=og.r, is_output=True)

    def layer0(self):
        S = self.S
        cfg = self.cfg
        d, o = self.din, self.dout
        mp = self.modp[0]
        Bp = self.TB["p"]
        for i in range(4):
            S.op("pool", lambda e, i=i: e.memset(Bp["pbuf"][i].t[:, :, 0:2], 0.0), writes=Bp["pbuf"][i].r)
            S.op("pool", lambda e, i=i: e.memset(Bp["ubuf"][i].t[:, :, 0:15], 0.0), writes=Bp["ubuf"][i].r)
        for i in range(2):
            S.op("pool", lambda e, i=i: e.memset(Bp["sbuf"][i].t[:, :, :], 0.0), writes=Bp["sbuf"][i].r)
        S.op("pool", lambda e: e.memset(Bp["fhalo"][0].t[:], 0.0), writes=Bp["fhalo"][0].r)
        if cfg.depth > 1:
            self.x1 = self.scratch("x1", [D, cfg.seq], F32)
        for ti in range(cfg.ntiles):
            first, last = ti == 0, ti == cfg.ntiles - 1
            t0 = ti * NT
            seqcols = [(0, 0, NT)]
            self.load_x_tokmajor(Bp, d["xp"][t0:t0 + NT, :])
            self.norm_mod(Bp, seqcols, mp["A1"], mp["mod"], 0)
            self.mixer_ab(Bp, seqcols, first)
            if last:
                self.store_rows_T(lambda c: Bp["pbuf"][c].t[:, 0, NT:NT + 2], 4, 2, lambda c0, nc_: o["a_p"][:, c0:c0 + nc_],
                                  sum([Bp["pbuf"][c].r for c in range(4)], []))
                self.store_rows_T(lambda c: Bp["ubuf"][c].t[:, 0, NT:NT + 15], 4, 15, lambda c0, nc_: o["b_p"][:, c0:c0 + nc_],
                                  sum([Bp["ubuf"][c].r for c in range(4)], []))
            else:
                self.halo_shift_ab(Bp)
            self.resid_update(Bp, seqcols, mp["G1"])
            self.norm_mod(Bp, seqcols, mp["A2"], mp["mod"], 24)
            self.ffn(Bp, 0, seqcols, first)
            if last:
                fh = Bp["fhalo"][0]
                for r in range(2):
                    self.store_rows_T(lambda c, r=r: fh.t[:, c, 0, r:r + 1], 44, 1,
                                      lambda c0, nc_, r=r: o["f_p"][0, r:r + 1, c0:c0 + nc_], fh.r)
            self.resid_update(Bp, seqcols, mp["G2"])
            if cfg.depth > 1:
                x1r = self.scr["x1"][1]
                for kc in range(8):
                    S.dma("pool", out=self.x1[kc * 128:(kc + 1) * 128, t0:t0 + NT], in_=Bp["xT"].t[:, kc, :],
                          reads=[Bp["xT"].r[kc]], writes=[x1r])
            if cfg.depth == 1 or cfg.debug_x1:
                self.store_x_tokmajor(Bp, o["yp"][t0:t0 + NT, :])
            if ti == 0:
                self.prep_weights(1)
        if cfg.do_sample:
            self.layer0_sample()

    def layer0_sample(self):
        S = self.S
        cfg = self.cfg
        d, o = self.din, self.dout
        mp = self.modp[0]
        Bs = self.TB["s"]
        seqcols = [(1 + s, 4 * s, 4) for s in range(4)]
        for i in range(4):
            pass
        self.load_state_T(d["sa"].rearrange("s r f -> (s r) f"), 8, 4, lambda c: Bs["pbuf"][c].t[:, :, 0:2], [Bs["pbuf"][c].r for c in range(4)], 4, 2)
        self.load_state_T(d["sb"].rearrange("s r f -> (s r) f"), 60, 4, lambda c: Bs["ubuf"][c].t[:, :, 0:15], [Bs["ubuf"][c].r for c in range(4)], 4, 15)
        fh = Bs["fhalo"][0]
        self.load_state_T(d["sf"][0].rearrange("s r f -> (s r) f"), 8, 44, lambda c: fh.t[:, c, :, :], [fh.r] * 44, 4, 2)
        self.load_x_tokmajor(Bs, d["xs"])
        self.norm_mod(Bs, seqcols, mp["A1"], mp["mod"], 0)
        self.mixer_ab(Bs, seqcols, False)
        self.store_state_T(lambda c: Bs["pbuf"][c].t[:, :, 4:6], sum([Bs["pbuf"][c].r for c in range(4)], []), 4, 4, 2,
                           lambda c0, nc_: o["a_s"].rearrange("s r f -> (s r) f")[:, c0:c0 + nc_])
        self.store_state_T(lambda c: Bs["ubuf"][c].t[:, :, 4:19], sum([Bs["ubuf"][c].r for c in range(4)], []), 4, 4, 15,
                           lambda c0, nc_: o["b_s"].rearrange("s r f -> (s r) f")[:, c0:c0 + nc_])
        self.resid_update(Bs, seqcols, mp["G1"])
        self.norm_mod(Bs, seqcols, mp["A2"], mp["mod"], 24)
        self.ffn(Bs, 0, seqcols, False)
        self.store_state_T(lambda c: fh.t[:, c, :, :], fh.r, 44, 4, 2,
                           lambda c0, nc_: o["f_s"][0].rearrange("s r f -> (s r) f")[:, c0:c0 + nc_])
        self.resid_update(Bs, seqcols, mp["G2"])
        if cfg.depth == 1 or cfg.debug_x1:
            self.store_x_tokmajor(Bs, o["ys"])

    def load_state_T(self, src_rows, R, nchunks, dst_fn, dst_res, nseg, nr):
        S = self.S
        stg = self.sstage
        for c0 in range(0, nchunks, 8):
            nn = min(8, nchunks - c0)
            S.dma("sp", out=stg.t[0:R, 0:nn * 128], in_=src_rows[:, c0 * 128:(c0 + nn) * 128], writes=stg.r)
            for c in range(nn):
                p = self.ps()
                S.op("pe", lambda e, c=c, p=p: e.transpose(p.t[:, 0:R], stg.t[0:R, c * 128:(c + 1) * 128], self.ident.t[0:R, 0:R]),
                     reads=stg.r + self.ident.r, writes=p.r)
                S.op("dve", lambda e, c=c, c0=c0, p=p: e.tensor_copy(dst_fn(c0 + c), p.t[:, 0:R].rearrange("p (s r) -> p s r", s=nseg)),
                     reads=p.r, writes=dst_res[c0 + c])

    def store_state_T(self, src_fn, src_res, nchunks, nseg, nr, dst_rows_fn):
        S = self.S
        R = nseg * nr
        stg = self.sstage2
        for c0 in range(0, nchunks, 4):
            nn = min(4, nchunks - c0)
            p = self.ps()
            for c in range(nn):
                cp = self.cpstg[self.cpi]
                self.cpi = (self.cpi + 1) % 2
                S.op("dve", lambda e, c=c, c0=c0, cp=cp: e.tensor_copy(cp.t[:, 0:R].rearrange("p (s r) -> p s r", s=nseg), src_fn(c0 + c)),
                     reads=src_res, writes=cp.r)
                S.op("pe", lambda e, c=c, p=p, cp=cp: e.transpose(p.t[0:R, c * 128:(c + 1) * 128], cp.t[:, 0:R], self.ident.t[:, :]),
                     reads=cp.r + self.ident.r, writes=p.r)
            S.op("act", lambda e, p=p, nn=nn: e.copy(stg.t[0:R, 0:nn * 128], p.t[0:R, 0:nn * 128]), reads=p.r, writes=stg.r)
            S.dma("pool", out=dst_rows_fn(c0 * 128, nn * 128), in_=stg.t[0:R, 0:nn * 128], reads=stg.r, is_output=True)

    def pbf(self):
        b = self.psbf[self.psbi]
        self.psbi = (self.psbi + 1) % 2
        return b

    def layer1(self):
        S = self.S
        cfg = self.cfg
        d, o = self.din, self.dout
        seq = cfg.seq
        S.barrier()
        self.qsc, self.ksc = [], []
        for g, (win, dil) in enumerate(C_PAIRS):
            self.qsc.append(self.scratch("qsc%d" % g, [2, 128, dil, seq // dil], BF16))
            self.ksc.append(self.scratch("ksc%d" % g, [2, 128, dil, seq // dil], BF16))
        self.vsc = self.scratch("vsc", [seq, 768], BF16)
        self.osc = self.scratch("osc", [seq, 768], F32)
        self.lsc = self.scratch("lsc", [seq, 12], F32)
        self.ydsc = self.scratch("ydsc", [512, seq], BF16)
        self.bsc = self.scratch("bsc", [12, 128, 384], F32)
        Bp = self.TB["p"]
        self.halfb = self.sb("halfb", [128, 1], F32)
        S.op("pool", lambda e: e.memset(self.halfb.t[:], 0.5), writes=self.halfb.r)
        self.qstg = [self.sb("qstg%d" % k, [128, NT], BF16) for k in range(2)]
        self.qsi = 0
        self.kvrow = []
        for k in range(2):
            b_ = Buf(Bp["pbuf"][k].t[:, 0, 0:512], 1, "kvrow%d" % k)
            b_.r = Bp["pbuf"][k].r
            self.kvrow.append(b_)
        self.kvi = 0
        self.vbf = [self.sb("vbf%d" % k, [128, 256], BF16) for k in range(2)]
        self.vbi = 0
        self.rb = self.sb("rb", [32, 12], F32)
        S.dma("sp", out=self.rb.t[:, :], in_=d["rel_bias"], writes=self.rb.r)
        self.ohg = []
        for g in range(3):
            t_ = Buf(Bp["fbuf"][g].t[0:33, 0, 0:384], 1, "ohg%d" % g)
            t_.r = Bp["fbuf"][g].r
            S.dma("sp", out=t_.t, in_=d["oh"][g], writes=t_.r)
            self.ohg.append(t_)
        self.mst = [self.sb("mst0", [128, 4, 4], F32), self.sb("mst1", [128, 4, 3, 4], F32), self.sb("mst2", [128, 4, 4], F32)]
        self.ycf = [self.sb("ycf%d" % k, [128, 256], F32) for k in range(2)]
        self.ycb = [self.sb("ycb%d" % k, [128, 256], BF16) for k in range(2)]
        self.ltile = self.sb("ltile", [128, 4, 3, 4], F32)
        if cfg.do_sample:
            self.qkT_s = self.sb("qkT_s", [128, 12, 16], BF16)
            self.kvn = []
            for k in range(2):
                kv_ = Buf(Bp["xT"].t[0:4, k, 0:512], 1, "kvn%d" % k)
                kv_.r = [Bp["xT"].r[k]]
                self.kvn.append(kv_)
            self.kvni = 0
            self.kvnb = self.sb("kvnb", [4, 4, 3, 256], BF16)
            self.rbrep = self.sb("rbrep", [128, 12], F32)
            for t_ in range(4):
                S.dma("sp", out=self.rbrep.t[t_ * 32:(t_ + 1) * 32, :], in_=d["rel_bias"], writes=self.rbrep.r)
            self.dmask = self.sb("dmask_sb", [128, 4], F32)
            S.dma("sp", out=self.dmask.t[:, :], in_=d["dmask"], writes=self.dmask.r)
            self.ohs = []
            for g in range(3):
                t_ = self.sb("ohs%d" % g, [128, 132], F32)
                S.dma("sp", out=t_.t[:, :], in_=d["ohs"][g], writes=t_.r)
                self.ohs.append(t_)
            self.masks = self.sb("masks_sb", [16, 3, 132], F32)
            S.dma("sp", out=self.masks.t[:, :, :], in_=d["masks"], writes=self.masks.r)
        S.op("pool", lambda e: e.memset(Bp["fhalo"][1].t[:], 0.0), writes=Bp["fhalo"][1].r)
        self.dg = [self.sb("dg%d" % k, [128, 128], BF16) for k in range(8)]
        self.dgi = 0
        for kind, B_ in self.TB.items():
            B_["dbb"] = [self.sb(kind + "dbb%d" % i, [128, B_["nseg"], 30 + B_["sl"]], BF16) for i in range(4)]
            B_["dbo"] = [self.sb(kind + "dbo%d" % i, [128, B_["nseg"], 30 + B_["sl"]], BF16) for i in range(4)]
            B_["dst"] = self.sb(kind + "dst", [128, 4, B_["nseg"], 30], F32)
        for i in range(4):
            S.op("pool", lambda e, i=i: e.memset(Bp["dbb"][i].t[:, :, 0:30], 0.0), writes=Bp["dbb"][i].r)
        if os.environ.get("KDEBUG_STOP") == "l1start":
            return
        for ti in range(cfg.ntiles):
            self.l1_p1(Bp, ti * NT, [(0, 0, NT)], last=(ti == cfg.ntiles - 1))
        if os.environ.get("KDEBUG_STOP") in ("p1a", "p1b", "p1p"):
            return
        if cfg.do_sample:
            self.l1_p1_sample()
        if os.environ.get("KDEBUG_STOP") in ("p1", "s1", "s2", "s3", "s4"):
            return
        S.barrier()
        self.l1_attn_setup()
        if os.environ.get("KDEBUG_STOP") == "p2setup":
            return
        self.l1_attn_prompt()
        if os.environ.get("KDEBUG_STOP") == "p2p":
            return
        if cfg.do_sample:
            self.l1_attn_sample()
        if os.environ.get("KDEBUG_STOP") == "p2":
            return
        S.barrier()
        for ti in range(cfg.ntiles):
            self.l1_p3(Bp, ti * NT, [(0, 0, NT)], last=(ti == cfg.ntiles - 1))
        if cfg.do_sample:
            self.l1_p3_sample()

    def l1_dpath(self, B, seqcols, ydst_fn, want_state):
        S = self.S
        n, nseg, sl = B["n"], B["nseg"], B["sl"]
        hT, y = B["hT"], B["y"]
        dcw, dcb, dlg, dlb = self.V["dcw"], self.V["dcb"], self.V["dlg"], self.V["dlb"]
        rhs_h = lambda kc: (hT.t[:, kc, :], hT.r)

        def v3(ap):
            return ap.rearrange("p (s t) -> p s t", s=nseg)

        zb = y
        sq = B["sq"]
        dstate = B["dst"]
        for g2 in range(2):
            sl_w, wv = self.load_w(self.W["d1"][g2])
            for i2 in range(2):
                i = g2 * 2 + i2
                pV = self.ps(); self.mm(pV, n, wv, i2 * 256, rhs_h, 8, sl_w.r)
                pG = self.ps(); self.mm(pG, n, wv, i2 * 256 + 128, rhs_h, 8, sl_w.r)
                t1 = self.tmp(B)
                S.op("act", lambda e, pG=pG, t1=t1: e.activation(t1.t[:, 0:n], pG.t[:, 0:n], AF.Tanh, scale=0.5), reads=pG.r, writes=t1.r)
                S.op("act", lambda e, t1=t1: e.activation(t1.t[:, 0:n], t1.t[:, 0:n], AF.Identity, bias=self.halfb.t[:, 0:1], scale=0.5), reads=t1.r + self.halfb.r, writes=t1.r)
                db = B["dbb"][i]
                S.op("dve", lambda e, pV=pV, t1=t1, db=db: e.tensor_tensor(db.t[:, :, 30:30 + sl], v3(pV.t[:, 0:n]), v3(t1.t[:, 0:n]), ALU.mult),
                     reads=pV.r + t1.r, writes=db.r)
                if want_state:
                    k = min(30, sl)
                    S.op("dve", lambda e, pV=pV, t1=t1, i=i, k=k: e.tensor_tensor(dstate.t[:, i, :, 30 - k:30], v3(pV.t[:, 0:n])[:, :, sl - k:sl], v3(t1.t[:, 0:n])[:, :, sl - k:sl], ALU.mult),
                         reads=pV.r + t1.r, writes=dstate.r)
                dbo = B["dbo"][i]
                S.op("pool", lambda e, db=db, dbo=dbo: e.tensor_copy(dbo.t[:, :, 0:29 + sl], db.t[:, :, 1:30 + sl]), reads=db.r, writes=dbo.r)
                pc = self.ps()
                for j in range(31):
                    dg = self.dg[self.dgi]
                    self.dgi = (self.dgi + 1) % len(self.dg)
                    S.op("dve", lambda e, dg=dg, j=j, i=i: e.tensor_scalar(dg.t[:, :], self.identb.t[:, :], dcw.t[:, j * 4 + i:j * 4 + i + 1], None, ALU.mult),
                         reads=self.identb.r + dcw.r, writes=dg.r)
                    src, jj = (db, j) if j % 2 == 0 else (dbo, j - 1)
                    rhs = src.t[:, 0, jj:jj + sl] if nseg == 1 else src.t[:, :, jj:jj + sl]
                    S.op("pe", lambda e, dg=dg, rhs=rhs, pc=pc, j=j: e.matmul(pc.t[:, 0:n], lhsT=dg.t[:, :], rhs=rhs, start=(j == 0), stop=(j == 30)),
                         reads=dg.r + src.r, writes=pc.r)
                S.op("act", lambda e, pc=pc, i=i: e.activation(zb.t[:, i, :], pc.t[:, 0:n], AF.Identity, bias=dcb.t[:, i:i + 1], scale=1.0),
                     reads=pc.r + dcb.r, writes=zb.r)
                S.op("act", lambda e, pc=pc, i=i: e.activation(sq.t[:, 4 + i, :], pc.t[:, 0:n], AF.Square, bias=dcb.t[:, i:i + 1], scale=1.0),
                     reads=pc.r + dcb.r, writes=sq.r)
                S.op("act", lambda e, pc=pc, i=i: e.activation(sq.t[:, i, :], pc.t[:, 0:n], AF.Identity, bias=dcb.t[:, i:i + 1], scale=1.0),
                     reads=pc.r + dcb.r, writes=sq.r)
        p1 = self.ps()
        for i in range(4):
            S.op("pe", lambda e, i=i: e.matmul(p1.t[:, 0:n], lhsT=self.onesb.t[:, :], rhs=sq.t[:, i, :], start=(i == 0), stop=(i == 3)),
                 reads=sq.r + self.onesb.r, writes=p1.r, signal=(i == 3))
        p2 = self.ps()
        for i in range(4):
            S.op("pe", lambda e, i=i: e.matmul(p2.t[:, 0:n], lhsT=self.onesb.t[:, :], rhs=sq.t[:, 4 + i, :], start=(i == 0), stop=(i == 3)),
                 reads=sq.r + self.onesb.r, writes=p2.r, signal=(i == 3))
        mean = Buf(zb.t[:, 4, :], 1, "ln_mean"); mean.r = zb.r
        var = Buf(zb.t[:, 5, :], 1, "ln_var"); var.r = zb.r
        S.op("dve", lambda e: e.tensor_scalar(mean.t[:, 0:n], p1.t[:, 0:n], 1.0 / 512, None, ALU.mult), reads=p1.r, writes=mean.r)
        S.op("dve", lambda e: e.tensor_tensor(var.t[:, 0:n], mean.t[:, 0:n], mean.t[:, 0:n], ALU.mult), reads=mean.r, writes=var.r)
        S.op("dve", lambda e: e.scalar_tensor_tensor(var.t[:, 0:n], p2.t[:, 0:n], 1.0 / 512, var.t[:, 0:n], ALU.mult, ALU.subtract),
             reads=p2.r + var.r, writes=var.r)
        S.op("act", lambda e: e.activation(var.t[:, 0:n], var.t[:, 0:n], AF.Sqrt, bias=self.epsb.t[:, 1:2], scale=1.0), reads=var.r + self.epsb.r, writes=var.r)
        S.op("dve", lambda e: e.reciprocal(var.t[:, 0:n], var.t[:, 0:n]), reads=var.r, writes=var.r)
        for i in range(4):
            t = self.tmp(B)
            S.op("dve", lambda e, i=i, t=t: e.tensor_tensor(t.t[:, 0:n], zb.t[:, i, :], mean.t[:, 0:n], ALU.subtract), reads=zb.r + mean.r, writes=t.r)
            S.op("dve", lambda e, t=t: e.tensor_tensor(t.t[:, 0:n], t.t[:, 0:n], var.t[:, 0:n], ALU.mult), reads=t.r + var.r, writes=t.r)
            dst, dres = ydst_fn(i)
            S.op("act", lambda e, i=i, t=t, dst=dst: e.activation(dst, t.t[:, 0:n], AF.Silu, bias=dlb.t[:, i:i + 1], scale=dlg.t[:, i:i + 1]),
                 reads=t.r + dlb.r + dlg.r, writes=dres)

    def l1_p1(self, B, t0, seqcols, last):
        S = self.S
        cfg = self.cfg
        d, o = self.din, self.dout
        mp = self.modp[1]
        n = NT
        xT, hT, cat = B["xT"], B["hT"], B["cat"]
        x1r = self.scr["x1"][1]
        for kc in range(8):
            S.dma("sp", out=xT.t[:, kc, :], in_=self.x1[kc * 128:(kc + 1) * 128, t0:t0 + NT], reads=[x1r], writes=[xT.r[kc]])
        self.norm_mod(B, seqcols, mp["A1"], mp["mod"], 0)
        rhs_h = lambda kc: (hT.t[:, kc, :], hT.r)
        for grp in range(3):
            sl_w, wv = self.load_w(self.W["qk1"][grp])
            for c4 in range(4):
                cc = grp * 4 + c4
                isq = cc < 6
                g = (cc // 2) if isq else ((cc - 6) // 2)
                half = cc % 2
                dil = C_PAIRS[g][1]
                M = NT // dil
                p = self.ps()
                self.mm(p, n, wv, c4 * 128, rhs_h, 8, sl_w.r)
                stg = self.qstg[self.qsi]
                self.qsi = (self.qsi + 1) % len(self.qstg)
                S.op("act", lambda e, p=p, stg=stg, dil=dil, isq=isq: e.activation(
                    stg.t[:, 0:n].rearrange("p (r m) -> p m r", r=dil), p.t[:, 0:n].rearrange("p (m r) -> p m r", r=dil),
                    AF.Copy, scale=(0.125 if isq else 1.0)), reads=p.r, writes=stg.r)
                dst = (self.qsc if isq else self.ksc)[g]
                dres = self.scr[("qsc%d" if isq else "ksc%d") % g][1]
                S.dma("act", out=dst[half, :, :, t0 // dil:t0 // dil + M], in_=stg.t[:, 0:n].rearrange("p (r m) -> p r m", r=dil),
                      reads=stg.r, writes=[dres])
        if os.environ.get("KDEBUG_STOP") == "p1a":
            return
        vres = self.scr["vsc"][1]
        for g in range(3):
            wb = C_PAIRS[g][0]
            sl_w, wv = self.load_w(self.W["kv1"][g])
            for s in range(4):
                p = self.ps()
                for kc in range(8):
                    S.op("pe", lambda e, kc=kc, s=s, p=p, wv=wv: e.matmul(p.t[:, 0:512], lhsT=hT.t[:, kc, s * 128:(s + 1) * 128], rhs=wv[:, kc, 0:512],
                                                                        start=(kc == 0), stop=(kc == 7)),
                         reads=hT.r + sl_w.r, writes=p.r, signal=(kc == 7))
                kv = self.kvrow[self.kvi]
                self.kvi = (self.kvi + 1) % 2
                S.op("act", lambda e, p=p, kv=kv: e.copy(kv.t[:, 0:512], p.t[:, 0:512]), reads=p.r, writes=kv.r)
                tok = t0 + s * 128
                row = tok - (cfg.seq - min(wb, cfg.seq))
                vb = self.vbf[self.vbi]
                self.vbi = (self.vbi + 1) % 2
                S.op("act", lambda e, p=p, vb=vb: e.copy(vb.t[:, :], p.t[:, 256:512]), reads=p.r, writes=vb.r)
                if row >= 0:
                    S.dma("act", out=o["c%d_p" % g][row:row + 128, :], in_=kv.t[:, 0:512], reads=kv.r, is_output=True)
                S.dma("act", out=self.vsc[tok:tok + 128, g * 256:(g + 1) * 256], in_=vb.t[:, :], reads=vb.r, writes=[vres])
        if os.environ.get("KDEBUG_STOP") == "p1b":
            return
        self.l1_dpath(B, seqcols, lambda i: (cat.t[:, 2 + i, :], [cat.r[2 + i]]), want_state=last)
        ydr = self.scr["ydsc"][1]
        for i in range(4):
            S.dma("act", out=self.ydsc[i * 128:(i + 1) * 128, t0:t0 + NT], in_=cat.t[:, 2 + i, :], reads=[cat.r[2 + i]], writes=[ydr])
        if last:
            self.store_rows_T(lambda c: B["dst"].t[:, c, 0, :], 4, 30, lambda c0, nc_: o["d_p"][:, c0:c0 + nc_], B["dst"].r)
        else:
            for i in range(4):
                db = B["dbb"][i]
                S.op("pool", lambda e, db=db: e.tensor_copy(db.t[:, :, 0:30], db.t[:, :, NT:NT + 30]), reads=db.r, writes=db.r)

    def l1_attn_setup(self):
        S = self.S
        d = self.din
        Bp = self.TB["p"]
        y = Bp["y"]
        self.biasmat = []
        for gh in range(12):
            bm = Buf(y.t[:, gh // 2, (gh % 2) * 256:(gh % 2) * 256 + 256], 1, "biasmat%d" % gh)
            self.biasmat.append(bm)
        rb = self.rb
        self.bl = [self.sb("bl%d" % k, [128, 128], BF16) for k in range(2)]
        hT_, act_ = Bp["hT"], Bp["act"]
        self.ohgb = [Buf(hT_.t[0:33, 6, 0:384], 1, "ohgb0"), Buf(hT_.t[0:33, 7, 0:384], 1, "ohgb1"), Buf(act_.t[0:33, 16, 0:384], 1, "ohgb2")]
        for g in range(3):
            S.op("dve", lambda e, g=g: e.tensor_copy(self.ohgb[g].t, self.ohg[g].t), reads=self.ohg[g].r, writes=self.ohgb[g].r)
        bres = self.scr["bsc"][1]
        for g in range(3):
            ohg = self.ohg[g]
            for h in range(4):
                gh = g * 4 + h
                lt = self.tmp(Bp)
                S.op("dve", lambda e, lt=lt: e.memset(lt.t[0:33, 0:128], 1.0), writes=lt.r)
                S.op("dve", lambda e, lt=lt, gh=gh: e.tensor_scalar(lt.t[0:32, 0:128], lt.t[0:32, 0:128], rb.t[0:32, gh:gh + 1], None, ALU.mult),
                     reads=lt.r + rb.r, writes=lt.r)
                hi, lo = self.bl[0], self.bl[1]
                S.op("dve", lambda e, lt=lt, hi=hi: e.tensor_copy(hi.t[0:33, :], lt.t[0:33, 0:128]), reads=lt.r, writes=hi.r)
                S.op("dve", lambda e, lt=lt, hi=hi: e.tensor_tensor(lt.t[0:33, 0:128], lt.t[0:33, 0:128], hi.t[0:33, :], ALU.subtract), reads=lt.r + hi.r, writes=lt.r)
                S.op("dve", lambda e, lt=lt, lo=lo: e.tensor_copy(lo.t[0:33, :], lt.t[0:33, 0:128]), reads=lt.r, writes=lo.r)
                p = self.ps()
                S.op("pe", lambda e, hi=hi, p=p, g=g: e.matmul(p.t[:, 0:384], lhsT=hi.t[0:33, :], rhs=self.ohgb[g].t, start=True, stop=False),
                     reads=hi.r + self.ohgb[g].r, writes=p.r)
                S.op("pe", lambda e, lo=lo, p=p, g=g: e.matmul(p.t[:, 0:384], lhsT=lo.t[0:33, :], rhs=self.ohgb[g].t, start=False, stop=True),
                     reads=lo.r + self.ohgb[g].r, writes=p.r)
                rt = self.tmp(Bp)
                S.op("act", lambda e, rt=rt, p=p: e.copy(rt.t[:, 0:384], p.t[:, 0:384]), reads=p.r, writes=rt.r)
                S.dma("sp", out=self.bsc[gh], in_=rt.t[:, 0:384], reads=rt.r, writes=[bres])
                skew = bass.AP(tensor=self.bsc.tensor, offset=gh * 128 * 384, ap=[[383, 128], [1, 256]])
                S.dma("sp", out=self.biasmat[gh].t, in_=skew, reads=[bres], writes=self.biasmat[gh].r)

    def l1_attn_prompt(self):
        S = self.S
        cfg = self.cfg
        Bp = self.TB["p"]
        hT, act = Bp["hT"], Bp["act"]
        seq = cfg.seq
        def view(ap, name):
            return Buf(ap, 1, name)
        qt = [view(hT.t[:, k, 0:256].rearrange("p (c q) -> p c q", c=2), "qt%d" % k) for k in range(2)]
        kt = [view(hT.t[:, 2 + k, 0:512].rearrange("p (c q) -> p c q", c=2), "kt%d" % k) for k in range(2)]
        vt = [view(hT.t[:, 4 + k, 0:512].rearrange("p (c q) -> p c q", c=2), "vt%d" % k) for k in range(2)]
        pb_ = [view(act.t[:, k, 0:256], "pbf%d" % k) for k in range(4)]
        pT = [view(act.t[:, 4 + k, 0:256].rearrange("p (c q) -> p c q", c=2), "pT%d" % k) for k in range(4)]
        sc = [view(Bp["pbuf"][k].t[:, 0, 0:256], "sc%d" % k) for k in range(4)]
        ost = [view(Bp["xT"].t[:, k, 0:256].rearrange("p (h e) -> p h e", h=4), "ost%d" % k) for k in range(2)]
        lst = [view(Bp["xT"].t[:, 2 + k, 0:4], "lst%d" % k) for k in range(2)]
        st = [view(Bp["xT"].t[:, 4 + k, 0:8], "stat%d" % k) for k in range(4)]
        osr, lsr, vres = self.scr["osc"][1], self.scr["lsc"][1], self.scr["vsc"][1]
        ui = 0
        hi = 0
        for g, (win, dil) in enumerate(C_PAIRS):
            L = seq // dil
            vview = self.vsc.rearrange("(m r) f -> r m f", r=dil)
            oview = self.osc.rearrange("(m r) f -> r m f", r=dil)
            lview = self.lsc.rearrange("(m r) f -> r m f", r=dil)
            qres, kres = self.scr["qsc%d" % g][1], self.scr["ksc%d" % g][1]
            for r in range(dil):
                for Bk in range(L // 128):
                    m0 = Bk * 128
                    nk = 128 if Bk == 0 else 256
                    k0 = 256 - nk
                    q_, k_, v_ = qt[ui % 2], kt[ui % 2], vt[ui % 2]
                    o_, l_ = ost[ui % 2], lst[ui % 2]
                    ui += 1
                    S.dma("sp", out=q_.t, in_=self.qsc[g][:, :, r, m0:m0 + 128].rearrange("c p q -> p c q"), reads=[qres], writes=q_.r)
                    S.dma("sp", out=k_.t[:, :, k0:256], in_=self.ksc[g][:, :, r, m0 + 128 - nk:m0 + 128].rearrange("c p q -> p c q"),
                          reads=[kres], writes=k_.r)
                    for hh in range(2 - nk // 128, 2):
                        ms = m0 - 128 + hh * 128
                        S.dma("sp", out=v_.t[:, hh, :], in_=vview[r, ms:ms + 128, g * 256:(g + 1) * 256], reads=[vres], writes=v_.r)
                    hs = []
                    for h in range(4):
                        c, pb0 = h // 2, (h % 2) * 64
                        gh = g * 4 + h
                        p = self.ps()
                        S.op("pe", lambda e, p=p, q_=q_, k_=k_, c=c, pb0=pb0, k0=k0: e.matmul(
                            p.t[:, k0:256], lhsT=q_.t[pb0:pb0 + 64, c, :], rhs=k_.t[pb0:pb0 + 64, c, k0:256], start=True, stop=True),
                             reads=q_.r + k_.r, writes=p.r)
                        s_ = sc[hi % 4]; pb = pb_[hi % 4]; pt = pT[hi % 4]; stt = st[hi % 4]
                        hi += 1
                        hs.append((h, pb, pt, stt))
                        bm = self.biasmat[gh]
                        S.op("dve", lambda e, p=p, s_=s_, bm=bm, k0=k0: e.tensor_tensor(s_.t[:, k0:256], p.t[:, k0:256], bm.t[:, k0:256], ALU.add),
                             reads=p.r + bm.r, writes=s_.r)
                        S.op("dve", lambda e, s_=s_, stt=stt, k0=k0: e.tensor_reduce(stt.t[:, 0:1], s_.t[:, k0:256], AX.X, ALU.max),
                             reads=s_.r, writes=stt.r)
                        S.op("dve", lambda e, stt=stt: e.tensor_scalar(stt.t[:, 1:2], stt.t[:, 0:1], -1.0, None, ALU.mult), reads=stt.r, writes=stt.r)
                        S.op("act", lambda e, s_=s_, pb=pb, stt=stt, k0=k0: e.activation(pb.t[:, k0:256], s_.t[:, k0:256], AF.Exp, bias=stt.t[:, 1:2], scale=1.0,
                                                                                       accum_out=stt.t[:, 2:3]),
                             reads=s_.r + stt.r, writes=pb.r + stt.r)
                        S.op("dve", lambda e, stt=stt: e.reciprocal(stt.t[:, 3:4], stt.t[:, 2:3]), reads=stt.r, writes=stt.r)
                    for (h, pb, pt, stt) in hs:
                        pp = self.pbf()
                        for hh in range(2 - nk // 128, 2):
                            S.op("pe", lambda e, pp=pp, pb=pb, hh=hh: e.transpose(pp.t[:, hh * 128:(hh + 1) * 128], pb.t[:, hh * 128:(hh + 1) * 128], self.identb.t[:, :]),
                                 reads=pb.r + self.identb.r, writes=pp.r, signal=(hh == 1))
                        S.op("act", lambda e, pp=pp, pt=pt, k0=k0: e.copy(pt.t[:, :, :].rearrange("p c q -> p (c q)")[:, k0:256], pp.t[:, k0:256]),
                             reads=pp.r, writes=pt.r)
                        po = self.ps()
                        for hh in range(2 - nk // 128, 2):
                            S.op("pe", lambda e, po=po, pt=pt, v_=v_, hh=hh, h=h, nk=nk: e.matmul(
                                po.t[:, 0:64], lhsT=pt.t[:, hh, :], rhs=v_.t[:, hh, h * 64:(h + 1) * 64], start=(hh == 2 - nk // 128), stop=(hh == 1)),
                                 reads=pt.r + v_.r, writes=po.r, signal=(hh == 1))
                        S.op("act", lambda e, po=po, o_=o_, stt=stt, h=h: e.activation(o_.t[:, h, :], po.t[:, 0:64], AF.Copy, scale=stt.t[:, 3:4]),
                             reads=po.r + stt.r, writes=o_.r)
                        S.op("act", lambda e, stt=stt: e.activation(stt.t[:, 4:5], stt.t[:, 2:3], AF.Ln), reads=stt.r, writes=stt.r)
                        S.op("act", lambda e, stt=stt, l_=l_, h=h: e.activation(l_.t[:, h:h + 1], stt.t[:, 4:5], AF.Identity, bias=stt.t[:, 0:1], scale=1.0),
                             reads=stt.r, writes=l_.r)
                    S.dma("act", out=oview[r, m0:m0 + 128, g * 256:(g + 1) * 256], in_=o_.t.rearrange("p h e -> p (h e)"), reads=o_.r, writes=[osr])
                    S.dma("act", out=lview[r, m0:m0 + 128, g * 4:(g + 1) * 4], in_=l_.t, reads=l_.r, writes=[lsr])

    def l1_merge(self, B, n, otile, ltile):
        S = self.S
        ns = max(1, n // 128)
        rows = min(n, 128)
        cat = B["cat"]
        mx, ex, sm = self.mst[0], self.mst[1], self.mst[2]
        lt = ltile.t
        R = slice(0, rows)
        S.op("dve", lambda e: e.tensor_tensor(mx.t[R, 0:ns, :], lt[R, :, 0, :], lt[R, :, 1, :], ALU.max), reads=ltile.r, writes=mx.r)
        S.op("dve", lambda e: e.tensor_tensor(mx.t[R, 0:ns, :], mx.t[R, 0:ns, :], lt[R, :, 2, :], ALU.max), reads=ltile.r + mx.r, writes=mx.r)
        for g in range(3):
            S.op("dve", lambda e, g=g: e.tensor_tensor(ex.t[R, 0:ns, g, :], lt[R, :, g, :], mx.t[R, 0:ns, :], ALU.subtract), reads=ltile.r + mx.r, writes=ex.r)
        S.op("act", lambda e: e.activation(ex.t[R, 0:ns, :, :].rearrange("p s g h -> p (s g h)"), ex.t[R, 0:ns, :, :].rearrange("p s g h -> p (s g h)"), AF.Exp),
             reads=ex.r, writes=ex.r)
        S.op("dve", lambda e: e.tensor_tensor(sm.t[R, 0:ns, :], ex.t[R, 0:ns, 0, :], ex.t[R, 0:ns, 1, :], ALU.add), reads=ex.r, writes=sm.r)
        S.op("dve", lambda e: e.tensor_tensor(sm.t[R, 0:ns, :], sm.t[R, 0:ns, :], ex.t[R, 0:ns, 2, :], ALU.add), reads=ex.r + sm.r, writes=sm.r)
        S.op("dve", lambda e: e.reciprocal(sm.t[R, 0:ns, :], sm.t[R, 0:ns, :]), reads=sm.r, writes=sm.r)
        for g in range(3):
            S.op("dve", lambda e, g=g: e.tensor_tensor(ex.t[R, 0:ns, g, :], ex.t[R, 0:ns, g, :], sm.t[R, 0:ns, :], ALU.mult), reads=ex.r + sm.r, writes=ex.r)
        for s in range(ns):
            yc = self.ycf[s % 2]
            for h in range(4):
                for g in range(3):
                    src = otile.t[R, s, g, h * 64:(h + 1) * 64]
                    wcol = ex.t[R, s, g, h:h + 1]
                    if g == 0:
                        S.op("dve", lambda e, yc=yc, src=src, wcol=wcol, h=h: e.tensor_scalar(yc.t[R, h * 64:(h + 1) * 64], src, wcol, None, ALU.mult),
                             reads=otile.r + ex.r, writes=yc.r)
                    else:
                        S.op("dve", lambda e, yc=yc, src=src, wcol=wcol, h=h: e.scalar_tensor_tensor(
                            yc.t[R, h * 64:(h + 1) * 64], src, wcol, yc.t[R, h * 64:(h + 1) * 64], ALU.mult, ALU.add),
                             reads=otile.r + ex.r + yc.r, writes=yc.r)
            ycb = self.ycb[s % 2]
            S.op("act", lambda e, yc=yc, ycb=ycb: e.copy(ycb.t[R, :], yc.t[R, :]), reads=yc.r, writes=ycb.r)
            pp = self.pbf()
            for c in range(2):
                S.op("pe", lambda e, pp=pp, ycb=ycb, c=c: e.transpose(pp.t[:, c * 128:c * 128 + rows], ycb.t[R, c * 128:(c + 1) * 128], self.identb.t[R, R]),
                     reads=ycb.r + self.identb.r, writes=pp.r, signal=(c == 1))
            for c in range(2):
                S.op("act", lambda e, pp=pp, c=c, s=s: e.copy(cat.t[:, c, s * rows:(s + 1) * rows], pp.t[:, c * 128:c * 128 + rows]),
                     reads=pp.r, writes=[cat.r[c]])

    def l1_p3(self, B, t0, seqcols, last):
        S = self.S
        cfg = self.cfg
        d, o = self.din, self.dout
        mp = self.modp[1]
        n = NT
        xT, cat, y = B["xT"], B["cat"], B["y"]
        x1r = self.scr["x1"][1]
        for kc in range(8):
            S.dma("sp", out=xT.t[:, kc, :], in_=self.x1[kc * 128:(kc + 1) * 128, t0:t0 + NT], reads=[x1r], writes=[xT.r[kc]])
        xt = B["xtok"]
        otile = Buf(xt.t[:, :, 0:768].rearrange("p s (g f) -> p s g f", g=3), 1, "otile")
        otile.r = xt.r
        S.dma("sp", out=xt.t[:, :, 0:768], in_=self.osc[t0:t0 + NT, :].rearrange("(s p) f -> p s f", p=128), reads=[self.scr["osc"][1]], writes=xt.r)
        lt = self.ltile
        S.dma("sp", out=lt.t[:, :, :, :].rearrange("p s g h -> p s (g h)"), in_=self.lsc[t0:t0 + NT, :].rearrange("(s p) f -> p s f", p=128),
              reads=[self.scr["lsc"][1]], writes=lt.r)
        self.l1_merge(B, n, otile, lt)
        ydr = self.scr["ydsc"][1]
        for i in range(4):
            S.dma("sp", out=cat.t[:, 2 + i, :], in_=self.ydsc[i * 128:(i + 1) * 128, t0:t0 + NT], reads=[ydr], writes=[cat.r[2 + i]])
        self.l1_tail(B, seqcols, last, lambda: self.store_x_tokmajor(B, o["yp"][t0:t0 + NT, :]), prompt=True)

    def l1_tail(self, B, seqcols, last, store_fn, prompt):
        S = self.S
        o = self.dout
        mp = self.modp[1]
        n = B["n"]
        cat, y = B["cat"], B["y"]
        for g in range(2):
            sl_w, wv = self.load_w(self.W["out1"][g])
            for c4 in range(4):
                oc = g * 4 + c4
                p = self.ps()
                self.mm(p, n, wv, c4 * 128, lambda kc: (cat.t[:, kc, :], [cat.r[kc]]), 6, sl_w.r)
                S.op("act", lambda e, oc=oc, p=p: e.copy(y.t[:, oc, :], p.t[:, 0:n]), reads=p.r, writes=[y.r[oc]])
        self.resid_update(B, seqcols, mp["G1"])
        self.norm_mod(B, seqcols, mp["A2"], mp["mod"], 24)
        self.ffn(B, 1, seqcols, False)
        fh = B["fhalo"][1]
        if prompt and last:
            for r in range(2):
                self.store_rows_T(lambda c, r=r: fh.t[:, c, 0, r:r + 1], 44, 1,
                                  lambda c0, nc_, r=r: o["f_p"][1, r:r + 1, c0:c0 + nc_], fh.r)
        if not prompt:
            self.store_state_T(lambda c: fh.t[:, c, :, :], fh.r, 44, 4, 2,
                               lambda c0, nc_: o["f_s"][1].rearrange("s r f -> (s r) f")[:, c0:c0 + nc_])
        self.resid_update(B, seqcols, mp["G2"])
        store_fn()

    def l1_p1_sample(self):
        S = self.S
        d, o = self.din, self.dout
        mp = self.modp[1]
        Bs = self.TB["s"]
        n = 16
        seqcols = [(1 + s, 4 * s, 4) for s in range(4)]
        hT, cat = Bs["hT"], Bs["cat"]
        Bp_ = self.TB["p"]
        sdin = Buf(Bp_["xT"].t[:, 2, 0:480].rearrange("p (c s r) -> p c s r", c=4, s=4), 1, "sdin")
        sdin.r = [Bp_["xT"].r[2]]
        self.load_state_T(d["sd"].rearrange("s r f -> (s r) f"), 120, 4, lambda c: sdin.t[:, c, :, :], [sdin.r] * 4, 4, 30)
        for i in range(4):
            S.op("dve", lambda e, i=i: e.tensor_copy(Bs["dbb"][i].t[:, :, 0:30], sdin.t[:, i, :, :]), reads=sdin.r, writes=Bs["dbb"][i].r)
            S.op("dve", lambda e, i=i: e.tensor_copy(Bs["dst"].t[:, i, :, 0:26], sdin.t[:, i, :, 4:30]), reads=sdin.r, writes=Bs["dst"].r)
        fh = Bs["fhalo"][1]
        self.load_state_T(d["sf"][1].rearrange("s r f -> (s r) f"), 8, 44, lambda c: fh.t[:, c, :, :], [fh.r] * 44, 4, 2)
        self.norm_mod(Bs, seqcols, mp["A1"], mp["mod"], 0)
        if os.environ.get("KDEBUG_STOP") == "s1":
            return
        rhs_h = lambda kc: (hT.t[:, kc, :], hT.r)
        qk = self.qkT_s
        for grp in range(3):
            sl_w, wv = self.load_w(self.W["qk1"][grp])
            for c4 in range(4):
                cc = grp * 4 + c4
                p = self.ps()
                self.mm(p, n, wv, c4 * 128, rhs_h, 8, sl_w.r)
                S.op("act", lambda e, p=p, cc=cc: e.activation(qk.t[:, cc, :], p.t[:, 0:n], AF.Copy, scale=(0.125 if cc < 6 else 1.0)),
                     reads=p.r, writes=qk.r)
        if os.environ.get("KDEBUG_STOP") == "s2":
            return
        kvnb = self.kvnb
        for g in range(3):
            wb = C_PAIRS[g][0]
            sl_w, wv = self.load_w(self.W["kv1"][g])
            for b in range(4):
                p = self.ps()
                for kc in range(8):
                    S.op("pe", lambda e, kc=kc, b=b, p=p, wv=wv: e.matmul(p.t[0:4, 0:512], lhsT=hT.t[:, kc, b * 4:(b + 1) * 4], rhs=wv[:, kc, 0:512],
                                                                        start=(kc == 0), stop=(kc == 7)),
                         reads=hT.r + sl_w.r, writes=p.r, signal=(kc == 7))
                kvn = self.kvn[self.kvni]
                self.kvni = (self.kvni + 1) % 2
                flags = os.environ.get("KDEBUG_S3", "")
                if "a" not in flags:
                    S.op("act", lambda e, p=p, kvn=kvn: e.copy(kvn.t[0:4, :], p.t[0:4, 0:512]), reads=p.r, writes=kvn.r)
                if "d" not in flags:
                    S.op("dve", lambda e, p=p, b=b, g=g: e.tensor_copy(kvnb.t[0:4, b, g, :], p.t[0:4, 256:512]), reads=p.r, writes=kvnb.r)
                cin = d[("c128", "c512", "c2048")[g]]
                for r0 in ([] if os.environ.get("KDEBUG_NOCOPY") else range(0, wb - 4, 256)):
                    r1 = min(wb - 4, r0 + 256)
                    S.dma("pool", out=o["c%d_s" % g][b, r0:r1, :], in_=cin[b, 4 + r0:4 + r1, :], is_output=True)
                if "o" not in flags:
                    S.dma("pool", out=o["c%d_s" % g][b, wb - 4:wb, :], in_=kvn.t[0:4, :], reads=kvn.r, is_output=True)
        if os.environ.get("KDEBUG_STOP") == "s3":
            return
        self.l1_dpath(Bs, seqcols, lambda i: (cat.t[:, 2 + i, :], [cat.r[2 + i]]), want_state=True)
        if os.environ.get("KDEBUG_STOP") == "s4":
            return
        self.store_state_T(lambda c: Bs["dst"].t[:, c, :, :], Bs["dst"].r, 4, 4, 30,
                           lambda c0, nc_: o["d_s"].rearrange("s r f -> (s r) f")[:, c0:c0 + nc_])

    def l1_attn_sample(self):
        S = self.S
        d = self.din
        Bp, Bs = self.TB["p"], self.TB["s"]
        act = Bp["act"]
        cat = Bs["cat"]
        qk = self.qkT_s
        kvnb = self.kvnb

        def view(ap, name):
            return Buf(ap, 1, name)
        y = Bp["y"]
        kcf = [view(y.t[:, 6 + k, :], "kcf%d" % k) for k in range(2)]
        kcb = [view(act.t[:, 8 + k, :], "kcb%d" % k) for k in range(3)]
        kTc = [view(act.t[:, 12 + k, 0:256].rearrange("p (c q) -> p c q", c=2), "kTc%d" % k) for k in range(2)]
        xk = Bp["xtok"]
        STs = view(xk.t[:, 1, 0:48].rearrange("p (g h t) -> p g h t", g=3, h=4), "STs")
        SNs = view(xk.t[:, 1, 64:112].rearrange("p (g h t) -> p g h t", g=3, h=4), "SNs")
        scs = view(xk.t[:, 2, 0:396].rearrange("p (g k) -> p g k", g=3), "scs")
        stt = view(xk.t[:, 1, 128:136], "stts")
        pn = view(act.t[:, 14, 0:396].rearrange("p (g k) -> p g k", g=3), "pn")
        pTs = view(act.t[:, 15, 0:48].rearrange("p (g q) -> p g q", g=3), "pTs")
        pTn = view(act.t[:, 15, 64:112].rearrange("p (g q) -> p g q", g=3), "pTn")
        biasS = view(xk.t[:, 0, 0:396].rearrange("p (g k) -> p g k", g=3), "biasS")
        rbrep = self.rbrep
        self.ohsb = [view(act.t[:, 17 + k, 0:132], "ohsb%d" % k) for k in range(2)]
        for g in range(3):
            lt = self.tmp(Bp)
            for h in range(4):
                gh = g * 4 + h
                S.op("dve", lambda e, lt=lt, h=h, gh=gh: e.tensor_scalar(lt.t[:, h * 4:(h + 1) * 4], self.dmask.t[:, 0:4], rbrep.t[:, gh:gh + 1], None, ALU.mult),
                     reads=self.dmask.r + rbrep.r, writes=lt.r)
            hi, lo = self.bl[0], self.bl[1]
            ohb = self.ohsb[g % 2]
            S.op("dve", lambda e, ohb=ohb, g=g: e.tensor_copy(ohb.t, self.ohs[g].t[:, 0:132]), reads=self.ohs[g].r, writes=ohb.r)
            S.op("dve", lambda e, lt=lt, hi=hi: e.tensor_copy(hi.t[:, 0:16], lt.t[:, 0:16]), reads=lt.r, writes=hi.r)
            S.op("dve", lambda e, lt=lt, hi=hi: e.tensor_tensor(lt.t[:, 0:16], lt.t[:, 0:16], hi.t[:, 0:16], ALU.subtract), reads=lt.r + hi.r, writes=lt.r)
            S.op("dve", lambda e, lt=lt, lo=lo: e.tensor_copy(lo.t[:, 0:16], lt.t[:, 0:16]), reads=lt.r, writes=lo.r)
            p = self.ps()
            S.op("pe", lambda e, hi=hi, p=p, ohb=ohb: e.matmul(p.t[0:16, 0:132], lhsT=hi.t[:, 0:16], rhs=ohb.t, start=True, stop=False),
                 reads=hi.r + ohb.r, writes=p.r)
            S.op("pe", lambda e, lo=lo, p=p, ohb=ohb: e.matmul(p.t[0:16, 0:132], lhsT=lo.t[:, 0:16], rhs=ohb.t, start=False, stop=True),
                 reads=lo.r + ohb.r, writes=p.r)
            S.op("dve", lambda e, p=p, g=g: e.tensor_tensor(biasS.t[0:16, g, :], p.t[0:16, 0:132], self.masks.t[0:16, g, :], ALU.add),
                 reads=p.r + self.masks.r, writes=biasS.r)
        self.kmask = [view(act.t[:, 19 + k, 0:96].rearrange("p (c t) -> p c t", c=6), "kmask%d" % k) for k in range(2)]
        for k in range(2):
            km = self.kmask[k]
            S.op("dve", lambda e, km=km: e.memset(km.t, 0.0), writes=km.r)
            S.op("dve", lambda e, km=km, k=k: e.tensor_copy(km.t[k * 64:(k + 1) * 64], qk.t[k * 64:(k + 1) * 64, 6:12, :]), reads=qk.r, writes=km.r)
        psY = self.pslong
        ci = 0
        SA = os.environ.get("KDEBUG_SA", "")
        if SA == "a":
            return
        for b in range(4):
            pST = self.ps()
            pSN = self.ps()
            vtiles = {}
            for g, (win, dil) in enumerate(C_PAIRS):
                cin = d[("c128", "c512", "c2048")[g]]
                ncls = 1 if g == 0 else 4
                for cls in range(ncls):
                    kf = kcf[ci % 2]; kb = kcb[ci % 3]; kT = kTc[ci % 2]
                    ci += 1
                    if g == 0:
                        src = cin[b, :, :]
                    else:
                        src = cin[b].rearrange("(i c) f -> c i f", c=dil)[cls]
                    S.dma("sp", out=kf.t, in_=src, writes=kf.r)
                    S.op("dve", lambda e, kf=kf, kb=kb: e.tensor_copy(kb.t, kf.t), reads=kf.r, writes=kb.r)
                    pp = self.pbf()
                    for c in range(2):
                        S.op("pe", lambda e, pp=pp, kb=kb, c=c: e.transpose(pp.t[:, c * 128:(c + 1) * 128], kb.t[:, c * 128:(c + 1) * 128], self.identb.t[:, :]),
                             reads=kb.r + self.identb.r, writes=pp.r, signal=(c == 1))
                    S.op("act", lambda e, pp=pp, kT=kT: e.copy(kT.t.rearrange("p c q -> p (c q)"), pp.t[:, 0:256]), reads=pp.r, writes=kT.r)
                    for h in ([] if SA == "b1" else range(4)):
                        c, pb0 = h // 2, (h % 2) * 64
                        if g == 0:
                            outap = pST.t[:, 0:48].rearrange("p (g h t) -> p g h t", g=3, h=4)[:, 0, h, 0:4]
                            rhs = qk.t[pb0:pb0 + 64, c, b * 4:b * 4 + 4]
                        else:
                            outap = pST.t[:, 0:48].rearrange("p (g h t) -> p g h t", g=3, h=4)[:, g, h, cls:cls + 1]
                            rhs = qk.t[pb0:pb0 + 64, g * 2 + c, b * 4 + cls:b * 4 + cls + 1]
                        S.op("pe", lambda e, outap=outap, kT=kT, c=c, pb0=pb0, rhs=rhs: e.matmul(outap, lhsT=kT.t[pb0:pb0 + 64, c, :], rhs=rhs, start=True, stop=True),
                             reads=kT.r + qk.r, writes=pST.r)
                    vtiles[(g, cls)] = kb
                for h in ([] if SA in ("b1", "b2") else range(4)):
                    c, pb0 = h // 2, (h % 2) * 64
                    outap = pSN.t[0:4, 0:48].rearrange("p (g h t) -> p g h t", g=3, h=4)[:, g, h, 0:4]
                    km = self.kmask[h % 2]
                    S.op("pe", lambda e, outap=outap, g=g, c=c, km=km, b=b: e.matmul(
                        outap, lhsT=km.t[:, g * 2 + c, b * 4:b * 4 + 4], rhs=qk.t[:, g * 2 + c, b * 4:b * 4 + 4], start=True, stop=True),
                         reads=qk.r + km.r, writes=pSN.r)
            if SA in ("b", "b1", "b2"):
                continue
            S.op("act", lambda e, pST=pST: e.copy(STs.t.rearrange("p g h t -> p (g h t)"), pST.t[:, 0:48]), reads=pST.r, writes=STs.r)
            S.op("act", lambda e, pSN=pSN: e.copy(SNs.t[0:4].rearrange("p g h t -> p (g h t)"), pSN.t[0:4, 0:48]), reads=pSN.r, writes=SNs.r)
            pq = self.ps()
            for g in range(3):
                S.op("pe", lambda e, pq=pq, g=g: e.transpose(pq.t[0:16, g * 132:g * 132 + 128], STs.t[:, g, :, :].rearrange("p h t -> p (h t)"), self.ident.t[:, :]),
                     reads=STs.r + self.ident.r, writes=pq.r)
                S.op("pe", lambda e, pq=pq, g=g: e.transpose(pq.t[0:16, g * 132 + 128:g * 132 + 132], SNs.t[0:4, g, :, :].rearrange("p h t -> p (h t)"), self.ident.t[0:4, 0:4]),
                     reads=SNs.r + self.ident.r, writes=pq.r)
            S.op("dve", lambda e, pq=pq: e.tensor_tensor(scs.t[0:16].rearrange("p g k -> p (g k)"), pq.t[0:16, 0:396], biasS.t[0:16].rearrange("p g k -> p (g k)"), ALU.add),
                 reads=pq.r + biasS.r, writes=scs.r)
            S.op("dve", lambda e: e.tensor_reduce(stt.t[0:16, 0:1], scs.t[0:16].rearrange("p g k -> p (g k)"), AX.X, ALU.max), reads=scs.r, writes=stt.r)
            S.op("dve", lambda e: e.tensor_scalar(stt.t[0:16, 1:2], stt.t[0:16, 0:1], -1.0, None, ALU.mult), reads=stt.r, writes=stt.r)
            S.op("act", lambda e: e.activation(scs.t[0:16].rearrange("p g k -> p (g k)"), scs.t[0:16].rearrange("p g k -> p (g k)"), AF.Exp,
                                               bias=stt.t[0:16, 1:2], scale=1.0, accum_out=stt.t[0:16, 2:3]), reads=scs.r + stt.r, writes=scs.r + stt.r)
            S.op("dve", lambda e: e.reciprocal(stt.t[0:16, 3:4], stt.t[0:16, 2:3]), reads=stt.r, writes=stt.r)
            S.op("dve", lambda e: e.tensor_scalar(pn.t[0:16].rearrange("p g k -> p (g k)"), scs.t[0:16].rearrange("p g k -> p (g k)"), stt.t[0:16, 3:4], None, ALU.mult),
                 reads=scs.r + stt.r, writes=pn.r)
            if SA == "c":
                continue
            pp = self.pbf()
            for g in range(3):
                S.op("pe", lambda e, pp=pp, g=g: e.transpose(pp.t[:, g * 16:(g + 1) * 16], pn.t[0:16, g, 0:128], self.identb.t[0:16, 0:16]),
                     reads=pn.r + self.identb.r, writes=pp.r)
                S.op("pe", lambda e, pp=pp, g=g: e.transpose(pp.t[0:4, 64 + g * 16:64 + (g + 1) * 16], pn.t[0:16, g, 128:132], self.identb.t[0:16, 0:16]),
                     reads=pn.r + self.identb.r, writes=pp.r)
            S.op("act", lambda e, pp=pp: e.copy(pTs.t.rearrange("p g q -> p (g q)"), pp.t[:, 0:48]), reads=pp.r, writes=pTs.r)
            S.op("act", lambda e, pp=pp: e.copy(pTn.t[0:4].rearrange("p g q -> p (g q)"), pp.t[0:4, 64:112]), reads=pp.r, writes=pTn.r)
            if SA == "d":
                continue
            for h in range(4):
                c, pb0 = h // 2, (h % 2) * 64
                first = True
                for g, (win, dil) in enumerate(C_PAIRS):
                    outap = psY.t[pb0:pb0 + 64, c * 16 + b * 4:c * 16 + b * 4 + 4]
                    S.op("pe", lambda e, outap=outap, g=g, h=h, b=b, first=first: e.matmul(
                        outap, lhsT=kvnb.t[0:4, b, g, h * 64:(h + 1) * 64], rhs=pTn.t[0:4, g, h * 4:(h + 1) * 4],
                        start=first, stop=False, skip_group_check=True),
                         reads=kvnb.r + pTn.r, writes=psY.r)
                    first = False
            for g, (win, dil) in enumerate(C_PAIRS):
                cin = d[("c128", "c512", "c2048")[g]]
                ncls = 1 if g == 0 else 4
                for cls in range(ncls):
                    kf = kcf[ci % 2]; kb = kcb[ci % 3]
                    ci += 1
                    if g == 0:
                        src = cin[b, :, 256:512]
                    else:
                        src = cin[b].rearrange("(i c) f -> c i f", c=dil)[cls][:, 256:512]
                    S.dma("sp", out=kf.t[:, 0:256], in_=src, writes=kf.r)
                    S.op("dve", lambda e, kf=kf, kb=kb: e.tensor_copy(kb.t[:, 0:256], kf.t[:, 0:256]), reads=kf.r, writes=kb.r)
                    for h in range(4):
                        c, pb0 = h // 2, (h % 2) * 64
                        if g == 0:
                            outap = psY.t[pb0:pb0 + 64, c * 16 + b * 4:c * 16 + b * 4 + 4]
                            rhs = pTs.t[:, 0, h * 4:(h + 1) * 4]
                        else:
                            outap = psY.t[pb0:pb0 + 64, c * 16 + b * 4 + cls:c * 16 + b * 4 + cls + 1]
                            rhs = pTs.t[:, g, h * 4 + cls:h * 4 + cls + 1]
                        S.op("pe", lambda e, outap=outap, kb=kb, h=h, rhs=rhs: e.matmul(outap, lhsT=kb.t[:, h * 64:(h + 1) * 64], rhs=rhs,
                                                                                  start=False, stop=False, skip_group_check=True),
                             reads=kb.r + pTs.r, writes=psY.r)
        if SA:
            return
        for c in range(2):
            S.op("act", lambda e, c=c: e.copy(cat.t[:, c, :], psY.t[:, c * 16:(c + 1) * 16]), reads=psY.r, writes=[cat.r[c]])

    def l1_p3_sample(self):
        o = self.dout
        Bs = self.TB["s"]
        seqcols = [(1 + s, 4 * s, 4) for s in range(4)]
        self.l1_tail(Bs, seqcols, True, lambda: self.store_x_tokmajor(Bs, o["ys"]), prompt=False)


def build_program(cfg):
    b = Builder(cfg)
    return b


_CACHE = {}


def get_nc(cfg_key):
    if cfg_key not in _CACHE:
        cfg = Cfg(*cfg_key)
        b = Builder(cfg)
        b.sstage = None
        nc = build_with_stages(b)
        _CACHE[cfg_key] = (nc, b)
    return _CACHE[cfg_key]


def build_with_stages(b):
    b.sstage = None
    b.sstage2 = None
    b.cpstg = [b.sb("cpstg%d" % k, [128, 128], F32) for k in range(2)]
    b.cpi = 0
    return b.build()


def make_oh():
    oh = np.zeros((3, 33, 384), np.float32)
    for g, (win, dil) in enumerate(C_PAIRS):
        taps = win // dil
        buckets = t5_bucket(dil * np.arange(taps + 1))
        for c in range(384):
            j = 128 - c
            if 0 <= j <= taps:
                oh[g, buckets[j], c] = 1.0
            else:
                oh[g, 32, c] = NEG
    return oh


def make_sample_consts():
    ohs = np.zeros((3, 128, 132), np.float32)
    masks = np.zeros((16, 3, 132), np.float32)
    dmask = np.zeros((128, 4), np.float32)
    for t in range(4):
        dmask[t * 32:(t + 1) * 32, t] = 1.0
    for g, (win, dil) in enumerate(C_PAIRS):
        taps = win // dil
        buckets = t5_bucket(dil * np.arange(taps + 1))
        for t in range(4):
            for col in range(132):
                if col < 128:
                    if g == 0:
                        valid, tap = col >= t, 128 + t - col
                    else:
                        valid, tap = True, 128 - col
                else:
                    j = col - 128
                    if g == 0:
                        valid, tap = j <= t, t - j
                    else:
                        valid, tap = j == t, 0
                if valid:
                    ohs[g, t * 32 + buckets[tap], col] = 1.0
                else:
                    for h in range(4):
                        masks[h * 4 + t, g, col] = NEG
    return ohs, dmask, masks


def make_in_maps(inputs, cfg, n_cores=8):
    f = lambda a: np.ascontiguousarray(np.asarray(a, dtype=np.float32))
    I = {k: f(v) for k, v in inputs.items()}
    oh = make_oh()
    ohs, dmask, masks = make_sample_consts()
    maps = []
    for c in range(n_cores):
        b = c // 2
        ss = slice(4 * c, 4 * c + 4)
        m = {
            "xp": I["x_prompt"][b, :cfg.seq], "cp": I["c_prompt"][b:b + 1],
            "xs": I["x_sample"][ss].reshape(16, D), "cs": I["c_sample"][ss],
            "sa": I["state_a_conv"][0, ss], "sb": I["state_b_pool"][0, ss],
            "c128": I["cache_c_win128"][0, ss].reshape(4, 128, 512),
            "c512": I["cache_c_win512"][0, ss].reshape(4, 512, 512),
            "c2048": I["cache_c_win2048"][0, ss].reshape(4, 2048, 512),
            "sd": I["state_d_conv"][0, ss], "sf": I["state_ffn_conv"][:, ss],
            "ada_w": I["ada_w"], "ada_b": I["ada_b"], "norm_g": I["norm_g"], "rel_bias": I["rel_bias"],
            "ab_w_in": I["ab_w_in"][0], "a_conv_w": I["a_conv_w"][0], "b_w_grp": I["b_w_grp"][0],
            "b_scale": I["b_scale"][0], "ab_w_out": I["ab_w_out"][0], "cd_w_in": I["cd_w_in"][0],
            "d_conv_w": I["d_conv_w"][0], "d_conv_b": I["d_conv_b"][0], "d_ln_g": I["d_ln_g"][0],
            "d_ln_b": I["d_ln_b"][0], "cd_w_out": I["cd_w_out"][0], "ffn_w_up": I["ffn_w_up"],
            "ffn_conv_w": I["ffn_conv_w"], "ffn_conv_b": I["ffn_conv_b"], "ffn_w_down": I["ffn_w_down"],
            "oh": oh, "ohs": ohs, "dmask": dmask, "masks": masks,
        }
        maps.append({k: np.ascontiguousarray(v) for k, v in m.items()})
    return maps


def kernel(**inputs):
    cfg_key = (8, True, 2, False)
    nc, b = get_nc(cfg_key)
    cfg = b.cfg
    maps = make_in_maps(inputs, cfg)
    res = run_bass_kernel_spmd(nc, maps, core_ids=list(range(8)))
    R = res.results
    cat = lambda name, cores: np.stack([np.asarray(R[c][name]) for c in cores])
    even = [0, 2, 4, 6]
    allc = list(range(8))
    y_prompt = cat("yp", even)
    y_sample = np.concatenate([np.asarray(R[c]["ys"]).reshape(4, 4, D) for c in allc])
    a_p = cat("a_p", even)[None]
    a_s = np.concatenate([np.asarray(R[c]["a_s"]) for c in allc])[None]
    b_p = cat("b_p", even)[None]
    b_s = np.concatenate([np.asarray(R[c]["b_s"]) for c in allc])[None]
    outs = [y_prompt, y_sample, a_p, a_s, b_p, b_s]
    for g, wb in enumerate((128, 512, 2048)):
        cp = cat("c%d_p" % g, even).reshape(1, 4, wb, 2, 4, 64)
        cs = np.concatenate([np.asarray(R[c]["c%d_s" % g]) for c in allc]).reshape(1, 32, wb, 2, 4, 64)
        outs += [cp, cs]
    d_p = cat("d_p", even)[None]
    d_s = np.concatenate([np.asarray(R[c]["d_s"]) for c in allc])[None]
    f_p = np.stack([np.asarray(R[c]["f_p"]) for c in even], axis=1)
    f_s = np.concatenate([np.asarray(R[c]["f_s"]) for c in allc], axis=1)
    outs += [d_p, d_s, f_p, f_s]
    return tuple(np.ascontiguousarray(x, dtype=np.float32) for x in outs)
```
